# Optimizing a Trainium2 kernel written in Bass

```python
import math
import jax, jax.numpy as jnp
from jax import lax
import numpy as np

D_MODEL = 1024
BATCH = 4
SEQ = 4096
DEPTH = 1

N_HEADS = 8
QK_NOPE_DIM = 64
QK_ROPE_DIM = 32
V_HEAD_DIM = 64
Q_LORA_RANK = 384
KV_LORA_RANK = 256
ROPE_THETA = 10000.0
Q_BLOCK = 128
SSM_CHANNELS = D_MODEL // 2
SSM_GROUP = 16
SSM_GROUPS = SSM_CHANNELS // SSM_GROUP
SSM_STATE = 64
SSM_DIRS = 2
DT_MIN = 1e-3
DT_MAX = 1e-1
FFN_HIDDEN = ((8 * D_MODEL // 3 + 255) // 256) * 256
N_BRANCHES = 2
EPS = 1e-6
SPLITS = [Q_LORA_RANK, Q_LORA_RANK + KV_LORA_RANK, Q_LORA_RANK + KV_LORA_RANK + QK_ROPE_DIM, Q_LORA_RANK + KV_LORA_RANK + QK_ROPE_DIM + SSM_CHANNELS]
IN_COLS = Q_LORA_RANK + KV_LORA_RANK + QK_ROPE_DIM + SSM_CHANNELS + N_BRANCHES * D_MODEL

kernel_name = 'hybrid_mla_s5_sandwich_adaln_block'


def _rms(x, g):
    x32 = x.astype(jnp.float32)
    y = x32 * lax.rsqrt(jnp.mean(x32 * x32, axis=-1, keepdims=True) + EPS) * g.astype(jnp.float32)
    return y.astype(x.dtype)


def _rope_tables(positions):
    inv_freq = ROPE_THETA ** (-jnp.arange(0, QK_ROPE_DIM, 2, dtype=jnp.float32) / QK_ROPE_DIM)
    ang = positions.astype(jnp.float32)[..., None] * inv_freq
    return jnp.cos(ang), jnp.sin(ang)


def _apply_rope(x, cos, sin):
    x32 = x.astype(jnp.float32)
    x1, x2 = jnp.split(x32, 2, axis=-1)
    return jnp.concatenate([x1 * cos - x2 * sin, x2 * cos + x1 * sin], axis=-1).astype(x.dtype)


def _mla(q_lat, kv_lat, k_rope_raw, positions, g_qn, g_kvn, w_uq, w_uk, w_uv, w_o):
    bsz, s, _ = q_lat.shape
    q = (_rms(q_lat, g_qn) @ w_uq).reshape(bsz, s, N_HEADS, QK_NOPE_DIM + QK_ROPE_DIM)
    q_nope, q_rope = q[..., :QK_NOPE_DIM], q[..., QK_NOPE_DIM:]
    kv = _rms(kv_lat, g_kvn)
    k_nope = (kv @ w_uk).reshape(bsz, s, N_HEADS, QK_NOPE_DIM)
    v = (kv @ w_uv).reshape(bsz, s, N_HEADS, V_HEAD_DIM)
    cos, sin = _rope_tables(positions)
    q_rope = _apply_rope(q_rope, cos[:, :, None, :], sin[:, :, None, :])
    k_rope = _apply_rope(k_rope_raw, cos, sin)
    scale = (QK_NOPE_DIM + QK_ROPE_DIM) ** -0.5
    nb = s // Q_BLOCK
    qn_b = (q_nope * scale).reshape(bsz, nb, Q_BLOCK, N_HEADS, QK_NOPE_DIM).transpose(1, 0, 2, 3, 4)
    qr_b = (q_rope * scale).reshape(bsz, nb, Q_BLOCK, N_HEADS, QK_ROPE_DIM).transpose(1, 0, 2, 3, 4)

    def block(args):
        qn, qr = args
        sc = jnp.einsum('bqhd,bkhd->bhqk', qn, k_nope) + jnp.einsum('bqhr,bkr->bhqk', qr, k_rope)
        p = jax.nn.softmax(sc.astype(jnp.float32), axis=-1).astype(v.dtype)
        return jnp.einsum('bhqk,bkhd->bqhd', p, v)

    o = lax.map(block, (qn_b, qr_b))
    o = o.transpose(1, 0, 2, 3, 4).reshape(bsz, s, N_HEADS * V_HEAD_DIM)
    return o @ w_o


def _scan_combine(left, right):
    ar1, ai1, br1, bi1 = left
    ar2, ai2, br2, bi2 = right
    return (ar2 * ar1 - ai2 * ai1,
            ar2 * ai1 + ai2 * ar1,
            ar2 * br1 - ai2 * bi1 + br2,
            ar2 * bi1 + ai2 * br1 + bi2)


def _s5_direction(u, lam_re, lam_im, log_dt, b_re, b_im, c_re, c_im, reverse):
    s = u.shape[1]
    lam_re = jnp.minimum(lam_re.astype(jnp.float32), -1e-4)
    lam_im = lam_im.astype(jnp.float32)
    dt = jnp.exp(log_dt.astype(jnp.float32))[:, None]
    mag = jnp.exp(lam_re * dt)
    ab_re = mag * jnp.cos(lam_im * dt)
    ab_im = mag * jnp.sin(lam_im * dt)
    nr, ni = ab_re - 1.0, ab_im
    den = lam_re * lam_re + lam_im * lam_im
    f_re = (nr * lam_re + ni * lam_im) / den
    f_im = (ni * lam_re - nr * lam_im) / den
    b_re = b_re.astype(jnp.float32)
    b_im = b_im.astype(jnp.float32)
    bb_re = f_re[..., None] * b_re - f_im[..., None] * b_im
    bb_im = f_re[..., None] * b_im + f_im[..., None] * b_re
    xr = jnp.einsum('bsgh,gph->bsgp', u, bb_re)
    xi = jnp.einsum('bsgh,gph->bsgp', u, bb_im)
    a_re = jnp.broadcast_to(ab_re[None, None], (1, s) + ab_re.shape)
    a_im = jnp.broadcast_to(ab_im[None, None], (1, s) + ab_im.shape)
    _, _, hr, hi = lax.associative_scan(_scan_combine, (a_re, a_im, xr, xi), reverse=reverse, axis=1)
    return (jnp.einsum('bsgp,ghp->bsgh', hr, c_re.astype(jnp.float32))
            - jnp.einsum('bsgp,ghp->bsgh', hi, c_im.astype(jnp.float32)))


def _s5_branch(u, lam_re, lam_im, log_dt, b_re, b_im, c_re, c_im, d_skip, w_glu):
    bsz, s, _ = u.shape
    ug = u.astype(jnp.float32).reshape(bsz, s, SSM_GROUPS, SSM_GROUP)
    y = (_s5_direction(ug, lam_re[0], lam_im[0], log_dt[0], b_re[0], b_im[0], c_re[0], c_im[0], False)
         + _s5_direction(ug, lam_re[1], lam_im[1], log_dt[1], b_re[1], b_im[1], c_re[1], c_im[1], True))
    y = y.reshape(bsz, s, SSM_CHANNELS) + d_skip.astype(jnp.float32) * u.astype(jnp.float32)
    y = jax.nn.gelu(y).astype(u.dtype)
    a, g = jnp.split(y @ w_glu, 2, axis=-1)
    return a * jax.nn.sigmoid(g)


def setup_inputs(seed: int = 0) -> dict:
    key = jax.random.key(seed)
    ks = jax.random.split(key, 32)
    f32 = jnp.float32
    L, D, H = DEPTH, D_MODEL, N_HEADS
    G, P, C = SSM_GROUPS, SSM_STATE, SSM_GROUP

    def nrm(k, shape, fan_in, mult=1.0):
        return jax.random.normal(k, shape, f32) * (mult * fan_in ** -0.5)

    def gain(k, shape):
        return 1.0 + 0.01 * jax.random.normal(k, shape, f32)

    x = jax.random.normal(ks[0], (BATCH, SEQ, D), f32)
    c = jax.random.normal(ks[1], (BATCH, D), f32)
    positions = (jnp.arange(SEQ, dtype=jnp.int32)[None, :]
                 + jax.random.randint(ks[2], (BATCH, 1), 0, 2048, dtype=jnp.int32))
    n_idx = jnp.arange(P, dtype=f32)
    lam_re = -0.5 + 1e-3 * jax.random.normal(ks[3], (L, SSM_DIRS, G, P), f32)
    lam_im = math.pi * n_idx + 1e-3 * jax.random.normal(ks[4], (L, SSM_DIRS, G, P), f32)
    log_dt = jax.random.uniform(ks[5], (L, SSM_DIRS, G), f32, math.log(DT_MIN), math.log(DT_MAX))
    return {
        'x': x,
        'c': c,
        'positions': positions,
        'w_ada': nrm(ks[6], (L, D, 6 * D), D, 0.5),
        'b_ada': 0.01 * jax.random.normal(ks[7], (L, 6 * D), f32),
        'g_pre_mix': gain(ks[8], (L, D)),
        'g_post_mix': gain(ks[9], (L, D)),
        'g_pre_ffn': gain(ks[10], (L, D)),
        'g_post_ffn': gain(ks[11], (L, D)),
        'w_in': nrm(ks[12], (L, D, IN_COLS), D),
        'g_q_norm': gain(ks[13], (L, Q_LORA_RANK)),
        'g_kv_norm': gain(ks[14], (L, KV_LORA_RANK)),
        'w_uq': nrm(ks[15], (L, Q_LORA_RANK, H * (QK_NOPE_DIM + QK_ROPE_DIM)), Q_LORA_RANK),
        'w_uk': nrm(ks[16], (L, KV_LORA_RANK, H * QK_NOPE_DIM), KV_LORA_RANK),
        'w_uv': nrm(ks[17], (L, KV_LORA_RANK, H * V_HEAD_DIM), KV_LORA_RANK),
        'w_attn_out': nrm(ks[18], (L, H * V_HEAD_DIM, D), H * V_HEAD_DIM),
        'ssm_lambda_re': lam_re,
        'ssm_lambda_im': lam_im,
        'ssm_log_dt': log_dt,
        'ssm_b_re': nrm(ks[19], (L, SSM_DIRS, G, P, C), 2 * C),
        'ssm_b_im': nrm(ks[20], (L, SSM_DIRS, G, P, C), 2 * C),
        'ssm_c_re': nrm(ks[21], (L, SSM_DIRS, G, C, P), 2 * P),
        'ssm_c_im': nrm(ks[22], (L, SSM_DIRS, G, C, P), 2 * P),
        'ssm_d': jax.random.normal(ks[23], (L, SSM_CHANNELS), f32),
        'w_glu': nrm(ks[24], (L, SSM_CHANNELS, 2 * D), SSM_CHANNELS),
        'w_mix_out': nrm(ks[25], (L, D, D), D),
        'w_ffn_in': nrm(ks[26], (L, D, 2 * FFN_HIDDEN), D),
        'w_ffn_out': nrm(ks[27], (L, FFN_HIDDEN, D), FFN_HIDDEN),
    }


def reference(x, c, positions, w_ada, b_ada, g_pre_mix, g_post_mix, g_pre_ffn, g_post_ffn,
              w_in, g_q_norm, g_kv_norm, w_uq, w_uk, w_uv, w_attn_out,
              ssm_lambda_re, ssm_lambda_im, ssm_log_dt, ssm_b_re, ssm_b_im, ssm_c_re, ssm_c_im,
              ssm_d, w_glu, w_mix_out, w_ffn_in, w_ffn_out):
    for l in range(DEPTH):
        ada = jax.nn.silu(c) @ w_ada[l] + b_ada[l]
        sh1, sc1, gt1, sh2, sc2, gt2 = jnp.split(ada[:, None, :], 6, axis=-1)
        h = _rms(x, g_pre_mix[l]) * (1.0 + sc1) + sh1
        proj = h @ w_in[l]
        q_lat, kv_lat, k_rope_raw, u, gate_cols = jnp.split(proj, SPLITS, axis=-1)
        branch_a = _mla(q_lat, kv_lat, k_rope_raw, positions, g_q_norm[l], g_kv_norm[l],
                        w_uq[l], w_uk[l], w_uv[l], w_attn_out[l])
        branch_b = _s5_branch(u, ssm_lambda_re[l], ssm_lambda_im[l], ssm_log_dt[l],
                              ssm_b_re[l], ssm_b_im[l], ssm_c_re[l], ssm_c_im[l],
                              ssm_d[l], w_glu[l])
        gate_a, gate_b = jnp.split(jax.nn.sigmoid(gate_cols), N_BRANCHES, axis=-1)
        mixed = (gate_a * branch_a + gate_b * branch_b) @ w_mix_out[l]
        x = x + gt1 * _rms(mixed, g_post_mix[l])
        h = _rms(x, g_pre_ffn[l]) * (1.0 + sc2) + sh2
        g, up = jnp.split(h @ w_ffn_in[l], 2, axis=-1)
        f = (jax.nn.silu(g) * up) @ w_ffn_out[l]
        x = x + gt2 * _rms(f, g_post_ffn[l])
    return x
```

```python
import math
from contextlib import ExitStack
import numpy as np
import concourse.bass as bass
import concourse.mybir as mybir
from concourse.bass_utils import run_bass_kernel_spmd

F32 = mybir.dt.float32
BF16 = mybir.dt.bfloat16
I32 = mybir.dt.int32
AF = mybir.ActivationFunctionType
ALU = mybir.AluOpType
AX = mybir.AxisListType

D = 1024
SEQ = 4096
OWN = 2048
NH = 8
QLR = 384
KVR = 256
ROPE = 32
SSMC = 512
NG = 32
PST = 64
FFH = 2816
EPS = 1e-6
IN_COLS = 3232
DEBUG = {}


class Op:
    __slots__ = ("eng", "fn", "reads", "writes", "dma", "deps", "needed", "token", "grp")

    def __init__(self, eng, fn, reads, writes, dma, grp=None):
        self.eng, self.fn, self.reads, self.writes, self.dma = eng, fn, reads, writes, dma
        self.grp = grp
        self.deps = ()
        self.needed = False
        self.token = None


class Prog:
    ENGS = ("pe", "act", "dve", "pool", "sp")

    def __init__(self, nc, es):
        self.nc, self.es = nc, es
        self.ops = []
        self.psem = {e: es.enter_context(nc.semaphore("ps_" + e)) for e in self.ENGS}
        self.pcnt = {e: 0 for e in self.ENGS}
        self.dsem = {}
        self.dcnt = {}
        self.waited = {e: {} for e in self.ENGS}
        self.sb_off = 16448
        self.sb_mark = 16448
        self.nalloc = 0

    def sb(self, shape, dt, name=None):
        nbytes = int(np.prod(shape[1:])) * (4 if dt in (F32, I32) else 2)
        nbytes = (nbytes + 63) // 64 * 64
        off = self.sb_off
        self.sb_off += nbytes
        assert self.sb_off <= 229000, ("SBUF overflow", self.sb_off, name)
        self.nalloc += 1
        t = self.nc.alloc_sbuf_tensor_at("%s_%d" % (name or "t", self.nalloc), list(shape), dt, offset=off)
        return t.ap()

    def skip(self, shape, dt):
        nbytes = int(np.prod(shape[1:])) * (4 if dt in (F32, I32) else 2)
        self.sb_off += (nbytes + 63) // 64 * 64
        assert self.sb_off <= 229000, ("SBUF overflow", self.sb_off)

    def mark(self):
        self.sb_mark = self.sb_off

    def release(self):
        self.sb_off = self.sb_mark

    grp = None

    def op(self, eng, fn, r=(), w=()):
        self.ops.append(Op(eng, fn, tuple(r), tuple(w), False, self.grp))

    def dma(self, eng, fn, r=(), w=()):
        self.ops.append(Op(eng, fn, tuple(r), tuple(w), True, self.grp))

    def capture(self, fn):
        saved, self.ops = self.ops, []
        fn()
        out, self.ops = self.ops, saved
        return out

    def merge(self, lists):
        lists = [l for l in lists if l]
        pos = [0] * len(lists)
        total = sum(len(l) for l in lists)
        while sum(pos) < total:
            best, bi = None, -1
            for i, l in enumerate(lists):
                if pos[i] < len(l):
                    frac = pos[i] / len(l)
                    if best is None or frac < best:
                        best, bi = frac, i
            l = lists[bi]
            g = l[pos[bi]].grp
            self.ops.append(l[pos[bi]])
            pos[bi] += 1
            while g is not None and pos[bi] < len(l) and l[pos[bi]].grp == g:
                self.ops.append(l[pos[bi]])
                pos[bi] += 1

    def flush(self):
        ops, self.ops = self.ops, []
        lastw, readers = {}, {}
        for i, o in enumerate(ops):
            deps = set()
            for k in o.reads:
                if k in lastw:
                    deps.add(lastw[k])
            for k in o.writes:
                if k in lastw:
                    deps.add(lastw[k])
                deps |= readers.get(k, set())
            deps.discard(i)
            o.deps = sorted(deps)
            for d in deps:
                ops[d].needed = True
            for k in o.reads:
                readers.setdefault(k, set()).add(i)
            for k in o.writes:
                lastw[k] = i
                readers[k] = set()
        last = {}
        for i, o in enumerate(ops):
            if not o.dma:
                last[o.eng] = i
        for i in last.values():
            ops[i].needed = True
        used_d = set()
        for o in ops:
            if o.dma:
                key = o.writes[0]
                if key not in self.dsem:
                    self.dsem[key] = self.es.enter_context(self.nc.semaphore("ds%d" % len(self.dsem)))
                    self.dcnt[key] = 0
                o.token = [self.dsem[key], None, key]
                used_d.add(key)
            elif o.needed:
                self.pcnt[o.eng] += 1
                o.token = (self.psem[o.eng], self.pcnt[o.eng])
        for o in ops:
            if o.dma:
                self.dcnt[o.token[2]] += 16 * o.fn.ndma
                o.token = (o.token[0], self.dcnt[o.token[2]])
        finals = {e: self.pcnt[e] for e in self.ENGS}
        dfinals = [(self.dsem[k], self.dcnt[k]) for k in sorted(used_d)]
        nc = self.nc
        prog = self

        def emit(ename, e):
            wt = prog.waited[ename]

            def wait(sem, val):
                if wt.get(sem.num, 0) < val:
                    e.wait_ge(sem, val)
                    wt[sem.num] = val

            for o in ops:
                if o.eng != ename:
                    continue
                need = {}
                for d in o.deps:
                    sem, val = ops[d].token
                    if need.get(sem.num, (None, 0))[1] < val:
                        need[sem.num] = (sem, val)
                for sem, val in need.values():
                    wait(sem, val)
                res = o.fn(e)
                if o.dma:
                    for ins in res:
                        ins.then_inc(o.token[0], 16)
                elif o.needed:
                    res.then_inc(o.token[0], 1)
            for e2 in prog.ENGS:
                if e2 != ename and finals[e2] > 0:
                    wait(prog.psem[e2], finals[e2])
            for sem, val in dfinals:
                wait(sem, val)

        with nc.Block() as block:
            @block.tensor
            def _(e):
                emit("pe", e)

            @block.scalar
            def _(e):
                emit("act", e)

            @block.vector
            def _(e):
                emit("dve", e)

            @block.gpsimd
            def _(e):
                emit("pool", e)

            @block.sync
            def _(e):
                emit("sp", e)


class DmaFn:
    def __init__(self, pairs, cast=False):
        self.pairs = pairs
        self.ndma = len(pairs)

    def __call__(self, e):
        return [e.dma_start(out=o, in_=i) for (o, i) in self.pairs]


def build(nc, dbg=()):
    es = ExitStack()
    P = Prog(nc, es)

    def din(name, shape, dt=F32):
        return nc.dram_tensor(name, list(shape), dt, kind="ExternalInput").ap()

    def dscr0(name, shape, dt):
        return nc.dram_tensor(name, list(shape), dt, kind="Internal").ap()

    x_d = din("x_loc", [SEQ, D])
    pos_d = din("pos_loc", [128, 32], I32)
    c_d = din("c_col", [128, 8])
    w_ada_d = din("w_ada", [D, 6 * D])
    b_ada_d = din("b_ada", [1, 6 * D])
    gvec_d = din("gvec", [1, 4 * D])
    w_in_d = din("w_in", [D, IN_COLS])
    gq_d = din("g_q", [1, QLR])
    gkv_d = din("g_kv", [1, KVR])
    w_uq_d = din("w_uq", [QLR, 768])
    w_uk_d = din("w_uk", [KVR, 512])
    w_uv_d = din("w_uv", [KVR, 512])
    w_o_d = din("w_o", [512, D])
    lam_d = din("lam", [64, 2, 64])
    ldt_d = din("ldt", [64, 64])
    sb_d = din("ssm_b", [64, 2, 64, 16])
    sc_d = din("ssm_c", [64, 2, 64, 16])
    sd_d = din("ssm_d", [128, 4])
    w_glu_d = din("w_glu", [SSMC, 2 * D])
    w_mix_d = din("w_mix", [D, D])
    w_f1_d = din("w_f1", [D, 2 * FFH])
    w_f2_d = din("w_f2", [FFH, D])
    out_d = nc.dram_tensor("out_loc", [OWN, D], F32, kind="ExternalOutput").ap()
    dbg_d = {}
    for nm, shp in dbg:
        dbg_d[nm] = nc.dram_tensor("dbg_" + nm, list(shp), F32, kind="ExternalOutput").ap()

    def dscr(name, shape, dt):
        return nc.dram_tensor(name, list(shape), dt, kind="Internal").ap()

    gate_s = dscr("gate_s", [16, 128, OWN], BF16)
    brb_s = dscr("brb_s", [8, 128, OWN], BF16)

    ident = P.sb([128, 128], BF16, "ident")
    identf = P.sb([128, 128], F32, "identf")
    ones_f = P.sb([128, 128], F32, "ones_f")
    AB_s = dscr0("AB_s", [128, 6, D], F32)
    cs_tab = P.sb([128, 32, 2, 16], F32, "cs_tab")
    iota_f_keep = P.sb([128, 128], F32, "iota_f_keep")
    eps_t = P.sb([128, 1], F32, "eps_t")
    P.mark()
    pf = [nc.alloc_psum_tensor("pf%d" % i, [128, 512], F32).ap() for i in range(8)]
    pb = [pf[6].bitcast(BF16), pf[7].bitcast(BF16)]
    ps_ada = pf[0]

    def act(fn, r=(), w=()):
        P.op("act", fn, r, w)

    def dve(fn, r=(), w=()):
        P.op("dve", fn, r, w)

    def pool(fn, r=(), w=()):
        P.op("pool", fn, r, w)

    def pe(fn, r=(), w=()):
        P.op("pe", fn, r, w)

    def ld(pairs, r=(), w=(), eng="sp"):
        P.dma(eng, DmaFn(pairs), r, w)

    w_in_bf0 = P.sb([128, 8, IN_COLS], BF16, "w_in_bf")
    w_uq_bf0 = P.sb([128, 3, 768], BF16, "w_uq_bf")
    w_uk_bf0 = P.sb([128, 2, 512], BF16, "w_uk_bf")
    w_uv_bf0 = P.sb([128, 2, 512], BF16, "w_uv_bf")
    w_in_v0 = w_in_d.rearrange("(k p) n -> p k n", p=128)
    ld([(w_in_bf0[:, k0:k0 + 2, :], w_in_v0[:, k0:k0 + 2, :]) for k0 in range(0, 8, 2)], w=["w_in_bf"], eng="pool")
    ld([(w_uq_bf0, w_uq_d.rearrange("(k p) n -> p k n", p=128))], w=["w_uq_bf"], eng="pool")
    ld([(w_uk_bf0, w_uk_d.rearrange("(k p) n -> p k n", p=128))], w=["w_uk_bf"], eng="pool")
    ld([(w_uv_bf0, w_uv_d.rearrange("(k p) n -> p k n", p=128))], w=["w_uv_bf"], eng="pool")
    AB = P.sb([128, 6, D], F32, "AB")
    iota_i = P.sb([128, 128], I32, "iota_i")
    iota_f = iota_f_keep
    pool(lambda e: e.iota(iota_i, pattern=[[1, 128]], base=0, channel_multiplier=-1), w=["iota_i"])
    dve(lambda e: e.tensor_copy(out=iota_f, in_=iota_i), r=["iota_i"], w=["iota_f"])
    dve(lambda e: e.tensor_single_scalar(out=identf, in_=iota_f, scalar=0.0, op=ALU.is_equal), r=["iota_f"], w=["identf"])
    dve(lambda e: e.tensor_copy(out=ident, in_=identf), r=["identf"], w=["ident"])
    dve(lambda e: e.memset(ones_f, 1.0), w=["ones_f"])
    dve(lambda e: e.memset(eps_t, EPS), w=["eps_t"])

    c_sb = P.sb([128, 8], F32, "c_sb")
    sc_sb = P.sb([128, 8], F32, "sc_sb")
    ld([(c_sb, c_d)], w=["c_sb"])
    act(lambda e: e.activation(out=sc_sb, in_=c_sb, func=AF.Silu), r=["c_sb"], w=["sc_sb"])
    ada_row = P.sb([1, 6 * D], F32, "ada_row")
    bada = P.sb([1, 6 * D], F32, "bada")
    gv = P.sb([1, 4 * D], F32, "gv")
    ld([(bada, b_ada_d)], w=["bada"])
    ld([(gv, gvec_d)], w=["gv"])
    wst = [P.sb([128, 8, 512], F32, "wst%d" % i) for i in range(2)]
    w_ada_v = w_ada_d.rearrange("(k p) n -> p k n", p=128)
    for ct in range(12):
        wb = wst[ct % 2]
        key = "wst%d" % (ct % 2)
        ld([(wb[:, 0:4, :], w_ada_v[:, 0:4, ct * 512:(ct + 1) * 512]),
            (wb[:, 4:8, :], w_ada_v[:, 4:8, ct * 512:(ct + 1) * 512])], w=[key])
        ps = ps_ada

        def mm(e, wb=wb, ps=ps):
            ins = None
            for k in range(8):
                ins = e.matmul(ps[0:1, :], lhsT=sc_sb[:, k:k + 1], rhs=wb[:, k, :], start=(k == 0), stop=(k == 7))
            return ins
        pe(mm, r=[key, "sc_sb"], w=["ps_ada"])
        dve(lambda e, ct=ct, ps=ps: e.tensor_tensor(out=ada_row[0:1, ct * 512:(ct + 1) * 512], in0=ps[0:1, :],
                                                  in1=bada[0:1, ct * 512:(ct + 1) * 512], op=ALU.add),
            r=["ps_ada", "bada"], w=["ada_row"])
    rows = bada.rearrange("p (k n) -> p k n", k=6)

    def seg(k):
        return ada_row[0:1, k * D:(k + 1) * D]

    def gseg(k):
        return gv[0:1, k * D:(k + 1) * D]
    dve(lambda e: e.scalar_tensor_tensor(out=rows[0:1, 0, :], in0=seg(1), scalar=1.0, in1=gseg(0), op0=ALU.add, op1=ALU.mult),
        r=["ada_row", "gv"], w=["rows"])
    dve(lambda e: e.tensor_copy(out=rows[0:1, 1, :], in_=seg(0)), r=["ada_row", "rows"], w=["rows"])
    dve(lambda e: e.tensor_tensor(out=rows[0:1, 2, :], in0=seg(2), in1=gseg(1), op=ALU.mult), r=["ada_row", "gv", "rows"], w=["rows"])
    dve(lambda e: e.scalar_tensor_tensor(out=rows[0:1, 3, :], in0=seg(4), scalar=1.0, in1=gseg(2), op0=ALU.add, op1=ALU.mult),
        r=["ada_row", "gv", "rows"], w=["rows"])
    dve(lambda e: e.tensor_copy(out=rows[0:1, 4, :], in_=seg(3)), r=["ada_row", "rows"], w=["rows"])
    dve(lambda e: e.tensor_tensor(out=rows[0:1, 5, :], in0=seg(5), in1=gseg(3), op=ALU.mult), r=["ada_row", "gv", "rows"], w=["rows"])
    for k in range(6):
        for hh in range(2):
            pe(lambda e, k=k, hh=hh: e.matmul(ps_ada[:, :], lhsT=ones_f[0:1, :], rhs=rows[0:1, k, hh * 512:(hh + 1) * 512],
                                             start=True, stop=True), r=["rows", "ones_f", "ps_ada"], w=["ps_ada"])
            act(lambda e, k=k, hh=hh: e.copy(out=AB[:, k, hh * 512:(hh + 1) * 512], in_=ps_ada[:, :]), r=["ps_ada"], w=["AB"])

    pos_i = P.sb([128, 32], I32, "pos_i")
    pos_f = P.sb([128, 32], F32, "pos_f")
    ld([(pos_i, pos_d)], w=["pos_i"])
    dve(lambda e: e.tensor_copy(out=pos_f, in_=pos_i), r=["pos_i"], w=["pos_f"])
    invf = P.sb([128, 16], F32, "invf")
    for j in range(16):
        val = float(np.float32(10000.0) ** np.float32(-(2.0 * j) / 32.0)) / (2.0 * math.pi)
        dve(lambda e, j=j, val=val: e.memset(invf[:, j:j + 1], val), r=["invf"], w=["invf"])
    turns = P.sb([128, 32, 2, 16], F32, "turns")
    tint = P.sb([128, 32, 2, 16], I32, "tint")
    tfl = P.sb([128, 32, 2, 16], F32, "tfl")
    for t in range(32):
        dve(lambda e, t=t: e.tensor_scalar(out=turns[:, t, 1, :], in0=invf, scalar1=pos_f[:, t:t + 1], scalar2=None, op0=ALU.mult),
            r=["invf", "pos_f", "turns"], w=["turns"])
    dve(lambda e: e.tensor_scalar_add(out=turns[:, :, 0, :], in0=turns[:, :, 1, :], scalar1=0.25), r=["turns"], w=["turns"])
    dve(lambda e: e.tensor_copy(out=tint, in_=turns), r=["turns"], w=["tint"])
    dve(lambda e: e.tensor_copy(out=tfl, in_=tint), r=["tint"], w=["tfl"])
    dve(lambda e: e.tensor_tensor(out=turns, in0=turns, in1=tfl, op=ALU.subtract), r=["turns", "tfl"], w=["turns"])
    dve(lambda e: e.tensor_single_scalar(out=tfl, in_=turns, scalar=0.5, op=ALU.is_gt), r=["turns", "tfl"], w=["tfl"])
    dve(lambda e: e.tensor_tensor(out=turns, in0=turns, in1=tfl, op=ALU.subtract), r=["turns", "tfl"], w=["turns"])
    dve(lambda e: e.tensor_single_scalar(out=tfl, in_=turns, scalar=-0.5, op=ALU.is_lt), r=["turns", "tfl"], w=["tfl"])
    dve(lambda e: e.tensor_tensor(out=turns, in0=turns, in1=tfl, op=ALU.add), r=["turns", "tfl"], w=["turns"])
    act(lambda e: e.activation(out=cs_tab, in_=turns, func=AF.Sin, scale=2.0 * math.pi), r=["turns"], w=["cs_tab"])
    ld([(AB_s, AB)], r=["AB"], w=["AB_s"], eng="pool")
    if "AB" in dbg_d:
        ld([(dbg_d["AB"], AB[0:1, :, :])], r=["AB"], w=["dbg_AB"], eng="pool")
    if "cs" in dbg_d:
        ld([(dbg_d["cs"], cs_tab)], r=["cs_tab"], w=["dbg_cs"], eng="pool")
    P.flush()
    P.release()
    L = dict(locals())
    L["iota_f_p"] = L["iota_f_keep"]
    stage1(L)
    stage_gates(L)
    s5_main, s5_tail, base = stage_s5(L)
    attn_core, attn_proj = stage_attn(L)
    P.merge([P.capture(s5_main), P.capture(attn_core)])
    P.flush()
    P.sb_off = P.sb_mark = base
    ABf0 = P.sb([128, 4, D], F32, "ABf")
    wm_bf0 = P.sb([128, 8, D], BF16, "wm_bf")
    w2_bf0 = P.sb([128, 22, D], BF16, "w2_bf")
    L["ABf0"], L["wm_bf0"], L["w2_bf0"] = ABf0, wm_bf0, w2_bf0
    la = P.capture(s5_tail)
    lb = P.capture(attn_proj)
    P.merge([la, lb])
    ld([(ABf0, AB_s[:, 2:6, :])], w=["ABf"])
    ld([(wm_bf0, w_mix_d.rearrange("(k p) n -> p k n", p=128))], w=["wm_bf"], eng="pool")
    ld([(w2_bf0[:, 0:11, :], w_f2_d[0:1408, :].rearrange("(k p) n -> p k n", p=128)),
        (w2_bf0[:, 11:22, :], w_f2_d[1408:2816, :].rearrange("(k p) n -> p k n", p=128))], w=["w2_bf"], eng="pool")
    P.flush()
    P.sb_off = P.sb_mark = base
    stage_ffn(L)
    return P, es, L


def rms_ops(P, L, src, n, ssq, rs, rstd, scratch, key_src, tag, sqkey="sqbuf"):
    act, dve = L["act"], L["dve"]
    act(lambda e: e.activation(out=scratch, in_=src, func=AF.Square), r=[key_src], w=[sqkey])
    dve(lambda e: e.reduce_sum(out=ssq, in_=scratch, axis=AX.X), r=[sqkey], w=["ssq" + tag])
    act(lambda e: e.activation(out=rs, in_=ssq, func=AF.Sqrt, scale=1.0 / n, bias=L["eps_t"][:, 0:1]), r=["ssq" + tag], w=["rs" + tag])
    dve(lambda e: e.reciprocal(out=rstd, in_=rs), r=["rs" + tag], w=["rstd" + tag])


def stage1(L):
    P, nc = L["P"], L["nc"]
    act, dve, pool, pe, ld = L["act"], L["dve"], L["pool"], L["pe"], L["ld"]
    pf, pb, ident, cs_tab, ones_f = L["pf"], L["pb"], L["ident"], L["cs_tab"], L["ones_f"]
    x_d, w_in_d = L["x_d"], L["w_in_d"]
    w_in_bf, w_uq_bf, w_uk_bf, w_uv_bf = L["w_in_bf0"], L["w_uq_bf0"], L["w_uk_bf0"], L["w_uv_bf0"]
    P.skip([128, 8, IN_COLS], BF16); P.skip([128, 3, 768], BF16); P.skip([128, 2, 512], BF16); P.skip([128, 2, 512], BF16)
    AB = P.sb([128, 2, D], F32, "AB1")
    ld([(AB, L["AB_s"][:, 0:2, :])], w=["AB"])

    def dscr(name, shape, dt):
        return nc.dram_tensor(name, list(shape), dt, kind="Internal").ap()
    KT_s = L["KT_s"] = dscr("KT_s", [96, 8, SEQ], BF16)
    V_s = L["V_s"] = dscr("V_s", [SEQ, 8 * 65], BF16)
    QT_s = L["QT_s"] = dscr("QT_s", [96, 8, OWN], BF16)
    uT_s = L["uT_s"] = dscr("uT_s", [SSMC, SEQ], BF16)
    uT2_s = L["uT2_s"] = dscr("uT2_s", [SSMC, 8, SEQ // 8], BF16)
    hT_s = L["hT_s"] = dscr("hT_s", [4, 128, 8, 512], BF16)
    osb2 = [P.sb([128, 8, 64], BF16, "osc%d" % i) for i in range(2)]
    gate_s = L["gate_s"]
    dbg_d = L["dbg_d"]

    eps_t = L["eps_t"]
    gq_b = P.sb([128, QLR], F32, "gq_b")
    gkv_b = P.sb([128, KVR], F32, "gkv_b")
    grow = P.sb([1, QLR + KVR], F32, "grow")
    ld([(grow[0:1, 0:QLR], L["gq_d"]), (grow[0:1, QLR:], L["gkv_d"])], w=["grow"])
    pe(lambda e: e.matmul(pf[0][:, 0:QLR], lhsT=ones_f[0:1, :], rhs=grow[0:1, 0:QLR], start=True, stop=True), r=["grow", "pf0"], w=["pf0"])
    act(lambda e: e.copy(out=gq_b, in_=pf[0][:, 0:QLR]), r=["pf0"], w=["gq_b"])
    pe(lambda e: e.matmul(pf[0][:, 0:KVR], lhsT=ones_f[0:1, :], rhs=grow[0:1, QLR:], start=True, stop=True), r=["grow", "pf0"], w=["pf0"])
    act(lambda e: e.copy(out=gkv_b, in_=pf[0][:, 0:KVR]), r=["pf0"], w=["gkv_b"])

    xt = [P.sb([128, D], F32, "xt%d" % i) for i in range(2)]
    sq = P.sb([128, D], F32, "sq")
    tmp = P.sb([128, D], F32, "tmp")
    hb = P.sb([128, D], BF16, "hb")
    hT = P.sb([128, 8, 512], BF16, "hT")
    st = P.sb([128, 16], F32, "st")
    osb = [P.sb([128, 512], BF16, "osb%d" % i) for i in range(2)]
    QSC = float(96.0 ** -0.5)
    oi = 0
    hTs = [hT, P.sb([128, 8, 512], BF16, "hT1")]
    sqx = P.sb([128, D], F32, "sqx")
    oi_box = [0]

    def chain(bi):
        hTb, hk = hTs[bi % 2], "hT%d" % (bi % 2)
        for tt in range(4):
            t = bi * 4 + tt
            xb, xk = xt[t % 2], "xt%d" % (t % 2)
            ld([(xb[:, 0:512], x_d[t * 128:(t + 1) * 128, 0:512]), (xb[:, 512:], x_d[t * 128:(t + 1) * 128, 512:])], w=[xk])
            rms_ops(P, L, xb, D, st[:, 0:1], st[:, 1:2], st[:, 2:3], sqx, xk, "x", "sqx")
            dve(lambda e, xb=xb: e.scalar_tensor_tensor(out=tmp, in0=xb, scalar=st[:, 2:3], in1=AB[:, 0, :], op0=ALU.mult, op1=ALU.mult),
                r=[xk, "rstdx", "AB"], w=["tmp"])
            dve(lambda e: e.tensor_tensor(out=hb, in0=tmp, in1=AB[:, 1, :], op=ALU.add), r=["tmp", "AB"], w=["hb"])

            def tr(e):
                ins = None
                for k in range(8):
                    ins = e.transpose(pb[0][:, k * 128:(k + 1) * 128], hb[:, k * 128:(k + 1) * 128], ident)
                return ins
            pe(tr, r=["hb", "ident"], w=["pb0"])
            act(lambda e, tt=tt, hTb=hTb: e.copy(out=hTb[:, :, tt * 128:(tt + 1) * 128], in_=pb[0].rearrange("p (k t) -> p k t", k=8)),
                r=["pb0"], w=[hk])

    def proj(bi):
        own = bi < 4
        hT, hk = hTs[bi % 2], "hT%d" % (bi % 2)
        oi = oi_box[0]
        bc = slice(bi * 512, (bi + 1) * 512)
        cols = [(672 + 128 * j, ("u", j)) for j in range(4)]
        if own:
            ld([(hT_s[bi], hT)], r=[hk], w=["hT_s"], eng="pool")
        for c0, (kind, j) in cols:
            psx, pk = pf[2], "pf2"
            ob, ok = osb[oi % 2], "osb%d" % (oi % 2)
            oi += 1
            oi_box[0] = oi

            def mm(e, c0=c0, psx=psx):
                ins = None
                for k in range(8):
                    ins = e.matmul(psx, lhsT=w_in_bf[:, k, c0:c0 + 128], rhs=hT[:, k, :], start=(k == 0), stop=(k == 7))
                return ins
            pe(mm, r=["w_in_bf", hk, pk], w=[pk])
            if kind == "u":
                act(lambda e, psx=psx, ob=ob: e.copy(out=ob, in_=psx), r=[pk], w=[ok])
                ld([(uT_s[j * 128:(j + 1) * 128, bc], ob)], r=[ok], w=["uT_s"], eng="pool")
                o2, o2k = osb2[j % 2], "osc%d" % (j % 2)
                dve(lambda e, ob=ob, o2=o2: e.tensor_copy(out=o2, in_=ob.rearrange("p (c s) -> p s c", s=8)), r=[ok], w=[o2k])
                ld([(uT2_s[j * 128:(j + 1) * 128, :, bi * 64:(bi + 1) * 64], o2)], r=[o2k], w=["uT2_s"], eng="pool")
            else:
                act(lambda e, psx=psx, ob=ob: e.activation(out=ob, in_=psx, func=AF.Sigmoid), r=[pk], w=[ok])
                ld([(gate_s[j, :, bc], ob)], r=[ok], w=["gate_s"], eng="pool")
        for t0 in (0, 2):
            P.merge([P.capture(lambda: tile_ops(bi, t0, hT, hk, own)), P.capture(lambda: tile_ops(bi, t0 + 1, hT, hk, own))])

    class TRes:
        def __init__(self, par, lat, latk, big, bigk, trb, trbk):
            self.p = str(par)
            self.lat, self.latk, self.big, self.bigk, self.trb, self.trbk = lat, latk, big, bigk, trb, trbk
            n = "_%d" % par
            self.st = P.sb([128, 16], F32, "tst" + n)
            self.sq = P.sb([128, QLR], F32, "tsq" + n)
            self.kvn = P.sb([128, KVR], BF16, "kvn" + n)
            self.kvnT = P.sb([128, 2, 128], BF16, "kvnT" + n)
            self.rp = P.sb([128, 6, 16], F32, "rp" + n)
            self.kr = P.sb([128, 32], BF16, "kr" + n)
            self.kasm = P.sb([128, 8, 96], BF16, "kasm" + n)
            self.vsb = P.sb([128, 8, 65], BF16, "vsb" + n)
            self.ktsb = P.sb([128, 8, 128], BF16, "ktsb" + n)
            self.qn = P.sb([128, QLR], BF16, "qn" + n)
            self.qnT = P.sb([128, 3, 128], BF16, "qnT" + n)
            self.qf = P.sb([128, 8, 96], F32, "qf" + n)
            self.qr = P.sb([128, 4, 8, 16], F32, "qr" + n)
            self.qasm = P.sb([128, 8, 96], BF16, "qasm" + n)
            self.qtsb = P.sb([128, 8, 128], BF16, "qtsb" + n)
            vs = self.vsb
            dve(lambda e: e.memset(vs, 1.0), w=["vsb" + self.p])
    tres = [TRes(0, pf[3], "pf3", pf[4], "pf4", pb[1], "pb1"),
            TRes(1, pf[0], "pf0", pf[1], "pf1", pf[5].bitcast(BF16), "pf5")]

    def tile_ops(bi, tt, hT, hk, own):
        R = tres[tt % 2]
        p_ = R.p
        K = lambda name: name + p_
        t = bi * 4 + tt
        tk = slice(tt * 128, (tt + 1) * 128)
        lat, latk, big, bigk, trb, trbk, st = R.lat, R.latk, R.big, R.bigk, R.trb, R.trbk, R.st
        kvn, kvnT, rp, kr, kasm, vsb, ktsb = R.kvn, R.kvnT, R.rp, R.kr, R.kasm, R.vsb, R.ktsb
        qn, qnT, qf, qr, qasm, qtsb, sq = R.qn, R.qnT, R.qf, R.qr, R.qasm, R.qtsb, R.sq

        def mmkv(e):
            ins = None
            for k in range(8):
                ins = e.matmul(lat[:, 0:288], lhsT=hT[:, k, tk], rhs=w_in_bf[:, k, 384:672], start=(k == 0), stop=(k == 7))
            return ins
        pe(mmkv, r=["w_in_bf", hk, latk], w=[latk])
        rms_ops(P, L, lat[:, 0:KVR], KVR, st[:, 4:5], st[:, 5:6], st[:, 6:7], sq[:, 0:KVR], latk, "kv" + p_, K("sqp"))
        dve(lambda e: e.scalar_tensor_tensor(out=kvn, in0=lat[:, 0:KVR], scalar=st[:, 6:7], in1=gkv_b, op0=ALU.mult, op1=ALU.mult),
            r=[latk, "rstdkv" + p_, "gkv_b"], w=[K("kvn")])
        cosv, sinv = cs_tab[:, t, 0, :], cs_tab[:, t, 1, :]
        x1, x2 = lat[:, 256:272], lat[:, 272:288]
        rk = [latk, "cs_tab", K("rp"), "rstdkv" + p_]
        dve(lambda e: e.tensor_tensor(out=rp[:, 0, :], in0=x1, in1=cosv, op=ALU.mult), r=rk, w=[K("rp")])
        dve(lambda e: e.tensor_tensor(out=rp[:, 1, :], in0=x2, in1=sinv, op=ALU.mult), r=rk, w=[K("rp")])
        dve(lambda e: e.tensor_tensor(out=rp[:, 2, :], in0=x2, in1=cosv, op=ALU.mult), r=rk, w=[K("rp")])
        dve(lambda e: e.tensor_tensor(out=rp[:, 3, :], in0=x1, in1=sinv, op=ALU.mult), r=rk, w=[K("rp")])
        dve(lambda e: e.tensor_tensor(out=kr[:, 0:16], in0=rp[:, 0, :], in1=rp[:, 1, :], op=ALU.subtract), r=[K("rp")], w=[K("kr")])
        dve(lambda e: e.tensor_tensor(out=kr[:, 16:32], in0=rp[:, 2, :], in1=rp[:, 3, :], op=ALU.add), r=[K("rp"), K("kr")], w=[K("kr")])
        dve(lambda e: e.tensor_copy(out=kasm[:, :, 64:96], in_=kr.unsqueeze(1).to_broadcast([128, 8, 32])), r=[K("kr"), K("kasm")], w=[K("kasm")])

        def tr2(e):
            ins = None
            for k in range(2):
                ins = e.transpose(trb[:, k * 128:(k + 1) * 128], kvn[:, k * 128:(k + 1) * 128], ident)
            return ins
        pe(tr2, r=[K("kvn"), "ident", trbk], w=[trbk])
        act(lambda e: e.copy(out=kvnT, in_=trb[:, 0:256].rearrange("p (k t) -> p k t", k=2)), r=[trbk], w=[K("kvnT")])

        def mmk(e, wsb):
            ins = None
            for k in range(2):
                ins = e.matmul(big, lhsT=kvnT[:, k, :], rhs=wsb[:, k, :], start=(k == 0), stop=(k == 1))
            return ins
        pe(lambda e: mmk(e, w_uk_bf), r=[K("kvnT"), "w_uk_bf", bigk], w=[bigk])
        act(lambda e: e.copy(out=kasm[:, :, 0:64], in_=big.rearrange("p (h d) -> p h d", h=8)), r=[bigk, K("kasm")], w=[K("kasm")])
        pe(lambda e: mmk(e, w_uv_bf), r=[K("kvnT"), "w_uv_bf", bigk], w=[bigk])
        act(lambda e: e.copy(out=vsb[:, :, 0:64], in_=big.rearrange("p (h d) -> p h d", h=8)), r=[bigk, K("vsb")], w=[K("vsb")])
        ld([(V_s[t * 128:(t + 1) * 128, :], vsb.rearrange("p h d -> p (h d)"))], r=[K("vsb")], w=["V_s" + p_], eng="pool")

        def trk(e, src):
            ins = None
            for h in range(8):
                ins = e.transpose(trb[0:96, h * 128:(h + 1) * 128], src[:, h, :], ident)
            return ins
        pe(lambda e: trk(e, kasm), r=[K("kasm"), "ident", trbk], w=[trbk])
        act(lambda e: e.copy(out=ktsb[0:96, :, :], in_=trb[0:96, :].rearrange("p (h t) -> p h t", h=8)), r=[trbk], w=[K("ktsb")])
        ld([(KT_s[:, :, t * 128:(t + 1) * 128], ktsb[0:96, :, :])], r=[K("ktsb")], w=["KT_s" + p_], eng="pool")
        if not own:
            return
        def mmq(e):
            ins = None
            for k in range(8):
                ins = e.matmul(lat[:, 0:QLR], lhsT=hT[:, k, tk], rhs=w_in_bf[:, k, 0:QLR], start=(k == 0), stop=(k == 7))
            return ins
        pe(mmq, r=["w_in_bf", hk, latk], w=[latk])
        rms_ops(P, L, lat[:, 0:QLR], QLR, st[:, 8:9], st[:, 9:10], st[:, 10:11], sq[:, 0:QLR], latk, "q" + p_, K("sqp"))
        dve(lambda e: e.scalar_tensor_tensor(out=qn, in0=lat[:, 0:QLR], scalar=st[:, 10:11], in1=gq_b, op0=ALU.mult, op1=ALU.mult),
            r=[latk, "rstdq" + p_, "gq_b"], w=[K("qn")])

        def tr3(e):
            ins = None
            for k in range(3):
                ins = e.transpose(trb[:, k * 128:(k + 1) * 128], qn[:, k * 128:(k + 1) * 128], ident)
            return ins
        pe(tr3, r=[K("qn"), "ident", trbk], w=[trbk])
        act(lambda e: e.copy(out=qnT, in_=trb[:, 0:384].rearrange("p (k t) -> p k t", k=3)), r=[trbk], w=[K("qnT")])

        def mmq2(e):
            ins = None
            for (psx, c0, cw) in ((big, 0, 512), (lat, 512, 256)):
                for k in range(3):
                    ins = e.matmul(psx[:, 0:cw], lhsT=qnT[:, k, :], rhs=w_uq_bf[:, k, c0:c0 + cw], start=(k == 0), stop=(k == 2))
            return ins
        pe(mmq2, r=[K("qnT"), "w_uq_bf", bigk, latk], w=[bigk, latk])
        qfl = qf.rearrange("p h d -> p (h d)")
        act(lambda e: e.activation(out=qfl[:, 0:512], in_=big, func=AF.Copy, scale=QSC), r=[bigk, K("qf")], w=[K("qf")])
        act(lambda e: e.activation(out=qfl[:, 512:768], in_=lat[:, 0:256], func=AF.Copy, scale=QSC), r=[latk, K("qf")], w=[K("qf")])
        dve(lambda e: e.tensor_copy(out=qasm[:, :, 0:64], in_=qf[:, :, 0:64]), r=[K("qf"), K("qasm")], w=[K("qasm")])
        cb_ = cosv.unsqueeze(1).to_broadcast([128, 8, 16])
        sb_ = sinv.unsqueeze(1).to_broadcast([128, 8, 16])
        qk = [K("qf"), "cs_tab", K("qr")]
        dve(lambda e: e.tensor_tensor(out=qr[:, 0], in0=qf[:, :, 64:80], in1=cb_, op=ALU.mult), r=qk, w=[K("qr")])
        dve(lambda e: e.tensor_tensor(out=qr[:, 1], in0=qf[:, :, 80:96], in1=sb_, op=ALU.mult), r=qk, w=[K("qr")])
        dve(lambda e: e.tensor_tensor(out=qr[:, 2], in0=qf[:, :, 80:96], in1=cb_, op=ALU.mult), r=qk, w=[K("qr")])
        dve(lambda e: e.tensor_tensor(out=qr[:, 3], in0=qf[:, :, 64:80], in1=sb_, op=ALU.mult), r=qk, w=[K("qr")])
        dve(lambda e: e.tensor_tensor(out=qasm[:, :, 64:80], in0=qr[:, 0], in1=qr[:, 1], op=ALU.subtract), r=[K("qr"), K("qasm")], w=[K("qasm")])
        dve(lambda e: e.tensor_tensor(out=qasm[:, :, 80:96], in0=qr[:, 2], in1=qr[:, 3], op=ALU.add), r=[K("qr"), K("qasm")], w=[K("qasm")])
        pe(lambda e: trk(e, qasm), r=[K("qasm"), "ident", trbk], w=[trbk])
        act(lambda e: e.copy(out=qtsb[0:96, :, :], in_=trb[0:96, :].rearrange("p (h t) -> p h t", h=8)), r=[trbk], w=[K("qtsb")])
        ld([(QT_s[:, :, t * 128:(t + 1) * 128], qtsb[0:96, :, :])], r=[K("qtsb")], w=["QT_s" + p_], eng="pool")

    w1b_s = L["w1b_s"] = dscr("w1b_s", [22, 128, 8, 256], BF16)
    w1v_ = L["w_f1_d"].rearrange("(k p) n -> p k n", p=128)
    sc_list = P.capture(lambda: s5_scalar(L))
    P.merge([P.capture(lambda: chain(0))])
    for bi in range(8):
        lists = [P.capture(lambda: proj(bi))]
        if bi + 1 < 8:
            lists.append(P.capture(lambda: chain(bi + 1)))
        if bi == 0:
            lists.append(sc_list)
        P.merge(lists)
        if bi < 4:
            prs = []
            for hc in range(bi * 6, min(22, bi * 6 + 6)):
                prs += [(w1b_s[hc, :, :, 0:128], w1v_[:, :, hc * 128:(hc + 1) * 128]),
                        (w1b_s[hc, :, :, 128:256], w1v_[:, :, FFH + hc * 128:FFH + (hc + 1) * 128])]
            ld(prs, w=["w1b_s_%d" % bi], eng="pool")
    for nm, src in (("KT", KT_s), ("QT", QT_s), ("V", V_s), ("uT", uT_s)):
        if nm in dbg_d:
            ld([(dbg_d[nm], src)], r=[nm + "_s"], w=["dbg_" + nm], eng="pool")
    P.flush()
    P.release()


def stage_gates(L):
    P, nc = L["P"], L["nc"]
    act, dve, pool, pe, ld = L["act"], L["dve"], L["pool"], L["pe"], L["ld"]
    pf, gate_s, hT_s = L["pf"], L["gate_s"], L["hT_s"]
    base_mark = P.sb_mark
    w_in_bf = L["w_in_bf0"]
    P.skip([128, 8, IN_COLS], BF16)
    hTg = [P.sb([128, 8, 512], BF16, "hTg%d" % i) for i in range(2)]
    obs = [P.sb([128, 512], BF16, "gob%d" % i) for i in range(4)]
    n = 0
    for bi in range(4):
        hT, hk = hTg[bi % 2], "hTg%d" % (bi % 2)
        ld([(hT, hT_s[bi])], w=[hk])
        bc = slice(bi * 512, (bi + 1) * 512)
        for j in range(16):
            c0 = 1184 + 128 * j
            psx, pk = pf[n % 6], "pf%d" % (n % 6)
            ob, ok = obs[n % 4], "gob%d" % (n % 4)
            n += 1

            def mm(e, c0=c0, psx=psx, hT=hT):
                ins = None
                for k in range(8):
                    ins = e.matmul(psx, lhsT=w_in_bf[:, k, c0:c0 + 128], rhs=hT[:, k, :], start=(k == 0), stop=(k == 7))
                return ins
            pe(mm, r=[hk, pk], w=[pk])
            act(lambda e, psx=psx, ob=ob: e.activation(out=ob, in_=psx, func=AF.Sigmoid), r=[pk], w=[ok])
            ld([(gate_s[j, :, bc], ob)], r=[ok], w=["gate_s"], eng="pool")
    P.flush()
    P.sb_off = P.sb_mark = base_mark


def s5_scalar(L):
    P, nc = L["P"], L["nc"]
    act, dve, pool, pe, ld = L["act"], L["dve"], L["pool"], L["pe"], L["ld"]

    def T(shape, dt=F32, name="s5s"):
        return P.sb(shape, dt, name)

    def bc3(v, m):
        return v.unsqueeze(2).to_broadcast([v.shape[0], v.shape[1], m])

    MU = T([64, 3, 2, 64])
    lam = T([64, 2, 64]); ldt = T([64, 64]); Bc = T([64, 2, 64, 16])
    ld([(lam, L["lam_d"]), (ldt, L["ldt_d"]), (Bc, L["sb_d"])], w=["lam", "ldt", "Bc"])
    sm = T([64, 24, 64])
    K = "sm"

    def d2(fn, r=(), w=()):
        dve(fn, r=list(r) + [K], w=list(w) + [K])
    lre, lim, dt_, a_, th, mag = (sm[:, i, :] for i in range(6))
    d2(lambda e: e.tensor_scalar_min(out=lre, in0=lam[:, 0, :], scalar1=-1e-4), r=["lam"])
    d2(lambda e: e.tensor_copy(out=lim, in_=lam[:, 1, :]), r=["lam"])
    act(lambda e: e.activation(out=dt_, in_=ldt, func=AF.Exp), r=["ldt", K], w=[K])
    d2(lambda e: e.tensor_tensor(out=a_, in0=lre, in1=dt_, op=ALU.mult))
    d2(lambda e: e.tensor_tensor(out=th, in0=lim, in1=dt_, op=ALU.mult))
    act(lambda e: e.activation(out=mag, in_=a_, func=AF.Exp), r=[K], w=[K])
    tr_ = T([64, 2, 64]); ti_ = T([64, 2, 64], I32); tf_ = T([64, 2, 64])
    d2(lambda e: e.tensor_single_scalar(out=tr_[:, 1, :], in_=th, scalar=1.0 / (2 * math.pi), op=ALU.mult), w=["tr_"])
    dve(lambda e: e.tensor_scalar_add(out=tr_[:, 0, :], in0=tr_[:, 1, :], scalar1=0.25), r=["tr_"], w=["tr_"])
    dve(lambda e: e.tensor_copy(out=ti_, in_=tr_), r=["tr_"], w=["ti_"])
    dve(lambda e: e.tensor_copy(out=tf_, in_=ti_), r=["ti_"], w=["tf_"])
    dve(lambda e: e.tensor_tensor(out=tr_, in0=tr_, in1=tf_, op=ALU.subtract), r=["tr_", "tf_"], w=["tr_"])
    dve(lambda e: e.tensor_single_scalar(out=tf_, in_=tr_, scalar=0.5, op=ALU.is_gt), r=["tr_", "tf_"], w=["tf_"])
    dve(lambda e: e.tensor_tensor(out=tr_, in0=tr_, in1=tf_, op=ALU.subtract), r=["tr_", "tf_"], w=["tr_"])
    dve(lambda e: e.tensor_single_scalar(out=tf_, in_=tr_, scalar=-0.5, op=ALU.is_lt), r=["tr_", "tf_"], w=["tf_"])
    dve(lambda e: e.tensor_tensor(out=tr_, in0=tr_, in1=tf_, op=ALU.add), r=["tr_", "tf_"], w=["tr_"])
    cs = T([64, 2, 64])
    act(lambda e: e.activation(out=cs, in_=tr_, func=AF.Sin, scale=2.0 * math.pi), r=["tr_"], w=["cs"])
    PW = T([64, 2, 16, 64])
    KP = "PW"

    def pw(k):
        return PW[:, 0, k + 7, :], PW[:, 1, k + 7, :]
    ab_re, ab_im = pw(1)
    dve(lambda e: e.tensor_tensor(out=ab_re, in0=mag, in1=cs[:, 0, :], op=ALU.mult), r=[K, "cs", KP], w=[KP])
    dve(lambda e: e.tensor_tensor(out=ab_im, in0=mag, in1=cs[:, 1, :], op=ALU.mult), r=[K, "cs", KP], w=[KP])
    one_re, one_im = pw(0)
    dve(lambda e: e.memset(one_re, 1.0), r=[KP], w=[KP]); dve(lambda e: e.memset(one_im, 0.0), r=[KP], w=[KP])
    s0, s1, s2, s3 = (sm[:, i, :] for i in range(6, 10))

    def cmul(o_re, o_im, x_re, x_im, y_re, y_im, keys):
        kk = list(keys) + [K]
        dve(lambda e: e.tensor_tensor(out=s0, in0=x_re, in1=y_re, op=ALU.mult), r=kk, w=[K])
        dve(lambda e: e.tensor_tensor(out=s1, in0=x_im, in1=y_im, op=ALU.mult), r=kk, w=[K])
        dve(lambda e: e.tensor_tensor(out=s2, in0=x_re, in1=y_im, op=ALU.mult), r=kk, w=[K])
        dve(lambda e: e.tensor_tensor(out=s3, in0=x_im, in1=y_re, op=ALU.mult), r=kk, w=[K])
        dve(lambda e: e.tensor_tensor(out=o_re, in0=s0, in1=s1, op=ALU.subtract), r=kk, w=kk)
        dve(lambda e: e.tensor_tensor(out=o_im, in0=s2, in1=s3, op=ALU.add), r=kk, w=kk)
    inv_re, inv_im = pw(-1)
    m2, rm2 = sm[:, 10, :], sm[:, 11, :]
    d2(lambda e: e.tensor_tensor(out=s0, in0=ab_re, in1=ab_re, op=ALU.mult), r=[KP])
    d2(lambda e: e.tensor_tensor(out=s1, in0=ab_im, in1=ab_im, op=ALU.mult), r=[KP])
    d2(lambda e: e.tensor_tensor(out=m2, in0=s0, in1=s1, op=ALU.add))
    d2(lambda e: e.reciprocal(out=rm2, in_=m2))
    dve(lambda e: e.tensor_tensor(out=inv_re, in0=ab_re, in1=rm2, op=ALU.mult), r=[K, KP], w=[KP])
    dve(lambda e: e.scalar_tensor_tensor(out=inv_im, in0=ab_im, scalar=-1.0, in1=rm2, op0=ALU.mult, op1=ALU.mult), r=[K, KP], w=[KP])
    for k in range(1, 8):
        cmul(*pw(k + 1), *pw(k), ab_re, ab_im, [KP])
    for k in range(1, 7):
        cmul(*pw(-k - 1), *pw(-k), inv_re, inv_im, [KP])
    dve(lambda e: e.tensor_copy(out=MU[:, 0, 0, :], in_=pw(8)[0]), r=[KP], w=["MU"])
    dve(lambda e: e.tensor_copy(out=MU[:, 0, 1, :], in_=pw(8)[1]), r=[KP, "MU"], w=["MU"])
    sq_ = T([64, 2, 2, 64])
    for lv in (1, 2):
        src = (MU[:, lv - 1, 0, :], MU[:, lv - 1, 1, :])
        for it in range(3):
            dst = (MU[:, lv, 0, :], MU[:, lv, 1, :]) if it == 2 else (sq_[:, it, 0, :], sq_[:, it, 1, :])
            cmul(dst[0], dst[1], src[0], src[1], src[0], src[1], ["MU", "sq_"])
            src = dst
    nr, den, rden, f_re, f_im = (sm[:, i, :] for i in range(12, 17))
    d2(lambda e: e.tensor_scalar_add(out=nr, in0=ab_re, scalar1=-1.0), r=[KP])
    d2(lambda e: e.tensor_tensor(out=s0, in0=lre, in1=lre, op=ALU.mult))
    d2(lambda e: e.tensor_tensor(out=s1, in0=lim, in1=lim, op=ALU.mult))
    d2(lambda e: e.tensor_tensor(out=den, in0=s0, in1=s1, op=ALU.add))
    d2(lambda e: e.reciprocal(out=rden, in_=den))
    d2(lambda e: e.tensor_tensor(out=s0, in0=nr, in1=lre, op=ALU.mult))
    d2(lambda e: e.tensor_tensor(out=s1, in0=ab_im, in1=lim, op=ALU.mult), r=[KP])
    d2(lambda e: e.tensor_tensor(out=s0, in0=s0, in1=s1, op=ALU.add))
    d2(lambda e: e.tensor_tensor(out=f_re, in0=s0, in1=rden, op=ALU.mult))
    d2(lambda e: e.tensor_tensor(out=s0, in0=ab_im, in1=lre, op=ALU.mult), r=[KP])
    d2(lambda e: e.tensor_tensor(out=s1, in0=nr, in1=lim, op=ALU.mult))
    d2(lambda e: e.tensor_tensor(out=s0, in0=s0, in1=s1, op=ALU.subtract))
    d2(lambda e: e.tensor_tensor(out=f_im, in0=s0, in1=rden, op=ALU.mult))
    Bb = T([64, 2, 64, 16]); t16 = T([64, 2, 64, 16])

    def cmul16(o_re, o_im, s_re, s_im, x_re, x_im, n, keys_r, keys_w, ta, tb, eng=None, tk="t16"):
        eng = eng or dve
        sr, si = bc3(s_re, 16), bc3(s_im, 16)
        kr = list(keys_r) + list(keys_w) + [tk]
        eng(lambda e: e.tensor_tensor(out=ta, in0=x_re, in1=sr, op=ALU.mult), r=kr, w=[tk])
        eng(lambda e: e.tensor_tensor(out=tb, in0=x_im, in1=si, op=ALU.mult), r=kr, w=[tk])
        eng(lambda e: e.tensor_tensor(out=o_re, in0=ta, in1=tb, op=ALU.subtract), r=kr, w=list(keys_w) + [tk])
        eng(lambda e: e.tensor_tensor(out=ta, in0=x_im, in1=sr, op=ALU.mult), r=kr, w=[tk])
        eng(lambda e: e.tensor_tensor(out=tb, in0=x_re, in1=si, op=ALU.mult), r=kr, w=[tk])
        eng(lambda e: e.tensor_tensor(out=o_im, in0=ta, in1=tb, op=ALU.add), r=kr, w=list(keys_w) + [tk])
    cmul16(Bb[:, 0], Bb[:, 1], f_re, f_im, Bc[:, 0], Bc[:, 1], 64, [K, "Bc"], ["Bb"], t16[:, 0], t16[:, 1])
    PW_s = L["PW_s"] = nc.dram_tensor("PW_s", [64, 2, 16, 64], F32, kind="Internal").ap()
    Bb_s = L["Bb_s"] = nc.dram_tensor("Bb_s", [64, 2, 64, 16], F32, kind="Internal").ap()
    MU_s = L["MU_s"] = nc.dram_tensor("MU_s", [64, 3, 2, 64], F32, kind="Internal").ap()
    ld([(PW_s, PW), (Bb_s, Bb), (MU_s, MU)], r=[KP, "Bb", "MU"], w=["PW_s", "Bb_s", "MU_s"], eng="pool")


def stage_s5(L):
    P, nc = L["P"], L["nc"]
    act, dve, pool, pe, ld = L["act"], L["dve"], L["pool"], L["pe"], L["ld"]
    pf, pb, identf, iota_f = L["pf"], L["pb"], L["identf"], L["iota_f_p"]
    uT_s, brb_s, dbg_d = L["uT_s"], L["brb_s"], L["dbg_d"]
    base_mark = P.sb_mark

    def T(shape, dt=F32, name="s5"):
        return P.sb(shape, dt, name)

    def bc3(v, m):
        return v.unsqueeze(2).to_broadcast([v.shape[0], v.shape[1], m])

    G = T([128, 8, 8, 128], BF16); MU = T([64, 3, 2, 64]); dsk = T([128, 4])
    MUS = T([128, 3, 2, 2, 4, 4])
    P.mark()
    Wq = [T([128, 32, 128], BF16)]
    T0q = [T([128, 32, 128], BF16)]
    Vbq = [T([128, 2, 16, 128], BF16)]
    W_s = nc.dram_tensor("W_s", [128, 64, 128], BF16, kind="Internal").ap()
    T0_s = nc.dram_tensor("T0_s", [128, 64, 128], BF16, kind="Internal").ap()
    Vb_s = nc.dram_tensor("Vb_s", [64, 2, 64, 128], BF16, kind="Internal").ap()
    GY_s = L["GY_s"] = nc.dram_tensor("GY_s", [4, 128, OWN], BF16, kind="Internal").ap()
    KP = "PW"
    PW = T([64, 2, 16, 64]); Bb = T([64, 2, 64, 16])
    ld([(PW, L["PW_s"]), (Bb, L["Bb_s"]), (MU, L["MU_s"])], w=[KP, "Bb", "MU"])

    def cmul16(o_re, o_im, s_re, s_im, x_re, x_im, n, keys_r, keys_w, ta, tb, eng=None, tk="t16"):
        eng = eng or dve
        sr, si = bc3(s_re, 16), bc3(s_im, 16)
        kr = list(keys_r) + list(keys_w) + [tk]
        eng(lambda e: e.tensor_tensor(out=ta, in0=x_re, in1=sr, op=ALU.mult), r=kr, w=[tk])
        eng(lambda e: e.tensor_tensor(out=tb, in0=x_im, in1=si, op=ALU.mult), r=kr, w=[tk])
        eng(lambda e: e.tensor_tensor(out=o_re, in0=ta, in1=tb, op=ALU.subtract), r=kr, w=list(keys_w) + [tk])
        eng(lambda e: e.tensor_tensor(out=ta, in0=x_im, in1=sr, op=ALU.mult), r=kr, w=[tk])
        eng(lambda e: e.tensor_tensor(out=tb, in0=x_re, in1=si, op=ALU.mult), r=kr, w=[tk])
        eng(lambda e: e.tensor_tensor(out=o_im, in0=ta, in1=tb, op=ALU.add), r=kr, w=list(keys_w) + [tk])
    expsE = ([7 - s_ for s_ in range(8)], [s_ for s_ in range(8)])
    expsF = ([i - 7 for i in range(8)], [-i for i in range(8)])
    expsV = ([i + 1 for i in range(8)], [8 - i for i in range(8)])
    Es = [T([128, 2, 16, 8, 16]) for _ in range(2)]
    Fs = [T([128, 2, 16, 8, 16]) for _ in range(2)]
    tS = T([128, 8, 2, 16, 16])
    tSv = T([128, 8, 2, 16, 16])
    PT = {nm: T([128, 2, 8, 32]) for nm in ("E", "F", "V")}
    Bb2 = T([128, 2, 32, 16]); Cc2 = T([128, 2, 32, 16])
    dmas = []
    for nm, exps in (("E", expsE), ("F", expsF), ("V", expsV)):
        for s_ in range(8):
            kA, kB = exps[0][s_] + 7, exps[1][s_] + 7
            dve(lambda e, nm=nm, s_=s_, kA=kA: e.tensor_copy(out=PT[nm][0:64, :, s_, :], in_=PW[:, :, kA, 0:32]), r=[KP, "PT"], w=["PT"])
            dmas.append((PT[nm][64:128, :, s_, :], PW[:, :, kB, 32:64]))
    ld(dmas, r=[KP], w=["PTb"])
    dve(lambda e: e.tensor_copy(out=Bb2[0:64], in_=Bb[:, :, 0:32, :]), r=["Bb"], w=["Bb2"])
    ld([(Bb2[64:128], Bb[:, :, 32:64, :])], r=["Bb"], w=["Bb2b"])
    ld([(Cc2[0:64], L["sc_d"][:, :, 0:32, :]), (Cc2[64:128], L["sc_d"][:, :, 32:64, :])], w=["Cc2"])

    def gen(dst, src, nm, keyd, keys, negim, b, scr=None, scrk="tS"):
        scr = tS if scr is None else scr
        gs = slice(b * 16, (b + 1) * 16)
        tab = PT[nm]
        for s_ in range(8):
            cmul16(dst[:, 0, :, s_, :], dst[:, 1, :, s_, :], tab[:, 0, s_, gs], tab[:, 1, s_, gs], src[:, 0, gs], src[:, 1, gs], 16,
                   ["PT", "PTb"] + list(keys), ["%s%d" % (keyd, s_)], scr[:, s_, 0], scr[:, s_, 1], tk="%s%d" % (scrk, s_))
            if negim:
                dve(lambda e, s_=s_: e.tensor_single_scalar(out=dst[:, 1, :, s_, :], in_=dst[:, 1, :, s_, :], scalar=-1.0, op=ALU.mult),
                    r=["%s%d" % (keyd, s_)], w=["%s%d" % (keyd, s_)])
    mi = T([128, 2, 128], I32); mf = T([128, 2, 128]); mask = T([128, 2, 128])
    pool(lambda e: e.iota(mi[:, 0, :], pattern=[[1, 128]], base=0, channel_multiplier=0), w=["mi"])
    pool(lambda e: e.iota(mi[:, 1, :], pattern=[[0, 128]], base=0, channel_multiplier=1), r=["mi"], w=["mi"])
    dve(lambda e: e.tensor_single_scalar(out=mi, in_=mi, scalar=4, op=ALU.arith_shift_right), r=["mi"], w=["mi"])
    dve(lambda e: e.tensor_copy(out=mf, in_=mi), r=["mi"], w=["mf"])
    dve(lambda e: e.tensor_tensor(out=mask[:, 0, :], in0=mf[:, 0, :], in1=mf[:, 1, :], op=ALU.is_ge), r=["mf"], w=["mask"])
    dve(lambda e: e.tensor_tensor(out=mask[:, 1, :], in0=mf[:, 1, :], in1=mf[:, 0, :], op=ALU.is_ge), r=["mf", "mask"], w=["mask"])
    rowm = T([128, 8])
    for a in range(8):
        dve(lambda e, a=a: e.tensor_single_scalar(out=rowm[:, a:a + 1], in_=mf[:, 1, 0:1], scalar=float(a), op=ALU.is_equal), r=["mf", "rowm"], w=["rowm"])
    for a in range(8):
        for b in range(8):
            dve(lambda e, a=a, b=b: e.tensor_scalar(out=G[:, a, b, :], in0=iota_f, scalar1=float(16 * (b - a)), scalar2=rowm[:, a:a + 1],
                                                   op0=ALU.is_equal, op1=ALU.mult), r=["iota_f", "rowm", "G"], w=["G"])
    def genA(b):
        gen(Es[b % 2], Bb2, "E", "E%d_" % (b % 2), ["Bb2", "Bb2b"], False, b)
        gen(Fs[b % 2], Cc2, "F", "F%d_" % (b % 2), ["Cc2"], True, b)

    def secB(b):
        par = b % 2
        E, F = Es[par], Fs[par]
        Wt, T0t, Vbt = Wq[0], T0q[0], Vbq[0]
        ek = ["E%d_%d" % (par, i_) for i_ in range(8)]
        fk = ["F%d_%d" % (par, i_) for i_ in range(8)]
        n_ = 0
        for d_ in range(2):
            pr_ = slice(64 * d_, 64 * d_ + 64)
            idn = identf[pr_, pr_]
            for gl in range(16):
                slot = d_ * 16 + gl
                psw = pf[n_ % 2]; pk = "pf%d" % (n_ % 2)
                pst = pf[2 + n_ % 2]; pk2 = "pf%d" % (2 + n_ % 2)
                n_ += 1

                def trw(e, gl=gl, psw=psw, pr_=pr_, idn=idn):
                    e.transpose(psw[:, 0:64], E[pr_, 0, gl].rearrange("p s h -> p (s h)"), idn)
                    return e.transpose(psw[:, 64:128], E[pr_, 1, gl].rearrange("p s h -> p (s h)"), idn)
                pe(trw, r=ek + [pk], w=[pk])
                act(lambda e, slot=slot, psw=psw: e.copy(out=Wt[:, slot, :], in_=psw[:, 0:128]), r=[pk, "Wq0"], w=["Wq0"])

                def mt0(e, gl=gl, pst=pst, pr_=pr_):
                    e.matmul(pst[:, 0:128], lhsT=E[pr_, 0, gl].rearrange("p s h -> p (s h)"), rhs=F[pr_, 0, gl].rearrange("p s h -> p (s h)"), start=True, stop=False)
                    return e.matmul(pst[:, 0:128], lhsT=E[pr_, 1, gl].rearrange("p s h -> p (s h)"), rhs=F[pr_, 1, gl].rearrange("p s h -> p (s h)"), start=False, stop=True)
                pe(mt0, r=ek + fk + [pk2], w=[pk2])
                dve(lambda e, slot=slot, d_=d_, pst=pst: e.tensor_tensor(out=T0t[:, slot, :], in0=pst[:, 0:128], in1=mask[:, d_, :], op=ALU.mult),
                    r=[pk2, "mask", "T0q0"], w=["T0q0"])
        gen(F, Cc2, "V", "F%d_" % par, ["Cc2"], True, b, tSv, "tSv")
        dve(lambda e: e.tensor_copy(out=Vbt, in_=F.rearrange("p r g i h -> p r g (i h)")), r=fk + ["Vbq0"], w=["Vbq0"])
        for d_ in range(2):
            qs = slice(d_ * 32 + 16 * b, d_ * 32 + 16 * b + 16)
            ld([(W_s[:, qs, :], Wt[:, d_ * 16:(d_ + 1) * 16, :])], r=["Wq0"], w=["W_s"], eng="pool")
            ld([(T0_s[:, qs, :], T0t[:, d_ * 16:(d_ + 1) * 16, :])], r=["T0q0"], w=["T0_s"], eng="pool")
            ld([(Vb_s[:, :, qs, :], Vbt[64 * d_:64 * d_ + 64])], r=["Vbq0"], w=["Vb_s"], eng="pool")

    genA(0)
    for b in range(2):
        lists = [P.capture(lambda: secB(b))]
        if b + 1 < 2:
            lists.append(P.capture(lambda: genA(b + 1)))
        P.merge(lists)
    ld([(dsk, L["sd_d"])], w=["dsk"])
    for hb__ in range(2):
        prs = []
        for lv in range(3):
            for ri in range(2):
                for d_ in range(2):
                    src = MU[:, lv, ri, d_ * 32:(d_ + 1) * 32].rearrange("p (j h q) -> p j h q", j=4, h=2, q=4)[:, :, hb__, :]
                    prs.append((MUS[64 * hb__:64 * hb__ + 64, lv, ri, d_, :, :], src))
        ld(prs, r=["MU"], w=["MUS%d" % hb__])
    P.flush()
    P.release()
    P.mark()

    NBT = 4
    uTj = T([128, SEQ], BF16)
    WJ = T([128, 16, 128], BF16); T0J = T([128, 16, 128], BF16); VJ = T([128, 2, 16, 128], BF16)
    Yb = T([128, 8, 256], BF16); ysb = T([128, OWN])
    y_s = nc.dram_tensor("y_s", [4, 128, OWN], F32, kind="Internal").ap()
    NP = 128

    class Scr:
        def __init__(self, tag):
            self.tag = tag
            self.zup = [T([NP, NBT, 2, 32]), T([NP, NBT, 2, 4])]
            self.pup = [T([NP, NBT, 2, 32]), T([NP, NBT, 2, 4])]
            self.accs = [T([NP, NBT, 2, 32]), T([NP, NBT, 2, 32])]
            self.tsc = T([NP, 4, NBT, 32])
            self.k = "scr" + tag
    class ZSet:
        def __init__(self, tag):
            self.t = tag
            self.Us = [T([128, NBT, 512], BF16), T([128, NBT, 512], BF16)]
            self.ZA = T([NP, NBT, 2, 256]); self.ZB = T([NP, NBT, 2, 256]); self.ZO = T([NP, NBT, 2, 256])
            self.NB = self.ZO
            self.PAb = self.ZA.bitcast(BF16)[:, :, :, 0:256]
            self.NBb = self.ZB.bitcast(BF16)[:, :, :, 0:256]
    zsets = [ZSet("e"), ZSet("o")]
    PA = T([NP, NBT, 2, 256])
    carry = T([NP, NBT, 2, 1])
    scrA, scrB = Scr("A"), Scr("B")

    def madd(sc, new, acc, z, mu_re, mu_im, m, keys):
        n_re, n_im = new; a_re, a_im = acc
        mr, mi_ = bc3(mu_re, m), bc3(mu_im, m)
        t1, t2, t3, t4 = (sc.tsc[:, i, :, 0:m] for i in range(4))
        kd, kp = sc.k + "d", sc.k + "p"
        kk = list(keys) + ["MUS0", "MUS1"]
        dve(lambda e: e.tensor_tensor(out=t1, in0=a_re, in1=mr, op=ALU.mult), r=kk + [kd], w=[kd])
        dve(lambda e: e.tensor_tensor(out=t2, in0=a_im, in1=mi_, op=ALU.mult), r=kk + [kd], w=[kd])
        dve(lambda e: e.tensor_tensor(out=t1, in0=t1, in1=t2, op=ALU.subtract), r=[kd], w=[kd])
        pool(lambda e: e.tensor_tensor(out=t3, in0=a_im, in1=mr, op=ALU.mult), r=kk + [kp], w=[kp])
        pool(lambda e: e.tensor_tensor(out=t4, in0=a_re, in1=mi_, op=ALU.mult), r=kk + [kp], w=[kp])
        pool(lambda e: e.tensor_tensor(out=t3, in0=t3, in1=t4, op=ALU.add), r=[kp], w=[kp])
        dve(lambda e: e.tensor_tensor(out=n_re, in0=t1, in1=z[0], op=ALU.add), r=[kd] + kk, w=kk[:-2])
        dve(lambda e: e.tensor_tensor(out=n_im, in0=t3, in1=z[1], op=ALU.add), r=[kp] + kk, w=kk[:-2])

    def vw(Zt, sl):
        return Zt[:, :, 0, sl], Zt[:, :, 1, sl]

    def blk(Zt, k, n):
        return (Zt[:, :, 0, 0:n].rearrange("p j (m k) -> p j k m", k=8)[:, :, k, :],
                Zt[:, :, 1, 0:n].rearrange("p j (m k) -> p j k m", k=8)[:, :, k, :])

    def mu_of(lv, dgs):
        d_, J_ = dgs
        return MUS[:, lv, 0, d_, J_, :], MUS[:, lv, 1, d_, J_, :]

    def horner(sc, Zt, n, lv, dgs, desc, out, keys):
        nb = n // 8
        order = list(range(7, -1, -1)) if desc else list(range(8))
        mr, mi_ = mu_of(lv, dgs)
        cur = blk(Zt, order[0], n)
        for ii, k in enumerate(order[1:]):
            dst = vw(out, slice(0, nb)) if ii == 6 else vw(sc.accs[ii % 2], slice(0, nb))
            madd(sc, dst, cur, blk(Zt, k, n), mr, mi_, nb, keys)
            cur = dst

    def seq_top(sc, Zt, n, lv, dgs, desc, init, out, keys, ikey=None):
        mr, mi_ = mu_of(lv, dgs)
        idx = list(range(n - 1, -1, -1)) if desc else list(range(n))
        if init is None:
            dve(lambda e: e.memset(out[:, :, :, idx[0]:idx[0] + 1], 0.0), r=keys, w=keys)
        else:
            dve(lambda e: e.tensor_copy(out=out[:, :, :, idx[0]:idx[0] + 1], in_=init), r=keys + [ikey], w=keys)
        for a, b in zip(idx[:-1], idx[1:]):
            madd(sc, vw(out, slice(b, b + 1)), vw(out, slice(a, a + 1)), vw(Zt, slice(a, a + 1)), mr, mi_, 1, keys)

    def exscan(sc, Zt, n, lv, dgs, desc, init, out, keys, ikey=None):
        if n <= 4:
            seq_top(sc, Zt, n, lv, dgs, desc, init, out, keys, ikey)
            return
        nb = n // 8
        zu, pu = sc.zup[lv], sc.pup[lv]
        horner(sc, Zt, n, lv, dgs, desc, zu, keys)
        exscan(sc, zu, nb, lv + 1, dgs, desc, init, pu, keys, ikey)
        order = list(range(7, -1, -1)) if desc else list(range(8))
        mr, mi_ = mu_of(lv, dgs)
        o0 = blk(out, order[0], n)
        dve(lambda e: e.tensor_copy(out=o0[0], in_=pu[:, :, 0, 0:nb]), r=keys, w=keys)
        pool(lambda e: e.tensor_copy(out=o0[1], in_=pu[:, :, 1, 0:nb]), r=keys, w=keys)
        for a, b in zip(order[:-1], order[1:]):
            madd(sc, blk(out, b, n), blk(out, a, n), blk(Zt, a, n), mr, mi_, nb, keys)

    GEL = 1.5957691216057308
    psA, pkA = pf[3], "pf3"
    psB, pkB = pf[7], "pf7"

    def pre(J):
        zs = zsets[J % 2]
        tg = zs.t
        ld([(uTj.rearrange("p (s c) -> p s c", s=8), L["uT2_s"][J * 128:(J + 1) * 128, :, :])], w=["uTj"])
        ld([(WJ[:, 0:8, :], W_s[:, 8 * J:8 * J + 8, :]), (WJ[:, 8:16, :], W_s[:, 32 + 8 * J:40 + 8 * J, :])], w=["WJ"])
        uview = uTj.rearrange("p (s c) -> p s c", s=8)
        for hb_ in range(2):
            U = zs.Us[hb_]
            ph = slice(64 * hb_, 64 * hb_ + 64)
            for jj in range(NBT):
                j = hb_ * NBT + jj
                uk = "U%s%d_%d" % (tg, hb_, jj)

                def msel(e, j=j):
                    ins = None
                    for s_ in range(8):
                        ins = e.matmul(psA, lhsT=G[:, j, s_, :], rhs=uview[:, s_, :], start=(s_ == 0), stop=(s_ == 7))
                    return ins
                pe(msel, r=["G", "uTj", pkA], w=[pkA])
                dve(lambda e, jj=jj, U=U: e.tensor_copy(out=U[:, jj, :], in_=psA), r=[pkA], w=[uk])
                for (Zt, zk, dl, c0) in ((zs.ZA, "ZA" + tg, j, 0), (zs.ZB, "ZB" + tg, 8 + j, 0), (zs.ZO, "ZO" + tg, 8 + j, 256)):
                    pkh = pkB

                    def mz(e, jj=jj, dl=dl, c0=c0, U=U, ph=ph):
                        e.matmul(psB[ph, 0:256], lhsT=WJ[:, dl, 0:64], rhs=U[:, jj, c0:c0 + 256], start=True, stop=True)
                        return e.matmul(psB[ph, 256:512], lhsT=WJ[:, dl, 64:128], rhs=U[:, jj, c0:c0 + 256], start=True, stop=True)
                    pe(mz, r=["WJ", uk, pkB], w=[pkB])
                    dve(lambda e, Zt=Zt, jj=jj, ph=ph: e.tensor_copy(out=Zt[ph, jj, :, :], in_=psB[ph, :].rearrange("p (r c) -> p r c", r=2)),
                        r=[pkh, zk], w=[zk])

    def scans(J):
        zs = zsets[J % 2]
        tg = zs.t
        dA = (0, J); dB = (1, J)

        def chainA():
            exscan(scrA, zs.ZA, 256, 0, dA, False, None, PA, ["ZA" + tg, "PA"], None)
            dve(lambda e: e.tensor_copy(out=zs.PAb, in_=PA), r=["PA", "ZA" + tg], w=["PAb" + tg, "ZA" + tg])

        def chainB():
            sc = scrB
            horner(sc, zs.ZO, 256, 0, dB, True, sc.zup[0], ["ZO" + tg])
            horner(sc, sc.zup[0], 32, 1, dB, True, sc.zup[1], ["ZO" + tg])
            mr2, mi2 = mu_of(2, dB)
            cur = vw(sc.zup[1], slice(3, 4))
            for n_ in (2, 1, 0):
                dst = vw(carry, slice(0, 1)) if n_ == 0 else vw(sc.accs[n_ % 2], slice(0, 1))
                madd(sc, dst, cur, vw(sc.zup[1], slice(n_, n_ + 1)), mr2, mi2, 1, ["ZO" + tg, "carry"])
                cur = dst
            exscan(sc, zs.ZB, 256, 0, dB, True, carry, zs.NB, ["ZB" + tg, "ZO" + tg, "carry"], "carry")
            pool(lambda e: e.tensor_copy(out=zs.NBb, in_=zs.NB), r=["ZO" + tg, "ZB" + tg], w=["NBb" + tg, "ZB" + tg])
        return [P.capture(chainA), P.capture(chainB)]

    def post(J):
        zs = zsets[J % 2]
        tg = zs.t
        ld([(T0J[:, 0:8, :], T0_s[:, 8 * J:8 * J + 8, :]), (T0J[:, 8:16, :], T0_s[:, 32 + 8 * J:40 + 8 * J, :])], w=["T0J"])
        ld([(VJ[hh * 64:hh * 64 + 64, :, 0:8, :], Vb_s[:, :, 8 * J:8 * J + 8, :]) for hh in range(2)]
           + [(VJ[hh * 64:hh * 64 + 64, :, 8:16, :], Vb_s[:, :, 32 + 8 * J:40 + 8 * J, :]) for hh in range(2)], w=["VJ"])
        for hb_ in range(2):
            U = zs.Us[hb_]
            ph = slice(64 * hb_, 64 * hb_ + 64)
            for jj in range(NBT):
                j = hb_ * NBT + jj
                uk = "U%s%d_%d" % (tg, hb_, jj)

                def my(e, j=j, jj=jj, U=U, ph=ph):
                    o = psA[:, 0:256]
                    e.matmul(o, lhsT=T0J[:, j, :], rhs=U[:, jj, 0:256], start=True, stop=False)
                    e.matmul(o, lhsT=T0J[:, 8 + j, :], rhs=U[:, jj, 0:256], start=False, stop=False)
                    e.matmul(o, lhsT=VJ[ph, 0, j, :], rhs=zs.PAb[ph, jj, 0, :], start=False, stop=False)
                    e.matmul(o, lhsT=VJ[ph, 1, j, :], rhs=zs.PAb[ph, jj, 1, :], start=False, stop=False)
                    e.matmul(o, lhsT=VJ[ph, 0, 8 + j, :], rhs=zs.NBb[ph, jj, 0, :], start=False, stop=False)
                    return e.matmul(o, lhsT=VJ[ph, 1, 8 + j, :], rhs=zs.NBb[ph, jj, 1, :], start=False, stop=True)
                pe(my, r=["T0J", "VJ", uk, "PAb" + tg, "NBb" + tg, "ZA" + tg, "ZB" + tg, pkA], w=[pkA])
                dve(lambda e, j=j: e.tensor_copy(out=Yb[:, j, :], in_=psA[:, 0:256]), r=[pkA, "Yb"], w=["Yb"])
        yv = ysb.rearrange("p (c i) -> p i c", i=8)
        for i in range(8):
            psd = psB[:, (i % 2) * 256:(i % 2) * 256 + 256]

            def md(e, i=i, psd=psd):
                ins = None
                for j in range(8):
                    ins = e.matmul(psd, lhsT=G[:, i, j, :], rhs=Yb[:, j, :], start=(j == 0), stop=(j == 7))
                return ins
            pe(md, r=["G", "Yb"], w=[pkB])
            dve(lambda e, i=i, psd=psd: e.tensor_copy(out=yv[:, i, :], in_=psd), r=[pkB, "ysb"], w=["ysb"])
        ld([(y_s[J], ysb)], r=["ysb"], w=["y_s%d" % J], eng="pool")
        if "ys5" in dbg_d:
            ld([(dbg_d["ys5"][J * 128:(J + 1) * 128, :], ysb)], r=["ysb"], w=["dbg_ys5"], eng="pool")

    def s5_main():
        pre(0)
        for J in range(4):
            lists = scans(J)

            def side(J=J):
                if J > 0:
                    post(J - 1)
                if J + 1 < 4:
                    pre(J + 1)
            lists.append(P.capture(side))
            P.merge(lists)
        post(3)

    def s5_tail():
      wg_bf = T([128, 4, 2 * D], BF16)
      ld([(wg_bf, L["w_glu_d"].rearrange("(k p) n -> p k n", p=128))], w=["wg_bf"], eng="pool")
      GY = T([128, 4, OWN], BF16)
      yb_ = [T([128, OWN])] * 2
      ub_ = [T([128, OWN], BF16), T([128, OWN], BF16)]
      g3 = T([128, OWN]); g4 = T([128, OWN])
      dsk2 = T([128, 4])
      ld([(dsk2, L["sd_d"])], w=["dsk2"])
      for J in range(4):
          yt, yk = yb_[J % 2], "ytl"
          ut, uk = ub_[J % 2], "utl%d" % (J % 2)
          ld([(yt, y_s[J])], r=["y_s%d" % J], w=[yk])
          ld([(ut, uT_s[J * 128:(J + 1) * 128, 0:OWN])], w=[uk])
          dve(lambda e, J=J, yt=yt, ut=ut: e.scalar_tensor_tensor(out=yt, in0=ut, scalar=dsk2[:, J:J + 1], in1=yt, op0=ALU.mult, op1=ALU.add),
              r=[yk, uk, "dsk2"], w=[yk])
          act(lambda e, yt=yt: e.activation(out=g3, in_=yt, func=AF.Square), r=[yk, "g3"], w=["g3"])
          dve(lambda e: e.tensor_scalar(out=g3, in0=g3, scalar1=0.044715, scalar2=1.0, op0=ALU.mult, op1=ALU.add), r=["g3"], w=["g3"])
          dve(lambda e, yt=yt: e.tensor_tensor(out=g3, in0=g3, in1=yt, op=ALU.mult), r=["g3", yk], w=["g3"])
          act(lambda e: e.activation(out=g4, in_=g3, func=AF.Sigmoid, scale=GEL), r=["g3", "g4"], w=["g4"])
          dve(lambda e, J=J, yt=yt: e.tensor_tensor(out=GY[:, J, :], in0=g4, in1=yt, op=ALU.mult), r=["g4", yk, "GY"], w=["GY"])
      sg = T([128, 512]); bo = [T([128, 512], BF16), T([128, 512], BF16)]
      n_ = 0
      for tb in range(4):
          tc_ = slice(tb * 512, (tb + 1) * 512)
          for dc in range(8):
              def mg(e, dc=dc, tc_=tc_):
                  ins = None
                  for (psx, c0) in ((pf[0], dc * 128), (pf[1], D + dc * 128)):
                      for k in range(4):
                          ins = e.matmul(psx, lhsT=wg_bf[:, k, c0:c0 + 128], rhs=GY[:, k, tc_], start=(k == 0), stop=(k == 3))
                  return ins
              pe(mg, r=["wg_bf", "GY", "pf0", "pf1"], w=["pf0", "pf1"])
              act(lambda e: e.activation(out=sg, in_=pf[1], func=AF.Sigmoid), r=["pf1"], w=["sg"])
              ob, ok = bo[n_ % 2], "bo%d" % (n_ % 2)
              n_ += 1
              dve(lambda e, ob=ob: e.tensor_tensor(out=ob, in0=pf[0], in1=sg, op=ALU.mult), r=["pf0", "sg"], w=[ok])
              ld([(brb_s[dc, :, tc_], ob)], r=[ok], w=["brb_s"], eng="pool")
      if "brb" in dbg_d:
          ld([(dbg_d["brb"], brb_s)], r=["brb_s"], w=["dbg_brb"], eng="pool")


    return s5_main, s5_tail, base_mark


def stage_attn(L):
    P, nc = L["P"], L["nc"]
    act, dve, pool, pe, ld = L["act"], L["dve"], L["pool"], L["pe"], L["ld"]
    pf, ones_f, dbg_d = L["pf"], L["ones_f"], L["dbg_d"]
    KT_s, V_s, QT_s = L["KT_s"], L["V_s"], L["QT_s"]
    bra_s = L["bra_s"] = nc.dram_tensor("bra_s", [8, 128, OWN], BF16, kind="Internal").ap()
    OT_s = nc.dram_tensor("OT_s", [64, 8, OWN], BF16, kind="Internal").ap()
    Vh = [P.sb([128, 32, 65], BF16, "Vh%d" % i) for i in range(2)]
    V_v = V_s.rearrange("(t p) (h d) -> p t h d", p=128, h=8)
    KTh = [P.sb([96, SEQ], BF16, "KTh%d" % i) for i in range(2)]
    QTh = [P.sb([96, OWN], BF16, "QTh%d" % i) for i in range(2)]
    PT3 = [P.sb([128, 512], BF16, "PT%d" % i) for i in range(3)]
    rcs = P.sb([128, 512], F32, "rcs")
    bcs = rcs[0:64, :]
    otb = [P.sb([64, 512], BF16, "otb%d" % i) for i in range(2)]
    sbank = [pf[0], pf[1], pf[6]]
    sbk = ["pf0", "pf1", "pf6"]
    steps = [(h, qb, kt) for h in range(NH) for qb in range(4) for kt in range(32)]
    LOOK = 2

    def emit_S(i):
        h, qb, kt = steps[i]
        kb, kk = KTh[h % 2], "KTh%d" % (h % 2)
        qb_, qk = QTh[h % 2], "QTh%d" % (h % 2)
        if qb == 0 and kt == 0:
            ld([(kb[:, 0:2048], KT_s[:, h, 0:2048]), (kb[:, 2048:], KT_s[:, h, 2048:])], w=[kk])
            ld([(qb_, QT_s[:, h, :])], w=[qk])
            ld([(Vh[h % 2][:, 0:16, :], V_v[:, 0:16, h, :]), (Vh[h % 2][:, 16:32, :], V_v[:, 16:32, h, :])], w=["Vh%d" % (h % 2)])
        qs = slice(qb * 512, (qb + 1) * 512)
        pss, pks = sbank[i % 3], sbk[i % 3]
        pe(lambda e: e.matmul(pss, lhsT=kb[:, kt * 128:(kt + 1) * 128], rhs=qb_[:, qs], start=True, stop=True), r=[kk, qk, pks], w=[pks])

    def emit_PV(i):
        h, qb, kt = steps[i]
        qs = slice(qb * 512, (qb + 1) * 512)
        pss, pks = sbank[i % 3], sbk[i % 3]
        pt, ptk = PT3[i % 3], "PT%d" % (i % 3)
        blk_i = i // 32
        acc, acck = (pf[2], "pf2") if blk_i % 2 == 0 else (pf[5], "pf5")
        act(lambda e: e.activation(out=pt, in_=pss, func=AF.Exp), r=[pks], w=[ptk])
        vb_, vk = Vh[h % 2], "Vh%d" % (h % 2)
        pe(lambda e: e.matmul(acc[0:65, :], lhsT=vb_[:, kt, :], rhs=pt, start=(kt == 0), stop=(kt == 31)),
           r=[vk, ptk, acck], w=[acck])
        if kt == 31:
            ob_, obk = otb[blk_i % 2], "otb%d" % (blk_i % 2)
            dve(lambda e: e.reciprocal(out=rcs[64:65, :], in_=acc[64:65, :]), r=[acck], w=["rcs"])

            def epilogue():
                pe(lambda e: e.matmul(pf[4][0:64, :], lhsT=ones_f[64:65, 0:64], rhs=rcs[64:65, :], start=True, stop=True), r=["rcs", "pf4"], w=["pf4"])
                dve(lambda e: e.tensor_copy(out=bcs, in_=pf[4][0:64, :]), r=["pf4"], w=["bcs"])
                dve(lambda e: e.tensor_tensor(out=ob_, in0=acc[0:64, :], in1=bcs, op=ALU.mult), r=[acck, "bcs", obk], w=[obk])
                ld([(OT_s[:, h, qs], ob_)], r=[obk], w=["OT_s"], eng="sp")
            pending.append((i + EPI_DELAY, epilogue))

    pending = []
    EPI_DELAY = 8

    def attn_core():
        for i in range(len(steps) + LOOK):
            if i < len(steps):
                emit_S(i)
            if i >= LOOK:
                emit_PV(i - LOOK)
            while pending and pending[0][0] <= i - LOOK:
                pending.pop(0)[1]()
        while pending:
            pending.pop(0)[1]()

    def attn_proj():
        OT = P.sb([64, 8, OWN], BF16, "OT")
        wo_bf = P.sb([64, 8, D], BF16, "wo_bf")
        ob = [P.sb([128, 512], BF16, "aob%d" % i) for i in range(2)]
        ld([(OT, OT_s)], r=["OT_s"], w=["OT"])
        ld([(wo_bf, L["w_o_d"].rearrange("(h p) n -> p h n", p=64))], w=["wo_bf"], eng="pool")
        n_ = 0
        for tb in range(4):
            ts_ = slice(tb * 512, (tb + 1) * 512)
            for j in range(8):
                psx, pk = pf[2 + n_ % 2], "pf%d" % (2 + n_ % 2)
                o_, okk = ob[n_ % 2], "aob%d" % (n_ % 2)
                n_ += 1

                def mo(e, j=j, ts_=ts_, psx=psx):
                    ins = None
                    for h in range(8):
                        ins = e.matmul(psx, lhsT=wo_bf[:, h, j * 128:(j + 1) * 128], rhs=OT[:, h, ts_], start=(h == 0), stop=(h == 7))
                    return ins
                pe(mo, r=["wo_bf", "OT", pk], w=[pk])
                act(lambda e, psx=psx, o_=o_: e.copy(out=o_, in_=psx), r=[pk], w=[okk])
                ld([(bra_s[j, :, ts_], o_)], r=[okk], w=["bra_s"], eng="pool")
        if "bra" in dbg_d:
            ld([(dbg_d["bra"], bra_s)], r=["bra_s"], w=["dbg_bra"], eng="pool")

    return attn_core, attn_proj


def stage_ffn(L):
    P, nc = L["P"], L["nc"]
    act, dve, pool, pe, ld = L["act"], L["dve"], L["pool"], L["pe"], L["ld"]
    pf, pb, ident, dbg_d = L["pf"], L["pb"], L["ident"], L["dbg_d"]
    gate_s, bra_s, brb_s, x_d, out_d = L["gate_s"], L["bra_s"], L["brb_s"], L["x_d"], L["out_d"]
    x1_s = nc.dram_tensor("x1_s", [OWN, D], F32, kind="Internal").ap()
    base_mark = P.sb_mark
    AB, wm_bf, w2_bf = L["ABf0"], L["wm_bf0"], L["w2_bf0"]
    P.skip([128, 4, D], F32); P.skip([128, 8, D], BF16); P.skip([128, 22, D], BF16)
    w1c = [P.sb([128, 8, 256], BF16, "w1c%d" % i) for i in range(3)]
    gin = [P.sb([128, 4, 512], BF16, "gin%d" % i) for i in range(2)]
    mt = P.sb([128, 2, 512], F32, "mt")
    mT = P.sb([128, 8, 512], BF16, "mT")
    mx = P.sb([128, D], F32, "mx"); sq = P.sb([128, D], F32, "fsq"); tmp = P.sb([128, D], F32, "ftmp")
    xt = P.sb([128, D], F32, "fxt"); x1t = P.sb([128, D], F32, "x1t"); hb = P.sb([128, D], BF16, "fhb")
    st = P.sb([128, 16], F32, "fst")
    mx2 = P.sb([128, D], F32, "mx2"); sq2 = P.sb([128, D], F32, "fsq2"); tmp2 = P.sb([128, D], F32, "ftmp2")
    xt2 = P.sb([128, D], F32, "fxt2"); st2 = P.sb([128, 16], F32, "fst2")
    h2Ts = [P.sb([128, 8, 512], BF16, "h2T%d" % i) for i in range(2)]
    aT = P.sb([128, 22, 512], BF16, "aT")
    sgls = [P.sb([128, 512], F32, "sgl%d" % i) for i in range(2)]
    w1v = L["w_f1_d"].rearrange("(k p) n -> p k n", p=128)
    nw = [0]

    def chainF(tb):
        ts_ = slice(tb * 512, (tb + 1) * 512)
        h2T, hk = h2Ts[tb % 2], "h2T%d" % (tb % 2)
        for j in range(8):
            gb_, gk = gin[j % 2], "gin%d" % (j % 2)
            ld([(gb_[:, 0, :], gate_s[j, :, ts_]), (gb_[:, 1, :], gate_s[8 + j, :, ts_]),
                (gb_[:, 2, :], bra_s[j, :, ts_]), (gb_[:, 3, :], brb_s[j, :, ts_])], r=["gate_s", "bra_s", "brb_s"], w=[gk])
            dve(lambda e, gb_=gb_: e.tensor_tensor(out=mt[:, 0, :], in0=gb_[:, 0, :], in1=gb_[:, 2, :], op=ALU.mult), r=[gk, "mt"], w=["mt"])
            dve(lambda e, gb_=gb_: e.tensor_tensor(out=mt[:, 1, :], in0=gb_[:, 1, :], in1=gb_[:, 3, :], op=ALU.mult), r=[gk, "mt"], w=["mt"])
            dve(lambda e, j=j: e.tensor_tensor(out=mT[:, j, :], in0=mt[:, 0, :], in1=mt[:, 1, :], op=ALU.add), r=["mt", "mT"], w=["mT"])
        for tt in range(4):
            t = tb * 4 + tt
            tk = slice(tt * 128, (tt + 1) * 128)
            ld([(xt, x_d[t * 128:(t + 1) * 128, :])], w=["fxt"])
            for hh in range(2):
                def mmx(e, tk=tk, hh=hh):
                    ins = None
                    for k in range(8):
                        ins = e.matmul(pf[7], lhsT=mT[:, k, tk], rhs=wm_bf[:, k, hh * 512:(hh + 1) * 512], start=(k == 0), stop=(k == 7))
                    return ins
                pe(mmx, r=["mT", "wm_bf", "pf7"], w=["pf7"])
                act(lambda e, hh=hh: e.copy(out=mx[:, hh * 512:(hh + 1) * 512], in_=pf[7]), r=["pf7", "mx"], w=["mx"])
            rms_ops(P, L, mx, D, st[:, 0:1], st[:, 1:2], st[:, 2:3], sq, "mx", "m", "fsq")
            dve(lambda e: e.scalar_tensor_tensor(out=tmp, in0=mx, scalar=st[:, 2:3], in1=AB[:, 0, :], op0=ALU.mult, op1=ALU.mult),
                r=["mx", "rstdm", "AB", "ftmp"], w=["ftmp"])
            dve(lambda e: e.tensor_tensor(out=x1t, in0=tmp, in1=xt, op=ALU.add), r=["ftmp", "fxt", "x1t"], w=["x1t"])
            ld([(x1_s[t * 128:(t + 1) * 128, :], x1t)], r=["x1t"], w=["x1_s%d" % t], eng="pool")
            rms_ops(P, L, x1t, D, st[:, 4:5], st[:, 5:6], st[:, 6:7], sq, "x1t", "h2", "fsq")
            dve(lambda e: e.scalar_tensor_tensor(out=tmp, in0=x1t, scalar=st[:, 6:7], in1=AB[:, 1, :], op0=ALU.mult, op1=ALU.mult),
                r=["x1t", "rstdh2", "AB", "ftmp"], w=["ftmp"])
            dve(lambda e: e.tensor_tensor(out=hb, in0=tmp, in1=AB[:, 2, :], op=ALU.add), r=["ftmp", "AB"], w=["fhb"])

            def tr(e):
                ins = None
                for k in range(8):
                    ins = e.transpose(pb[0][:, k * 128:(k + 1) * 128], hb[:, k * 128:(k + 1) * 128], ident)
                return ins
            pe(tr, r=["fhb", "ident", "pb0"], w=["pb0"])
            act(lambda e, tt=tt, h2T=h2T: e.copy(out=h2T[:, :, tt * 128:(tt + 1) * 128], in_=pb[0].rearrange("p (k t) -> p k t", k=8)),
                r=["pb0", hk], w=[hk])

    def ffnF(tb):
        h2T, hk = h2Ts[tb % 2], "h2T%d" % (tb % 2)
        for hc in range(22):
            wc, wk = w1c[nw[0] % 3], "w1c%d" % (nw[0] % 3)
            nw[0] += 1
            ld([(wc, L["w1b_s"][hc])], w=[wk])
            pa, pbk = (2, 3) if hc % 2 == 0 else (4, 5)
            sgl, sgk = sgls[hc % 2], "sgl%d" % (hc % 2)

            def mf(e, wc=wc, pa=pa, pbk=pbk):
                ins = None
                for (psx, c0) in ((pf[pa], 0), (pf[pbk], 128)):
                    for k in range(8):
                        ins = e.matmul(psx, lhsT=wc[:, k, c0:c0 + 128], rhs=h2T[:, k, :], start=(k == 0), stop=(k == 7))
                return ins
            pe(mf, r=[wk, hk, "pf%d" % pa, "pf%d" % pbk], w=["pf%d" % pa, "pf%d" % pbk])
            act(lambda e, pa=pa, sgl=sgl: e.activation(out=sgl, in_=pf[pa], func=AF.Silu), r=["pf%d" % pa, sgk], w=[sgk])
            dve(lambda e, hc=hc, pbk=pbk, sgl=sgl: e.tensor_tensor(out=aT[:, hc, :], in0=pf[pbk], in1=sgl, op=ALU.mult),
                r=["pf%d" % pbk, sgk], w=["aT%d" % hc])
        for tt in range(4):
            t = tb * 4 + tt
            tk = slice(tt * 128, (tt + 1) * 128)
            ld([(xt2, x1_s[t * 128:(t + 1) * 128, :])], r=["x1_s%d" % t], w=["fxt2"])

            def mo(e, tk=tk):
                ins = None
                for hh in range(2):
                    for k in range(22):
                        ins = e.matmul(pf[hh], lhsT=aT[:, k, tk], rhs=w2_bf[:, k, hh * 512:(hh + 1) * 512], start=(k == 0), stop=(k == 21))
                return ins
            pe(mo, r=["aT%d" % k for k in range(22)] + ["w2_bf", "pf0", "pf1"], w=["pf0", "pf1"])
            act(lambda e: e.copy(out=mx2[:, 0:512], in_=pf[0]), r=["pf0", "mx2"], w=["mx2"])
            act(lambda e: e.copy(out=mx2[:, 512:], in_=pf[1]), r=["pf1", "mx2"], w=["mx2"])
            rms_ops(P, L, mx2, D, st2[:, 8:9], st2[:, 9:10], st2[:, 10:11], sq2, "mx2", "f", "fsq2")
            dve(lambda e: e.scalar_tensor_tensor(out=tmp2, in0=mx2, scalar=st2[:, 10:11], in1=AB[:, 3, :], op0=ALU.mult, op1=ALU.mult),
                r=["mx2", "rstdf", "AB", "ftmp2"], w=["ftmp2"])
            dve(lambda e: e.tensor_tensor(out=tmp2, in0=tmp2, in1=xt2, op=ALU.add), r=["ftmp2", "fxt2"], w=["ftmp2"])
            ld([(out_d[t * 128:(t + 1) * 128, :], tmp2)], r=["ftmp2"], w=["out_d"], eng="pool")

    P.merge([P.capture(lambda: chainF(0))])
    for tb in range(4):
        lists = [P.capture(lambda: ffnF(tb))]
        if tb + 1 < 4:
            lists.append(P.capture(lambda: chainF(tb + 1)))
        P.merge(lists)
    P.flush()
    P.sb_off = P.sb_mark = base_mark


def prep_core(inp, core):
    b, hf = core // 2, core % 2
    rev = (hf == 1)
    f = lambda a: np.ascontiguousarray(a, dtype=np.float32)
    x = inp["x"][b]
    pos = inp["positions"][b]
    if rev:
        x = x[::-1]
        pos = pos[::-1]
    m = {}
    m["x_loc"] = f(x)
    m["pos_loc"] = np.ascontiguousarray(np.asarray(pos, dtype=np.int32).reshape(32, 128).T)
    m["c_col"] = f(inp["c"][b].reshape(8, 128).T)
    m["w_ada"] = f(inp["w_ada"][0])
    m["b_ada"] = f(inp["b_ada"][0].reshape(1, -1))
    m["gvec"] = f(np.concatenate([inp["g_pre_mix"][0], inp["g_post_mix"][0], inp["g_pre_ffn"][0], inp["g_post_ffn"][0]]).reshape(1, -1))
    m["w_in"] = f(inp["w_in"][0])
    m["g_q"] = f(inp["g_q_norm"][0].reshape(1, -1))
    m["g_kv"] = f(inp["g_kv_norm"][0].reshape(1, -1))
    m["w_uq"] = f(inp["w_uq"][0])
    m["w_uk"] = f(inp["w_uk"][0])
    m["w_uv"] = f(inp["w_uv"][0])
    m["w_o"] = f(inp["w_attn_out"][0])
    order = [1, 0] if rev else [0, 1]
    def dg(a):
        a = np.asarray(a)[order]
        return a.reshape((64,) + a.shape[2:])
    lre = dg(inp["ssm_lambda_re"][0]); lim = dg(inp["ssm_lambda_im"][0])
    m["lam"] = f(np.stack([lre.T, lim.T], axis=1))
    m["ldt"] = f(np.broadcast_to(dg(inp["ssm_log_dt"][0]).reshape(1, 64), (64, 64)))
    bre = dg(inp["ssm_b_re"][0]); bim = dg(inp["ssm_b_im"][0])
    m["ssm_b"] = f(np.stack([bre.transpose(1, 0, 2), bim.transpose(1, 0, 2)], axis=1))
    cre = dg(inp["ssm_c_re"][0]); cim = dg(inp["ssm_c_im"][0])
    m["ssm_c"] = f(np.stack([cre.transpose(2, 0, 1), cim.transpose(2, 0, 1)], axis=1))
    m["ssm_d"] = f(inp["ssm_d"][0].reshape(4, 128).T)
    m["w_glu"] = f(inp["w_glu"][0])
    m["w_mix"] = f(inp["w_mix_out"][0])
    m["w_f1"] = f(inp["w_ffn_in"][0])
    m["w_f2"] = f(inp["w_ffn_out"][0])
    return m


def kernel(**inputs):
    inputs = {k: np.asarray(v) for k, v in inputs.items()}
    nc = bass.Bass("TRN2", target_bir_lowering=False)
    build(nc)
    in_maps = [prep_core(inputs, c) for c in range(8)]
    res = run_bass_kernel_spmd(nc, in_maps, core_ids=list(range(8)))
    out = np.zeros((4, SEQ, D), np.float32)
    for c in range(8):
        b, hf = c // 2, c % 2
        o = np.asarray(res.results[c]["out_loc"], dtype=np.float32)
        if hf == 0:
            out[b, :OWN] = o
        else:
            out[b, OWN:] = o[::-1]
    return out
```

```python
import math
from contextlib import ExitStack
import numpy as np
import concourse.bass as bass
import concourse.mybir as mybir
from concourse.bass_utils import run_bass_kernel_spmd

F32 = mybir.dt.float32
BF16 = mybir.dt.bfloat16
I32 = mybir.dt.int32
AF = mybir.ActivationFunctionType
ALU = mybir.AluOpType
AX = mybir.AxisListType

D = 1024
SEQ = 4096
OWN = 2048
NH = 8
QLR = 384
KVR = 256
ROPE = 32
SSMC = 512
NG = 32
PST = 64
FFH = 2816
EPS = 1e-6
IN_COLS = 3232
DEBUG = {}


class Op:
    __slots__ = ("eng", "fn", "reads", "writes", "dma", "deps", "needed", "token", "grp")

    def __init__(self, eng, fn, reads, writes, dma, grp=None):
        self.eng, self.fn, self.reads, self.writes, self.dma = eng, fn, reads, writes, dma
        self.grp = grp
        self.deps = ()
        self.needed = False
        self.token = None


class Prog:
    ENGS = ("pe", "act", "dve", "pool", "sp")

    def __init__(self, nc, es):
        self.nc, self.es = nc, es
        self.ops = []
        self.psem = {e: es.enter_context(nc.semaphore("ps_" + e)) for e in self.ENGS}
        self.pcnt = {e: 0 for e in self.ENGS}
        self.dsem = {}
        self.dcnt = {}
        self.waited = {e: {} for e in self.ENGS}
        self.sb_off = 16448
        self.sb_mark = 16448
        self.nalloc = 0

    def sb(self, shape, dt, name=None):
        nbytes = int(np.prod(shape[1:])) * (4 if dt in (F32, I32) else 2)
        nbytes = (nbytes + 63) // 64 * 64
        off = self.sb_off
        self.sb_off += nbytes
        assert self.sb_off <= 229000, ("SBUF overflow", self.sb_off, name)
        self.nalloc += 1
        t = self.nc.alloc_sbuf_tensor_at("%s_%d" % (name or "t", self.nalloc), list(shape), dt, offset=off)
        return t.ap()

    def skip(self, shape, dt):
        nbytes = int(np.prod(shape[1:])) * (4 if dt in (F32, I32) else 2)
        self.sb_off += (nbytes + 63) // 64 * 64
        assert self.sb_off <= 229000, ("SBUF overflow", self.sb_off)

    def mark(self):
        self.sb_mark = self.sb_off

    def release(self):
        self.sb_off = self.sb_mark

    grp = None

    def op(self, eng, fn, r=(), w=()):
        self.ops.append(Op(eng, fn, tuple(r), tuple(w), False, self.grp))

    def dma(self, eng, fn, r=(), w=()):
        self.ops.append(Op(eng, fn, tuple(r), tuple(w), True, self.grp))

    def capture(self, fn):
        saved, self.ops = self.ops, []
        fn()
        out, self.ops = self.ops, saved
        return out

    def merge(self, lists):
        lists = [l for l in lists if l]
        pos = [0] * len(lists)
        total = sum(len(l) for l in lists)
        while sum(pos) < total:
            best, bi = None, -1
            for i, l in enumerate(lists):
                if pos[i] < len(l):
                    frac = pos[i] / len(l)
                    if best is None or frac < best:
                        best, bi = frac, i
            l = lists[bi]
            g = l[pos[bi]].grp
            self.ops.append(l[pos[bi]])
            pos[bi] += 1
            while g is not None and pos[bi] < len(l) and l[pos[bi]].grp == g:
                self.ops.append(l[pos[bi]])
                pos[bi] += 1

    def flush(self):
        ops, self.ops = self.ops, []
        lastw, readers = {}, {}
        for i, o in enumerate(ops):
            deps = set()
            for k in o.reads:
                if k in lastw:
                    deps.add(lastw[k])
            for k in o.writes:
                if k in lastw:
                    deps.add(lastw[k])
                deps |= readers.get(k, set())
            deps.discard(i)
            o.deps = sorted(deps)
            for d in deps:
                ops[d].needed = True
            for k in o.reads:
                readers.setdefault(k, set()).add(i)
            for k in o.writes:
                lastw[k] = i
                readers[k] = set()
        last = {}
        for i, o in enumerate(ops):
            if not o.dma:
                last[o.eng] = i
        for i in last.values():
            ops[i].needed = True
        used_d = set()
        for o in ops:
            if o.dma:
                key = o.writes[0]
                if key not in self.dsem:
                    self.dsem[key] = self.es.enter_context(self.nc.semaphore("ds%d" % len(self.dsem)))
                    self.dcnt[key] = 0
                o.token = [self.dsem[key], None, key]
                used_d.add(key)
            elif o.needed:
                self.pcnt[o.eng] += 1
                o.token = (self.psem[o.eng], self.pcnt[o.eng])
        for o in ops:
            if o.dma:
                self.dcnt[o.token[2]] += 16 * o.fn.ndma
                o.token = (o.token[0], self.dcnt[o.token[2]])
        finals = {e: self.pcnt[e] for e in self.ENGS}
        dfinals = [(self.dsem[k], self.dcnt[k]) for k in sorted(used_d)]
        nc = self.nc
        prog = self

        def emit(ename, e):
            wt = prog.waited[ename]

            def wait(sem, val):
                if wt.get(sem.num, 0) < val:
                    e.wait_ge(sem, val)
                    wt[sem.num] = val

            for o in ops:
                if o.eng != ename:
                    continue
                need = {}
                for d in o.deps:
                    sem, val = ops[d].token
                    if need.get(sem.num, (None, 0))[1] < val:
                        need[sem.num] = (sem, val)
                for sem, val in need.values():
                    wait(sem, val)
                res = o.fn(e)
                if o.dma:
                    for ins in res:
                        ins.then_inc(o.token[0], 16)
                elif o.needed:
                    res.then_inc(o.token[0], 1)
            for e2 in prog.ENGS:
                if e2 != ename and finals[e2] > 0:
                    wait(prog.psem[e2], finals[e2])
            for sem, val in dfinals:
                wait(sem, val)

        with nc.Block() as block:
            @block.tensor
            def _(e):
                emit("pe", e)

            @block.scalar
            def _(e):
                emit("act", e)

            @block.vector
            def _(e):
                emit("dve", e)

            @block.gpsimd
            def _(e):
                emit("pool", e)

            @block.sync
            def _(e):
                emit("sp", e)


class DmaFn:
    def __init__(self, pairs, cast=False):
        self.pairs = pairs
        self.ndma = len(pairs)

    def __call__(self, e):
        return [e.dma_start(out=o, in_=i) for (o, i) in self.pairs]


def build(nc, dbg=()):
    es = ExitStack()
    P = Prog(nc, es)

    def din(name, shape, dt=F32):
        return nc.dram_tensor(name, list(shape), dt, kind="ExternalInput").ap()

    def dscr0(name, shape, dt):
        return nc.dram_tensor(name, list(shape), dt, kind="Internal").ap()

    x_d = din("x_loc", [SEQ, D])
    pos_d = din("pos_loc", [128, 32], I32)
    c_d = din("c_col", [128, 8])
    w_ada_d = din("w_ada", [D, 6 * D])
    b_ada_d = din("b_ada", [1, 6 * D])
    gvec_d = din("gvec", [1, 4 * D])
    w_in_d = din("w_in", [D, IN_COLS])
    gq_d = din("g_q", [1, QLR])
    gkv_d = din("g_kv", [1, KVR])
    w_uq_d = din("w_uq", [QLR, 768])
    w_uk_d = din("w_uk", [KVR, 512])
    w_uv_d = din("w_uv", [KVR, 512])
    w_o_d = din("w_o", [512, D])
    lam_d = din("lam", [64, 2, 64])
    ldt_d = din("ldt", [64, 64])
    sb_d = din("ssm_b", [64, 2, 64, 16])
    sc_d = din("ssm_c", [64, 2, 64, 16])
    sd_d = din("ssm_d", [128, 4])
    w_glu_d = din("w_glu", [SSMC, 2 * D])
    w_mix_d = din("w_mix", [D, D])
    w_f1_d = din("w_f1", [D, 2 * FFH])
    w_f2_d = din("w_f2", [FFH, D])
    out_d = nc.dram_tensor("out_loc", [OWN, D], F32, kind="ExternalOutput").ap()
    dbg_d = {}
    for nm, shp in dbg:
        dbg_d[nm] = nc.dram_tensor("dbg_" + nm, list(shp), F32, kind="ExternalOutput").ap()

    def dscr(name, shape, dt):
        return nc.dram_tensor(name, list(shape), dt, kind="Internal").ap()

    gate_s = dscr("gate_s", [16, 128, OWN], BF16)
    brb_s = dscr("brb_s", [8, 128, OWN], BF16)

    ident = P.sb([128, 128], BF16, "ident")
    identf = P.sb([128, 128], F32, "identf")
    ones_f = P.sb([128, 128], F32, "ones_f")
    AB_s = dscr0("AB_s", [128, 6, D], F32)
    cs_tab = P.sb([128, 32, 2, 16], F32, "cs_tab")
    iota_f_keep = P.sb([128, 128], F32, "iota_f_keep")
    eps_t = P.sb([128, 1], F32, "eps_t")
    P.mark()
    pf = [nc.alloc_psum_tensor("pf%d" % i, [128, 512], F32).ap() for i in range(8)]
    pb = [pf[6].bitcast(BF16), pf[7].bitcast(BF16)]
    ps_ada = pf[0]

    def act(fn, r=(), w=()):
        P.op("act", fn, r, w)

    def dve(fn, r=(), w=()):
        P.op("dve", fn, r, w)

    def pool(fn, r=(), w=()):
        P.op("pool", fn, r, w)

    def pe(fn, r=(), w=()):
        P.op("pe", fn, r, w)

    def ld(pairs, r=(), w=(), eng="sp"):
        P.dma(eng, DmaFn(pairs), r, w)

    w_in_bf0 = P.sb([128, 8, IN_COLS], BF16, "w_in_bf")
    w_uq_bf0 = P.sb([128, 3, 768], BF16, "w_uq_bf")
    w_uk_bf0 = P.sb([128, 2, 512], BF16, "w_uk_bf")
    w_uv_bf0 = P.sb([128, 2, 512], BF16, "w_uv_bf")
    w_in_v0 = w_in_d.rearrange("(k p) n -> p k n", p=128)
    for k0 in range(0, 8, 2):
        ld([(w_in_bf0[:, k0:k0 + 2, :], w_in_v0[:, k0:k0 + 2, :])], w=["w_in_bf%d" % k0], eng="pool")
    ld([(w_uq_bf0, w_uq_d.rearrange("(k p) n -> p k n", p=128))], w=["w_uq_bf"], eng="pool")
    ld([(w_uk_bf0, w_uk_d.rearrange("(k p) n -> p k n", p=128))], w=["w_uk_bf"], eng="pool")
    ld([(w_uv_bf0, w_uv_d.rearrange("(k p) n -> p k n", p=128))], w=["w_uv_bf"], eng="pool")
    AB = P.sb([128, 6, D], F32, "AB")
    iota_i = P.sb([128, 128], I32, "iota_i")
    iota_f = iota_f_keep
    pool(lambda e: e.iota(iota_i, pattern=[[1, 128]], base=0, channel_multiplier=-1), w=["iota_i"])
    dve(lambda e: e.tensor_copy(out=iota_f, in_=iota_i), r=["iota_i"], w=["iota_f"])
    dve(lambda e: e.tensor_single_scalar(out=identf, in_=iota_f, scalar=0.0, op=ALU.is_equal), r=["iota_f"], w=["identf"])
    dve(lambda e: e.tensor_copy(out=ident, in_=identf), r=["identf"], w=["ident"])
    dve(lambda e: e.memset(ones_f, 1.0), w=["ones_f"])
    dve(lambda e: e.memset(eps_t, EPS), w=["eps_t"])

    c_sb = P.sb([128, 8], F32, "c_sb")
    sc_sb = P.sb([128, 8], F32, "sc_sb")
    ld([(c_sb, c_d)], w=["c_sb"])
    act(lambda e: e.activation(out=sc_sb, in_=c_sb, func=AF.Silu), r=["c_sb"], w=["sc_sb"])
    ada_row = P.sb([1, 6 * D], F32, "ada_row")
    bada = P.sb([1, 6 * D], F32, "bada")
    gv = P.sb([1, 4 * D], F32, "gv")
    ld([(bada, b_ada_d)], w=["bada"])
    ld([(gv, gvec_d)], w=["gv"])
    wst = [P.sb([128, 8, 512], F32, "wst%d" % i) for i in range(2)]
    w_ada_v = w_ada_d.rearrange("(k p) n -> p k n", p=128)
    for ct in range(12):
        wb = wst[ct % 2]
        key = "wst%d" % (ct % 2)
        ld([(wb[:, 0:4, :], w_ada_v[:, 0:4, ct * 512:(ct + 1) * 512]),
            (wb[:, 4:8, :], w_ada_v[:, 4:8, ct * 512:(ct + 1) * 512])], w=[key])
        ps = ps_ada

        def mm(e, wb=wb, ps=ps):
            ins = None
            for k in range(8):
                ins = e.matmul(ps[0:1, :], lhsT=sc_sb[:, k:k + 1], rhs=wb[:, k, :], start=(k == 0), stop=(k == 7))
            return ins
        pe(mm, r=[key, "sc_sb"], w=["ps_ada"])
        dve(lambda e, ct=ct, ps=ps: e.tensor_tensor(out=ada_row[0:1, ct * 512:(ct + 1) * 512], in0=ps[0:1, :],
                                                  in1=bada[0:1, ct * 512:(ct + 1) * 512], op=ALU.add),
            r=["ps_ada", "bada"], w=["ada_row"])
    rows = bada.rearrange("p (k n) -> p k n", k=6)

    def seg(k):
        return ada_row[0:1, k * D:(k + 1) * D]

    def gseg(k):
        return gv[0:1, k * D:(k + 1) * D]
    dve(lambda e: e.scalar_tensor_tensor(out=rows[0:1, 0, :], in0=seg(1), scalar=1.0, in1=gseg(0), op0=ALU.add, op1=ALU.mult),
        r=["ada_row", "gv"], w=["rows"])
    dve(lambda e: e.tensor_copy(out=rows[0:1, 1, :], in_=seg(0)), r=["ada_row", "rows"], w=["rows"])
    dve(lambda e: e.tensor_tensor(out=rows[0:1, 2, :], in0=seg(2), in1=gseg(1), op=ALU.mult), r=["ada_row", "gv", "rows"], w=["rows"])
    dve(lambda e: e.scalar_tensor_tensor(out=rows[0:1, 3, :], in0=seg(4), scalar=1.0, in1=gseg(2), op0=ALU.add, op1=ALU.mult),
        r=["ada_row", "gv", "rows"], w=["rows"])
    dve(lambda e: e.tensor_copy(out=rows[0:1, 4, :], in_=seg(3)), r=["ada_row", "rows"], w=["rows"])
    dve(lambda e: e.tensor_tensor(out=rows[0:1, 5, :], in0=seg(5), in1=gseg(3), op=ALU.mult), r=["ada_row", "gv", "rows"], w=["rows"])
    for k in range(6):
        for hh in range(2):
            pe(lambda e, k=k, hh=hh: e.matmul(ps_ada[:, :], lhsT=ones_f[0:1, :], rhs=rows[0:1, k, hh * 512:(hh + 1) * 512],
                                             start=True, stop=True), r=["rows", "ones_f", "ps_ada"], w=["ps_ada"])
            act(lambda e, k=k, hh=hh: e.copy(out=AB[:, k, hh * 512:(hh + 1) * 512], in_=ps_ada[:, :]), r=["ps_ada"], w=["AB"])

    pos_i = P.sb([128, 32], I32, "pos_i")
    pos_f = P.sb([128, 32], F32, "pos_f")
    ld([(pos_i, pos_d)], w=["pos_i"])
    dve(lambda e: e.tensor_copy(out=pos_f, in_=pos_i), r=["pos_i"], w=["pos_f"])
    invf = P.sb([128, 16], F32, "invf")
    for j in range(16):
        val = float(np.float32(10000.0) ** np.float32(-(2.0 * j) / 32.0)) / (2.0 * math.pi)
        dve(lambda e, j=j, val=val: e.memset(invf[:, j:j + 1], val), r=["invf"], w=["invf"])
    turns = P.sb([128, 32, 2, 16], F32, "turns")
    tint = P.sb([128, 32, 2, 16], I32, "tint")
    tfl = P.sb([128, 32, 2, 16], F32, "tfl")
    for t in range(32):
        dve(lambda e, t=t: e.tensor_scalar(out=turns[:, t, 1, :], in0=invf, scalar1=pos_f[:, t:t + 1], scalar2=None, op0=ALU.mult),
            r=["invf", "pos_f", "turns"], w=["turns"])
    dve(lambda e: e.tensor_scalar_add(out=turns[:, :, 0, :], in0=turns[:, :, 1, :], scalar1=0.25), r=["turns"], w=["turns"])
    dve(lambda e: e.tensor_copy(out=tint, in_=turns), r=["turns"], w=["tint"])
    dve(lambda e: e.tensor_copy(out=tfl, in_=tint), r=["tint"], w=["tfl"])
    dve(lambda e: e.tensor_tensor(out=turns, in0=turns, in1=tfl, op=ALU.subtract), r=["turns", "tfl"], w=["turns"])
    dve(lambda e: e.tensor_single_scalar(out=tfl, in_=turns, scalar=0.5, op=ALU.is_gt), r=["turns", "tfl"], w=["tfl"])
    dve(lambda e: e.tensor_tensor(out=turns, in0=turns, in1=tfl, op=ALU.subtract), r=["turns", "tfl"], w=["turns"])
    dve(lambda e: e.tensor_single_scalar(out=tfl, in_=turns, scalar=-0.5, op=ALU.is_lt), r=["turns", "tfl"], w=["tfl"])
    dve(lambda e: e.tensor_tensor(out=turns, in0=turns, in1=tfl, op=ALU.add), r=["turns", "tfl"], w=["turns"])
    act(lambda e: e.activation(out=cs_tab, in_=turns, func=AF.Sin, scale=2.0 * math.pi), r=["turns"], w=["cs_tab"])
    ld([(AB_s, AB)], r=["AB"], w=["AB_s"], eng="pool")
    if "AB" in dbg_d:
        ld([(dbg_d["AB"], AB[0:1, :, :])], r=["AB"], w=["dbg_AB"], eng="pool")
    if "cs" in dbg_d:
        ld([(dbg_d["cs"], cs_tab)], r=["cs_tab"], w=["dbg_cs"], eng="pool")
    P.flush()
    P.release()
    L = dict(locals())
    L["iota_f_p"] = L["iota_f_keep"]
    stage1(L)
    s5_main, s5_tail, base = stage_s5(L)
    attn_core, attn_proj = stage_attn(L)
    P.merge([P.capture(s5_main), P.capture(attn_core)])
    P.flush()
    P.sb_off = P.sb_mark = base
    ABf0 = P.sb([128, 4, D], F32, "ABf")
    wm_bf0 = P.sb([128, 8, D], BF16, "wm_bf")
    w2_bf0 = P.sb([128, 22, D], BF16, "w2_bf")
    L["ABf0"], L["wm_bf0"], L["w2_bf0"] = ABf0, wm_bf0, w2_bf0
    la = P.capture(s5_tail)
    lb = P.capture(attn_proj)
    P.merge([la, lb])
    ld([(ABf0, AB_s[:, 2:6, :])], w=["ABf"])
    ld([(wm_bf0, w_mix_d.rearrange("(k p) n -> p k n", p=128))], w=["wm_bf"], eng="pool")
    ld([(w2_bf0[:, 0:11, :], w_f2_d[0:1408, :].rearrange("(k p) n -> p k n", p=128)),
        (w2_bf0[:, 11:22, :], w_f2_d[1408:2816, :].rearrange("(k p) n -> p k n", p=128))], w=["w2_bf"], eng="pool")
    P.flush()
    P.sb_off = P.sb_mark = base
    stage_ffn(L)
    return P, es, L


def rms_ops(P, L, src, n, ssq, rs, rstd, scratch, key_src, tag, sqkey="sqbuf"):
    act, dve = L["act"], L["dve"]
    act(lambda e: e.activation(out=scratch, in_=src, func=AF.Square), r=[key_src], w=[sqkey])
    dve(lambda e: e.reduce_sum(out=ssq, in_=scratch, axis=AX.X), r=[sqkey], w=["ssq" + tag])
    act(lambda e: e.activation(out=rs, in_=ssq, func=AF.Sqrt, scale=1.0 / n, bias=L["eps_t"][:, 0:1]), r=["ssq" + tag], w=["rs" + tag])
    dve(lambda e: e.reciprocal(out=rstd, in_=rs), r=["rs" + tag], w=["rstd" + tag])


def stage1(L):
    P, nc = L["P"], L["nc"]
    act, dve, pool, pe, ld = L["act"], L["dve"], L["pool"], L["pe"], L["ld"]
    pf, pb, ident, cs_tab, ones_f = L["pf"], L["pb"], L["ident"], L["cs_tab"], L["ones_f"]
    x_d, w_in_d = L["x_d"], L["w_in_d"]
    w_in_bf, w_uq_bf, w_uk_bf, w_uv_bf = L["w_in_bf0"], L["w_uq_bf0"], L["w_uk_bf0"], L["w_uv_bf0"]
    P.skip([128, 8, IN_COLS], BF16); P.skip([128, 3, 768], BF16); P.skip([128, 2, 512], BF16); P.skip([128, 2, 512], BF16)
    AB = P.sb([128, 2, D], F32, "AB1")
    ld([(AB, L["AB_s"][:, 0:2, :])], w=["AB"])

    def dscr(name, shape, dt):
        return nc.dram_tensor(name, list(shape), dt, kind="Internal").ap()
    KT_s = L["KT_s"] = dscr("KT_s", [96, 8, SEQ], BF16)
    V_s = L["V_s"] = dscr("V_s", [SEQ, 8 * 65], BF16)
    QT_s = L["QT_s"] = dscr("QT_s", [96, 8, OWN], BF16)
    uT_s = L["uT_s"] = dscr("uT_s", [SSMC, SEQ], BF16)
    uT2_s = L["uT2_s"] = dscr("uT2_s", [SSMC, 8, SEQ // 8], BF16)
    osb2 = [P.sb([128, 8, 64], BF16, "osc%d" % i) for i in range(2)]
    gate_s = L["gate_s"]
    dbg_d = L["dbg_d"]

    eps_t = L["eps_t"]
    gq_b = P.sb([128, QLR], F32, "gq_b")
    gkv_b = P.sb([128, KVR], F32, "gkv_b")
    grow = P.sb([1, QLR + KVR], F32, "grow")
    ld([(grow[0:1, 0:QLR], L["gq_d"]), (grow[0:1, QLR:], L["gkv_d"])], w=["grow"])
    pe(lambda e: e.matmul(pf[0][:, 0:QLR], lhsT=ones_f[0:1, :], rhs=grow[0:1, 0:QLR], start=True, stop=True), r=["grow", "pf0"], w=["pf0"])
    act(lambda e: e.copy(out=gq_b, in_=pf[0][:, 0:QLR]), r=["pf0"], w=["gq_b"])
    pe(lambda e: e.matmul(pf[0][:, 0:KVR], lhsT=ones_f[0:1, :], rhs=grow[0:1, QLR:], start=True, stop=True), r=["grow", "pf0"], w=["pf0"])
    act(lambda e: e.copy(out=gkv_b, in_=pf[0][:, 0:KVR]), r=["pf0"], w=["gkv_b"])

    xt = [P.sb([128, D], F32, "xt%d" % i) for i in range(2)]
    sq = P.sb([128, D], F32, "sq")
    tmp = P.sb([128, D], F32, "tmp")
    hb = P.sb([128, D], BF16, "hb")
    hT = P.sb([128, 8, 512], BF16, "hT")
    st = P.sb([128, 16], F32, "st")
    osb = [P.sb([128, 512], BF16, "osb%d" % i) for i in range(2)]
    QSC = float(96.0 ** -0.5)
    oi = 0
    hTs = [hT, P.sb([128, 8, 512], BF16, "hT1")]
    sqx = P.sb([128, D], F32, "sqx")
    oi_box = [0]

    hbs = [hb, P.sb([128, D], BF16, "hb1")]

    def chain(bi):
        hTb, hk = hTs[bi % 2], "hT%d" % (bi % 2)

        def partA(tt):
            t = bi * 4 + tt
            xb, xk = xt[t % 2], "xt%d" % (t % 2)
            hbt, hbk = hbs[tt % 2], "hb%d" % (tt % 2)
            ld([(xb[:, 0:512], x_d[t * 128:(t + 1) * 128, 0:512]), (xb[:, 512:], x_d[t * 128:(t + 1) * 128, 512:])], w=[xk])
            rms_ops(P, L, xb, D, st[:, 0:1], st[:, 1:2], st[:, 2:3], sqx, xk, "x", "sqx")
            dve(lambda e, xb=xb: e.scalar_tensor_tensor(out=tmp, in0=xb, scalar=st[:, 2:3], in1=AB[:, 0, :], op0=ALU.mult, op1=ALU.mult),
                r=[xk, "rstdx", "AB"], w=["tmp"])
            dve(lambda e, hbt=hbt: e.tensor_tensor(out=hbt, in0=tmp, in1=AB[:, 1, :], op=ALU.add), r=["tmp", "AB"], w=[hbk])

        def partB(tt):
            hbt, hbk = hbs[tt % 2], "hb%d" % (tt % 2)

            def tr(e, hbt=hbt):
                ins = None
                for k in range(8):
                    ins = e.transpose(pb[0][:, k * 128:(k + 1) * 128], hbt[:, k * 128:(k + 1) * 128], ident)
                return ins
            pe(tr, r=[hbk, "ident"], w=["pb0"])
            act(lambda e, tt=tt, hTb=hTb: e.copy(out=hTb[:, :, tt * 128:(tt + 1) * 128], in_=pb[0].rearrange("p (k t) -> p k t", k=8)),
                r=["pb0"], w=[hk])
        for tt in range(5):
            if tt < 4:
                partA(tt)
            if tt >= 1:
                partB(tt - 1)

    def proj(bi):
        own = bi < 4
        hT, hk = hTs[bi % 2], "hT%d" % (bi % 2)
        oi = oi_box[0]
        bc = slice(bi * 512, (bi + 1) * 512)
        cols = [(672 + 128 * j, ("u", j)) for j in range(4)]
        if own:
            cols += [(1184 + 128 * j, ("g", j)) for j in range(16)]
        for c0, (kind, j) in cols:
            psx, pk = pf[2], "pf2"
            ob, ok = osb[oi % 2], "osb%d" % (oi % 2)
            oi += 1
            oi_box[0] = oi

            def mm(e, c0=c0, psx=psx):
                ins = None
                for k in range(8):
                    ins = e.matmul(psx, lhsT=w_in_bf[:, k, c0:c0 + 128], rhs=hT[:, k, :], start=(k == 0), stop=(k == 7))
                return ins
            pe(mm, r=["w_in_bf", hk, pk], w=[pk])
            if kind == "u":
                act(lambda e, psx=psx, ob=ob: e.copy(out=ob, in_=psx), r=[pk], w=[ok])
                ld([(uT_s[j * 128:(j + 1) * 128, bc], ob)], r=[ok], w=["uT_s"], eng="pool")
                o2, o2k = osb2[j % 2], "osc%d" % (j % 2)
                dve(lambda e, ob=ob, o2=o2: e.tensor_copy(out=o2, in_=ob.rearrange("p (c s) -> p s c", s=8)), r=[ok], w=[o2k])
                ld([(uT2_s[j * 128:(j + 1) * 128, :, bi * 64:(bi + 1) * 64], o2)], r=[o2k], w=["uT2_s"], eng="pool")
            else:
                act(lambda e, psx=psx, ob=ob: e.activation(out=ob, in_=psx, func=AF.Sigmoid), r=[pk], w=[ok])
                ld([(gate_s[j, :, bc], ob)], r=[ok], w=["gate_s"], eng="pool")
        for t0 in (0, 2):
            P.merge([P.capture(lambda: tile_ops(bi, t0, hT, hk, own)), P.capture(lambda: tile_ops(bi, t0 + 1, hT, hk, own))])

    class TRes:
        def __init__(self, par, lat, latk, big, bigk, trb, trbk):
            self.p = str(par)
            self.lat, self.latk, self.big, self.bigk, self.trb, self.trbk = lat, latk, big, bigk, trb, trbk
            n = "_%d" % par
            self.st = P.sb([128, 16], F32, "tst" + n)
            self.sq = P.sb([128, QLR], F32, "tsq" + n)
            self.kvn = P.sb([128, KVR], BF16, "kvn" + n)
            self.kvnT = P.sb([128, 2, 128], BF16, "kvnT" + n)
            self.rp = P.sb([128, 6, 16], F32, "rp" + n)
            self.kr = P.sb([128, 32], BF16, "kr" + n)
            self.kasm = P.sb([128, 8, 96], BF16, "kasm" + n)
            self.vsb = P.sb([128, 8, 65], BF16, "vsb" + n)
            self.ktsb = P.sb([128, 8, 128], BF16, "ktsb" + n)
            self.qn = P.sb([128, QLR], BF16, "qn" + n)
            self.qnT = P.sb([128, 3, 128], BF16, "qnT" + n)
            self.qf = P.sb([128, 8, 96], F32, "qf" + n)
            self.qr = P.sb([128, 4, 8, 16], F32, "qr" + n)
            self.qasm = P.sb([128, 8, 96], BF16, "qasm" + n)
            self.qtsb = P.sb([128, 8, 128], BF16, "qtsb" + n)
            vs = self.vsb
            dve(lambda e: e.memset(vs, 1.0), w=["vsb" + self.p])
    tres = [TRes(0, pf[3], "pf3", pf[4], "pf4", pb[1], "pb1"),
            TRes(1, pf[0], "pf0", pf[1], "pf1", pf[5].bitcast(BF16), "pf5")]

    def tile_ops(bi, tt, hT, hk, own):
        R = tres[tt % 2]
        p_ = R.p
        K = lambda name: name + p_
        t = bi * 4 + tt
        tk = slice(tt * 128, (tt + 1) * 128)
        lat, latk, big, bigk, trb, trbk, st = R.lat, R.latk, R.big, R.bigk, R.trb, R.trbk, R.st
        kvn, kvnT, rp, kr, kasm, vsb, ktsb = R.kvn, R.kvnT, R.rp, R.kr, R.kasm, R.vsb, R.ktsb
        qn, qnT, qf, qr, qasm, qtsb, sq = R.qn, R.qnT, R.qf, R.qr, R.qasm, R.qtsb, R.sq

        def mmkv(e):
            ins = None
            for k in range(8):
                ins = e.matmul(lat[:, 0:288], lhsT=hT[:, k, tk], rhs=w_in_bf[:, k, 384:672], start=(k == 0), stop=(k == 7))
            return ins
        pe(mmkv, r=["w_in_bf", hk, latk], w=[latk])
        rms_ops(P, L, lat[:, 0:KVR], KVR, st[:, 4:5], st[:, 5:6], st[:, 6:7], sq[:, 0:KVR], latk, "kv" + p_, K("sqp"))
        dve(lambda e: e.scalar_tensor_tensor(out=kvn, in0=lat[:, 0:KVR], scalar=st[:, 6:7], in1=gkv_b, op0=ALU.mult, op1=ALU.mult),
            r=[latk, "rstdkv" + p_, "gkv_b"], w=[K("kvn")])
        cosv, sinv = cs_tab[:, t, 0, :], cs_tab[:, t, 1, :]
        x1, x2 = lat[:, 256:272], lat[:, 272:288]
        rk = [latk, "cs_tab", K("rp"), "rstdkv" + p_]
        dve(lambda e: e.tensor_tensor(out=rp[:, 0, :], in0=x1, in1=cosv, op=ALU.mult), r=rk, w=[K("rp")])
        dve(lambda e: e.tensor_tensor(out=rp[:, 1, :], in0=x2, in1=sinv, op=ALU.mult), r=rk, w=[K("rp")])
        dve(lambda e: e.tensor_tensor(out=rp[:, 2, :], in0=x2, in1=cosv, op=ALU.mult), r=rk, w=[K("rp")])
        dve(lambda e: e.tensor_tensor(out=rp[:, 3, :], in0=x1, in1=sinv, op=ALU.mult), r=rk, w=[K("rp")])
        dve(lambda e: e.tensor_tensor(out=kr[:, 0:16], in0=rp[:, 0, :], in1=rp[:, 1, :], op=ALU.subtract), r=[K("rp")], w=[K("kr")])
        dve(lambda e: e.tensor_tensor(out=kr[:, 16:32], in0=rp[:, 2, :], in1=rp[:, 3, :], op=ALU.add), r=[K("rp"), K("kr")], w=[K("kr")])
        dve(lambda e: e.tensor_copy(out=kasm[:, :, 64:96], in_=kr.unsqueeze(1).to_broadcast([128, 8, 32])), r=[K("kr"), K("kasm")], w=[K("kasm")])

        def tr2(e):
            ins = None
            for k in range(2):
                ins = e.transpose(trb[:, k * 128:(k + 1) * 128], kvn[:, k * 128:(k + 1) * 128], ident)
            return ins
        pe(tr2, r=[K("kvn"), "ident", trbk], w=[trbk])
        act(lambda e: e.copy(out=kvnT, in_=trb[:, 0:256].rearrange("p (k t) -> p k t", k=2)), r=[trbk], w=[K("kvnT")])

        def mmk(e, wsb):
            ins = None
            for k in range(2):
                ins = e.matmul(big, lhsT=kvnT[:, k, :], rhs=wsb[:, k, :], start=(k == 0), stop=(k == 1))
            return ins
        pe(lambda e: mmk(e, w_uk_bf), r=[K("kvnT"), "w_uk_bf", bigk], w=[bigk])
        act(lambda e: e.copy(out=kasm[:, :, 0:64], in_=big.rearrange("p (h d) -> p h d", h=8)), r=[bigk, K("kasm")], w=[K("kasm")])
        pe(lambda e: mmk(e, w_uv_bf), r=[K("kvnT"), "w_uv_bf", bigk], w=[bigk])
        act(lambda e: e.copy(out=vsb[:, :, 0:64], in_=big.rearrange("p (h d) -> p h d", h=8)), r=[bigk, K("vsb")], w=[K("vsb")])
        ld([(V_s[t * 128:(t + 1) * 128, :], vsb.rearrange("p h d -> p (h d)"))], r=[K("vsb")], w=["V_s" + p_], eng="pool")

        def trk(e, src):
            ins = None
            for h in range(8):
                ins = e.transpose(trb[0:96, h * 128:(h + 1) * 128], src[:, h, :], ident)
            return ins
        pe(lambda e: trk(e, kasm), r=[K("kasm"), "ident", trbk], w=[trbk])
        act(lambda e: e.copy(out=ktsb[0:96, :, :], in_=trb[0:96, :].rearrange("p (h t) -> p h t", h=8)), r=[trbk], w=[K("ktsb")])
        ld([(KT_s[:, :, t * 128:(t + 1) * 128], ktsb[0:96, :, :])], r=[K("ktsb")], w=["KT_s" + p_], eng="pool")
        if not own:
            return
        def mmq(e):
            ins = None
            for k in range(8):
                ins = e.matmul(lat[:, 0:QLR], lhsT=hT[:, k, tk], rhs=w_in_bf[:, k, 0:QLR], start=(k == 0), stop=(k == 7))
            return ins
        pe(mmq, r=["w_in_bf", hk, latk], w=[latk])
        rms_ops(P, L, lat[:, 0:QLR], QLR, st[:, 8:9], st[:, 9:10], st[:, 10:11], sq[:, 0:QLR], latk, "q" + p_, K("sqp"))
        dve(lambda e: e.scalar_tensor_tensor(out=qn, in0=lat[:, 0:QLR], scalar=st[:, 10:11], in1=gq_b, op0=ALU.mult, op1=ALU.mult),
            r=[latk, "rstdq" + p_, "gq_b"], w=[K("qn")])

        def tr3(e):
            ins = None
            for k in range(3):
                ins = e.transpose(trb[:, k * 128:(k + 1) * 128], qn[:, k * 128:(k + 1) * 128], ident)
            return ins
        pe(tr3, r=[K("qn"), "ident", trbk], w=[trbk])
        act(lambda e: e.copy(out=qnT, in_=trb[:, 0:384].rearrange("p (k t) -> p k t", k=3)), r=[trbk], w=[K("qnT")])

        def mmq2(e):
            ins = None
            for (psx, c0, cw) in ((big, 0, 512), (lat, 512, 256)):
                for k in range(3):
                    ins = e.matmul(psx[:, 0:cw], lhsT=qnT[:, k, :], rhs=w_uq_bf[:, k, c0:c0 + cw], start=(k == 0), stop=(k == 2))
            return ins
        pe(mmq2, r=[K("qnT"), "w_uq_bf", bigk, latk], w=[bigk, latk])
        qfl = qf.rearrange("p h d -> p (h d)")
        act(lambda e: e.activation(out=qfl[:, 0:512], in_=big, func=AF.Copy, scale=QSC), r=[bigk, K("qf")], w=[K("qf")])
        act(lambda e: e.activation(out=qfl[:, 512:768], in_=lat[:, 0:256], func=AF.Copy, scale=QSC), r=[latk, K("qf")], w=[K("qf")])
        dve(lambda e: e.tensor_copy(out=qasm[:, :, 0:64], in_=qf[:, :, 0:64]), r=[K("qf"), K("qasm")], w=[K("qasm")])
        cb_ = cosv.unsqueeze(1).to_broadcast([128, 8, 16])
        sb_ = sinv.unsqueeze(1).to_broadcast([128, 8, 16])
        qk = [K("qf"), "cs_tab", K("qr")]
        dve(lambda e: e.tensor_tensor(out=qr[:, 0], in0=qf[:, :, 64:80], in1=cb_, op=ALU.mult), r=qk, w=[K("qr")])
        dve(lambda e: e.tensor_tensor(out=qr[:, 1], in0=qf[:, :, 80:96], in1=sb_, op=ALU.mult), r=qk, w=[K("qr")])
        dve(lambda e: e.tensor_tensor(out=qr[:, 2], in0=qf[:, :, 80:96], in1=cb_, op=ALU.mult), r=qk, w=[K("qr")])
        dve(lambda e: e.tensor_tensor(out=qr[:, 3], in0=qf[:, :, 64:80], in1=sb_, op=ALU.mult), r=qk, w=[K("qr")])
        dve(lambda e: e.tensor_tensor(out=qasm[:, :, 64:80], in0=qr[:, 0], in1=qr[:, 1], op=ALU.subtract), r=[K("qr"), K("qasm")], w=[K("qasm")])
        dve(lambda e: e.tensor_tensor(out=qasm[:, :, 80:96], in0=qr[:, 2], in1=qr[:, 3], op=ALU.add), r=[K("qr"), K("qasm")], w=[K("qasm")])
        pe(lambda e: trk(e, qasm), r=[K("qasm"), "ident", trbk], w=[trbk])
        act(lambda e: e.copy(out=qtsb[0:96, :, :], in_=trb[0:96, :].rearrange("p (h t) -> p h t", h=8)), r=[trbk], w=[K("qtsb")])
        ld([(QT_s[:, :, t * 128:(t + 1) * 128], qtsb[0:96, :, :])], r=[K("qtsb")], w=["QT_s" + p_], eng="pool")

    w1b_s = L["w1b_s"] = dscr("w1b_s", [22, 128, 8, 256], BF16)
    w1v_ = L["w_f1_d"].rearrange("(k p) n -> p k n", p=128)
    sc_list = P.capture(lambda: s5_scalar(L))
    P.merge([P.capture(lambda: chain(0))])
    for bi in range(8):
        lists = [P.capture(lambda: proj(bi))]
        if bi + 1 < 8:
            lists.append(P.capture(lambda: chain(bi + 1)))
        if bi == 0:
            lists.append(sc_list)
        P.merge(lists)
        if bi < 4:
            prs = []
            for hc in range(bi * 6, min(22, bi * 6 + 6)):
                prs += [(w1b_s[hc, :, :, 0:128], w1v_[:, :, hc * 128:(hc + 1) * 128]),
                        (w1b_s[hc, :, :, 128:256], w1v_[:, :, FFH + hc * 128:FFH + (hc + 1) * 128])]
            ld(prs, w=["w1b_s_%d" % bi], eng="pool")
    for nm, src in (("KT", KT_s), ("QT", QT_s), ("V", V_s), ("uT", uT_s)):
        if nm in dbg_d:
            ld([(dbg_d[nm], src)], r=[nm + "_s"], w=["dbg_" + nm], eng="pool")
    P.flush()
    P.release()


def s5_scalar(L):
    P, nc = L["P"], L["nc"]
    act, dve, pool, pe, ld = L["act"], L["dve"], L["pool"], L["pe"], L["ld"]

    def T(shape, dt=F32, name="s5s"):
        return P.sb(shape, dt, name)

    def bc3(v, m):
        return v.unsqueeze(2).to_broadcast([v.shape[0], v.shape[1], m])

    MU = T([64, 3, 2, 64])
    lam = T([64, 2, 64]); ldt = T([64, 64]); Bc = T([64, 2, 64, 16])
    ld([(lam, L["lam_d"]), (ldt, L["ldt_d"]), (Bc, L["sb_d"])], w=["lam", "ldt", "Bc"])
    sm = T([64, 24, 64])
    K = "sm"

    def d2(fn, r=(), w=()):
        dve(fn, r=list(r) + [K], w=list(w) + [K])
    lre, lim, dt_, a_, th, mag = (sm[:, i, :] for i in range(6))
    d2(lambda e: e.tensor_scalar_min(out=lre, in0=lam[:, 0, :], scalar1=-1e-4), r=["lam"])
    d2(lambda e: e.tensor_copy(out=lim, in_=lam[:, 1, :]), r=["lam"])
    act(lambda e: e.activation(out=dt_, in_=ldt, func=AF.Exp), r=["ldt", K], w=[K])
    d2(lambda e: e.tensor_tensor(out=a_, in0=lre, in1=dt_, op=ALU.mult))
    d2(lambda e: e.tensor_tensor(out=th, in0=lim, in1=dt_, op=ALU.mult))
    act(lambda e: e.activation(out=mag, in_=a_, func=AF.Exp), r=[K], w=[K])
    tr_ = T([64, 2, 64]); ti_ = T([64, 2, 64], I32); tf_ = T([64, 2, 64])
    d2(lambda e: e.tensor_single_scalar(out=tr_[:, 1, :], in_=th, scalar=1.0 / (2 * math.pi), op=ALU.mult), w=["tr_"])
    dve(lambda e: e.tensor_scalar_add(out=tr_[:, 0, :], in0=tr_[:, 1, :], scalar1=0.25), r=["tr_"], w=["tr_"])
    dve(lambda e: e.tensor_copy(out=ti_, in_=tr_), r=["tr_"], w=["ti_"])
    dve(lambda e: e.tensor_copy(out=tf_, in_=ti_), r=["ti_"], w=["tf_"])
    dve(lambda e: e.tensor_tensor(out=tr_, in0=tr_, in1=tf_, op=ALU.subtract), r=["tr_", "tf_"], w=["tr_"])
    dve(lambda e: e.tensor_single_scalar(out=tf_, in_=tr_, scalar=0.5, op=ALU.is_gt), r=["tr_", "tf_"], w=["tf_"])
    dve(lambda e: e.tensor_tensor(out=tr_, in0=tr_, in1=tf_, op=ALU.subtract), r=["tr_", "tf_"], w=["tr_"])
    dve(lambda e: e.tensor_single_scalar(out=tf_, in_=tr_, scalar=-0.5, op=ALU.is_lt), r=["tr_", "tf_"], w=["tf_"])
    dve(lambda e: e.tensor_tensor(out=tr_, in0=tr_, in1=tf_, op=ALU.add), r=["tr_", "tf_"], w=["tr_"])
    cs = T([64, 2, 64])
    act(lambda e: e.activation(out=cs, in_=tr_, func=AF.Sin, scale=2.0 * math.pi), r=["tr_"], w=["cs"])
    PW = T([64, 2, 16, 64])
    KP = "PW"

    def pw(k):
        return PW[:, 0, k + 7, :], PW[:, 1, k + 7, :]
    ab_re, ab_im = pw(1)
    dve(lambda e: e.tensor_tensor(out=ab_re, in0=mag, in1=cs[:, 0, :], op=ALU.mult), r=[K, "cs", KP], w=[KP])
    dve(lambda e: e.tensor_tensor(out=ab_im, in0=mag, in1=cs[:, 1, :], op=ALU.mult), r=[K, "cs", KP], w=[KP])
    one_re, one_im = pw(0)
    dve(lambda e: e.memset(one_re, 1.0), r=[KP], w=[KP]); dve(lambda e: e.memset(one_im, 0.0), r=[KP], w=[KP])
    s0, s1, s2, s3 = (sm[:, i, :] for i in range(6, 10))

    def cmul(o_re, o_im, x_re, x_im, y_re, y_im, keys):
        kk = list(keys) + [K]
        dve(lambda e: e.tensor_tensor(out=s0, in0=x_re, in1=y_re, op=ALU.mult), r=kk, w=[K])
        dve(lambda e: e.tensor_tensor(out=s1, in0=x_im, in1=y_im, op=ALU.mult), r=kk, w=[K])
        dve(lambda e: e.tensor_tensor(out=s2, in0=x_re, in1=y_im, op=ALU.mult), r=kk, w=[K])
        dve(lambda e: e.tensor_tensor(out=s3, in0=x_im, in1=y_re, op=ALU.mult), r=kk, w=[K])
        dve(lambda e: e.tensor_tensor(out=o_re, in0=s0, in1=s1, op=ALU.subtract), r=kk, w=kk)
        dve(lambda e: e.tensor_tensor(out=o_im, in0=s2, in1=s3, op=ALU.add), r=kk, w=kk)
    inv_re, inv_im = pw(-1)
    m2, rm2 = sm[:, 10, :], sm[:, 11, :]
    d2(lambda e: e.tensor_tensor(out=s0, in0=ab_re, in1=ab_re, op=ALU.mult), r=[KP])
    d2(lambda e: e.tensor_tensor(out=s1, in0=ab_im, in1=ab_im, op=ALU.mult), r=[KP])
    d2(lambda e: e.tensor_tensor(out=m2, in0=s0, in1=s1, op=ALU.add))
    d2(lambda e: e.reciprocal(out=rm2, in_=m2))
    dve(lambda e: e.tensor_tensor(out=inv_re, in0=ab_re, in1=rm2, op=ALU.mult), r=[K, KP], w=[KP])
    dve(lambda e: e.scalar_tensor_tensor(out=inv_im, in0=ab_im, scalar=-1.0, in1=rm2, op0=ALU.mult, op1=ALU.mult), r=[K, KP], w=[KP])
    for k in range(1, 8):
        cmul(*pw(k + 1), *pw(k), ab_re, ab_im, [KP])
    for k in range(1, 7):
        cmul(*pw(-k - 1), *pw(-k), inv_re, inv_im, [KP])
    dve(lambda e: e.tensor_copy(out=MU[:, 0, 0, :], in_=pw(8)[0]), r=[KP], w=["MU"])
    dve(lambda e: e.tensor_copy(out=MU[:, 0, 1, :], in_=pw(8)[1]), r=[KP, "MU"], w=["MU"])
    sq_ = T([64, 2, 2, 64])
    for lv in (1, 2):
        src = (MU[:, lv - 1, 0, :], MU[:, lv - 1, 1, :])
        for it in range(3):
            dst = (MU[:, lv, 0, :], MU[:, lv, 1, :]) if it == 2 else (sq_[:, it, 0, :], sq_[:, it, 1, :])
            cmul(dst[0], dst[1], src[0], src[1], src[0], src[1], ["MU", "sq_"])
            src = dst
    nr, den, rden, f_re, f_im = (sm[:, i, :] for i in range(12, 17))
    d2(lambda e: e.tensor_scalar_add(out=nr, in0=ab_re, scalar1=-1.0), r=[KP])
    d2(lambda e: e.tensor_tensor(out=s0, in0=lre, in1=lre, op=ALU.mult))
    d2(lambda e: e.tensor_tensor(out=s1, in0=lim, in1=lim, op=ALU.mult))
    d2(lambda e: e.tensor_tensor(out=den, in0=s0, in1=s1, op=ALU.add))
    d2(lambda e: e.reciprocal(out=rden, in_=den))
    d2(lambda e: e.tensor_tensor(out=s0, in0=nr, in1=lre, op=ALU.mult))
    d2(lambda e: e.tensor_tensor(out=s1, in0=ab_im, in1=lim, op=ALU.mult), r=[KP])
    d2(lambda e: e.tensor_tensor(out=s0, in0=s0, in1=s1, op=ALU.add))
    d2(lambda e: e.tensor_tensor(out=f_re, in0=s0, in1=rden, op=ALU.mult))
    d2(lambda e: e.tensor_tensor(out=s0, in0=ab_im, in1=lre, op=ALU.mult), r=[KP])
    d2(lambda e: e.tensor_tensor(out=s1, in0=nr, in1=lim, op=ALU.mult))
    d2(lambda e: e.tensor_tensor(out=s0, in0=s0, in1=s1, op=ALU.subtract))
    d2(lambda e: e.tensor_tensor(out=f_im, in0=s0, in1=rden, op=ALU.mult))
    Bb = T([64, 2, 64, 16]); t16 = T([64, 2, 64, 16])

    def cmul16(o_re, o_im, s_re, s_im, x_re, x_im, n, keys_r, keys_w, ta, tb, eng=None, tk="t16"):
        eng = eng or dve
        sr, si = bc3(s_re, 16), bc3(s_im, 16)
        kr = list(keys_r) + list(keys_w) + [tk]
        eng(lambda e: e.tensor_tensor(out=ta, in0=x_re, in1=sr, op=ALU.mult), r=kr, w=[tk])
        eng(lambda e: e.tensor_tensor(out=tb, in0=x_im, in1=si, op=ALU.mult), r=kr, w=[tk])
        eng(lambda e: e.tensor_tensor(out=o_re, in0=ta, in1=tb, op=ALU.subtract), r=kr, w=list(keys_w) + [tk])
        eng(lambda e: e.tensor_tensor(out=ta, in0=x_im, in1=sr, op=ALU.mult), r=kr, w=[tk])
        eng(lambda e: e.tensor_tensor(out=tb, in0=x_re, in1=si, op=ALU.mult), r=kr, w=[tk])
        eng(lambda e: e.tensor_tensor(out=o_im, in0=ta, in1=tb, op=ALU.add), r=kr, w=list(keys_w) + [tk])
    cmul16(Bb[:, 0], Bb[:, 1], f_re, f_im, Bc[:, 0], Bc[:, 1], 64, [K, "Bc"], ["Bb"], t16[:, 0], t16[:, 1])
    PW_s = L["PW_s"] = nc.dram_tensor("PW_s", [64, 2, 16, 64], F32, kind="Internal").ap()
    Bb_s = L["Bb_s"] = nc.dram_tensor("Bb_s", [64, 2, 64, 16], F32, kind="Internal").ap()
    MU_s = L["MU_s"] = nc.dram_tensor("MU_s", [64, 3, 2, 64], F32, kind="Internal").ap()
    ld([(PW_s, PW), (Bb_s, Bb), (MU_s, MU)], r=[KP, "Bb", "MU"], w=["PW_s", "Bb_s", "MU_s"], eng="pool")


def stage_s5(L):
    P, nc = L["P"], L["nc"]
    act, dve, pool, pe, ld = L["act"], L["dve"], L["pool"], L["pe"], L["ld"]
    pf, pb, identf, iota_f = L["pf"], L["pb"], L["identf"], L["iota_f_p"]
    uT_s, brb_s, dbg_d = L["uT_s"], L["brb_s"], L["dbg_d"]
    base_mark = P.sb_mark

    def T(shape, dt=F32, name="s5"):
        return P.sb(shape, dt, name)

    def bc3(v, m):
        return v.unsqueeze(2).to_broadcast([v.shape[0], v.shape[1], m])

    G = T([128, 8, 8, 128], BF16); MU = T([64, 3, 2, 64]); dsk = T([128, 4])
    MUS = T([128, 3, 2, 2, 4, 4])
    P.mark()
    Wq = [T([128, 32, 128], BF16)]
    T0q = [T([128, 32, 128], BF16)]
    Vbq = [T([128, 2, 16, 128], BF16)]
    W_s = nc.dram_tensor("W_s", [128, 64, 128], BF16, kind="Internal").ap()
    T0_s = nc.dram_tensor("T0_s", [128, 64, 128], BF16, kind="Internal").ap()
    Vb_s = nc.dram_tensor("Vb_s", [64, 2, 64, 128], BF16, kind="Internal").ap()
    GY_s = L["GY_s"] = nc.dram_tensor("GY_s", [4, 128, OWN], BF16, kind="Internal").ap()
    KP = "PW"
    PW = T([64, 2, 16, 64]); Bb = T([64, 2, 64, 16])
    ld([(PW, L["PW_s"]), (Bb, L["Bb_s"]), (MU, L["MU_s"])], w=[KP, "Bb", "MU"])

    def cmul16(o_re, o_im, s_re, s_im, x_re, x_im, n, keys_r, keys_w, ta, tb, eng=None, tk="t16"):
        eng = eng or dve
        sr, si = bc3(s_re, 16), bc3(s_im, 16)
        kr = list(keys_r) + list(keys_w) + [tk]
        eng(lambda e: e.tensor_tensor(out=ta, in0=x_re, in1=sr, op=ALU.mult), r=kr, w=[tk])
        eng(lambda e: e.tensor_tensor(out=tb, in0=x_im, in1=si, op=ALU.mult), r=kr, w=[tk])
        eng(lambda e: e.tensor_tensor(out=o_re, in0=ta, in1=tb, op=ALU.subtract), r=kr, w=list(keys_w) + [tk])
        eng(lambda e: e.tensor_tensor(out=ta, in0=x_im, in1=sr, op=ALU.mult), r=kr, w=[tk])
        eng(lambda e: e.tensor_tensor(out=tb, in0=x_re, in1=si, op=ALU.mult), r=kr, w=[tk])
        eng(lambda e: e.tensor_tensor(out=o_im, in0=ta, in1=tb, op=ALU.add), r=kr, w=list(keys_w) + [tk])
    expsE = ([7 - s_ for s_ in range(8)], [s_ for s_ in range(8)])
    expsF = ([i - 7 for i in range(8)], [-i for i in range(8)])
    expsV = ([i + 1 for i in range(8)], [8 - i for i in range(8)])
    Es = [T([128, 2, 16, 8, 16]) for _ in range(2)]
    Fs = [T([128, 2, 16, 8, 16]) for _ in range(2)]
    tS = T([128, 8, 2, 16, 16])
    tSv = T([128, 8, 2, 16, 16])
    PT = {nm: T([128, 2, 8, 32]) for nm in ("E", "F", "V")}
    Bb2 = T([128, 2, 32, 16]); Cc2 = T([128, 2, 32, 16])
    dmas = []
    for nm, exps in (("E", expsE), ("F", expsF), ("V", expsV)):
        for s_ in range(8):
            kA, kB = exps[0][s_] + 7, exps[1][s_] + 7
            dve(lambda e, nm=nm, s_=s_, kA=kA: e.tensor_copy(out=PT[nm][0:64, :, s_, :], in_=PW[:, :, kA, 0:32]), r=[KP, "PT"], w=["PT"])
            dmas.append((PT[nm][64:128, :, s_, :], PW[:, :, kB, 32:64]))
    ld(dmas, r=[KP], w=["PTb"])
    dve(lambda e: e.tensor_copy(out=Bb2[0:64], in_=Bb[:, :, 0:32, :]), r=["Bb"], w=["Bb2"])
    ld([(Bb2[64:128], Bb[:, :, 32:64, :])], r=["Bb"], w=["Bb2b"])
    ld([(Cc2[0:64], L["sc_d"][:, :, 0:32, :]), (Cc2[64:128], L["sc_d"][:, :, 32:64, :])], w=["Cc2"])

    def gen(dst, src, nm, keyd, keys, negim, b, scr=None, scrk="tS"):
        scr = tS if scr is None else scr
        gs = slice(b * 16, (b + 1) * 16)
        tab = PT[nm]
        for s_ in range(8):
            cmul16(dst[:, 0, :, s_, :], dst[:, 1, :, s_, :], tab[:, 0, s_, gs], tab[:, 1, s_, gs], src[:, 0, gs], src[:, 1, gs], 16,
                   ["PT", "PTb"] + list(keys), ["%s%d" % (keyd, s_)], scr[:, s_, 0], scr[:, s_, 1], tk="%s%d" % (scrk, s_))
            if negim:
                dve(lambda e, s_=s_: e.tensor_single_scalar(out=dst[:, 1, :, s_, :], in_=dst[:, 1, :, s_, :], scalar=-1.0, op=ALU.mult),
                    r=["%s%d" % (keyd, s_)], w=["%s%d" % (keyd, s_)])
    mi = T([128, 2, 128], I32); mf = T([128, 2, 128]); mask = T([128, 2, 128])
    pool(lambda e: e.iota(mi[:, 0, :], pattern=[[1, 128]], base=0, channel_multiplier=0), w=["mi"])
    pool(lambda e: e.iota(mi[:, 1, :], pattern=[[0, 128]], base=0, channel_multiplier=1), r=["mi"], w=["mi"])
    dve(lambda e: e.tensor_single_scalar(out=mi, in_=mi, scalar=4, op=ALU.arith_shift_right), r=["mi"], w=["mi"])
    dve(lambda e: e.tensor_copy(out=mf, in_=mi), r=["mi"], w=["mf"])
    dve(lambda e: e.tensor_tensor(out=mask[:, 0, :], in0=mf[:, 0, :], in1=mf[:, 1, :], op=ALU.is_ge), r=["mf"], w=["mask"])
    dve(lambda e: e.tensor_tensor(out=mask[:, 1, :], in0=mf[:, 1, :], in1=mf[:, 0, :], op=ALU.is_ge), r=["mf", "mask"], w=["mask"])
    rowm = T([128, 8])
    for a in range(8):
        dve(lambda e, a=a: e.tensor_single_scalar(out=rowm[:, a:a + 1], in_=mf[:, 1, 0:1], scalar=float(a), op=ALU.is_equal), r=["mf", "rowm"], w=["rowm"])
    for a in range(8):
        for b in range(8):
            dve(lambda e, a=a, b=b: e.tensor_scalar(out=G[:, a, b, :], in0=iota_f, scalar1=float(16 * (b - a)), scalar2=rowm[:, a:a + 1],
                                                   op0=ALU.is_equal, op1=ALU.mult), r=["iota_f", "rowm", "G"], w=["G"])
    def genA(b):
        gen(Es[b % 2], Bb2, "E", "E%d_" % (b % 2), ["Bb2", "Bb2b"], False, b)
        gen(Fs[b % 2], Cc2, "F", "F%d_" % (b % 2), ["Cc2"], True, b)

    def secB(b):
        par = b % 2
        E, F = Es[par], Fs[par]
        Wt, T0t, Vbt = Wq[0], T0q[0], Vbq[0]
        ek = ["E%d_%d" % (par, i_) for i_ in range(8)]
        fk = ["F%d_%d" % (par, i_) for i_ in range(8)]
        n_ = 0
        for d_ in range(2):
            pr_ = slice(64 * d_, 64 * d_ + 64)
            idn = identf[pr_, pr_]
            for gl in range(16):
                slot = d_ * 16 + gl
                psw = pf[n_ % 2]; pk = "pf%d" % (n_ % 2)
                pst = pf[2 + n_ % 2]; pk2 = "pf%d" % (2 + n_ % 2)
                n_ += 1

                def trw(e, gl=gl, psw=psw, pr_=pr_, idn=idn):
                    e.transpose(psw[:, 0:64], E[pr_, 0, gl].rearrange("p s h -> p (s h)"), idn)
                    return e.transpose(psw[:, 64:128], E[pr_, 1, gl].rearrange("p s h -> p (s h)"), idn)
                pe(trw, r=ek + [pk], w=[pk])
                act(lambda e, slot=slot, psw=psw: e.copy(out=Wt[:, slot, :], in_=psw[:, 0:128]), r=[pk, "Wq0"], w=["Wq0"])

                def mt0(e, gl=gl, pst=pst, pr_=pr_):
                    e.matmul(pst[:, 0:128], lhsT=E[pr_, 0, gl].rearrange("p s h -> p (s h)"), rhs=F[pr_, 0, gl].rearrange("p s h -> p (s h)"), start=True, stop=False)
                    return e.matmul(pst[:, 0:128], lhsT=E[pr_, 1, gl].rearrange("p s h -> p (s h)"), rhs=F[pr_, 1, gl].rearrange("p s h -> p (s h)"), start=False, stop=True)
                pe(mt0, r=ek + fk + [pk2], w=[pk2])
                dve(lambda e, slot=slot, d_=d_, pst=pst: e.tensor_tensor(out=T0t[:, slot, :], in0=pst[:, 0:128], in1=mask[:, d_, :], op=ALU.mult),
                    r=[pk2, "mask", "T0q0"], w=["T0q0"])
        gen(F, Cc2, "V", "F%d_" % par, ["Cc2"], True, b, tSv, "tSv")
        dve(lambda e: e.tensor_copy(out=Vbt, in_=F.rearrange("p r g i h -> p r g (i h)")), r=fk + ["Vbq0"], w=["Vbq0"])
        for d_ in range(2):
            qs = slice(d_ * 32 + 16 * b, d_ * 32 + 16 * b + 16)
            ld([(W_s[:, qs, :], Wt[:, d_ * 16:(d_ + 1) * 16, :])], r=["Wq0"], w=["W_s"], eng="pool")
            ld([(T0_s[:, qs, :], T0t[:, d_ * 16:(d_ + 1) * 16, :])], r=["T0q0"], w=["T0_s"], eng="pool")
            ld([(Vb_s[:, :, qs, :], Vbt[64 * d_:64 * d_ + 64])], r=["Vbq0"], w=["Vb_s"], eng="pool")

    genA(0)
    for b in range(2):
        lists = [P.capture(lambda: secB(b))]
        if b + 1 < 2:
            lists.append(P.capture(lambda: genA(b + 1)))
        P.merge(lists)
    ld([(dsk, L["sd_d"])], w=["dsk"])
    for hb__ in range(2):
        prs = []
        for lv in range(3):
            for ri in range(2):
                for d_ in range(2):
                    src = MU[:, lv, ri, d_ * 32:(d_ + 1) * 32].rearrange("p (j h q) -> p j h q", j=4, h=2, q=4)[:, :, hb__, :]
                    prs.append((MUS[64 * hb__:64 * hb__ + 64, lv, ri, d_, :, :], src))
        ld(prs, r=["MU"], w=["MUS%d" % hb__])
    P.flush()
    P.release()
    P.mark()

    NBT = 4
    uTj = T([128, SEQ], BF16)
    WJ = T([128, 16, 128], BF16); T0J = T([128, 16, 128], BF16); VJ = T([128, 2, 16, 128], BF16)
    Yb = T([128, 8, 256], BF16); ysb = T([128, OWN])
    y_s = nc.dram_tensor("y_s", [4, 128, OWN], F32, kind="Internal").ap()
    NP = 128

    class Scr:
        def __init__(self, tag):
            self.tag = tag
            self.zup = [T([NP, NBT, 2, 32]), T([NP, NBT, 2, 4])]
            self.pup = [T([NP, NBT, 2, 32]), T([NP, NBT, 2, 4])]
            self.accs = [T([NP, NBT, 2, 32]), T([NP, NBT, 2, 32])]
            self.tsc = T([NP, 4, NBT, 32])
            self.k = "scr" + tag
    class ZSet:
        def __init__(self, tag):
            self.t = tag
            self.Us = [T([128, NBT, 512], BF16), T([128, NBT, 512], BF16)]
            self.ZA = T([NP, NBT, 2, 256]); self.ZB = T([NP, NBT, 2, 256]); self.ZO = T([NP, NBT, 2, 256])
            self.NB = self.ZO
            self.PAb = self.ZA.bitcast(BF16)[:, :, :, 0:256]
            self.NBb = self.ZB.bitcast(BF16)[:, :, :, 0:256]
    zsets = [ZSet("e"), ZSet("o")]
    PA = T([NP, NBT, 2, 256])
    carry = T([NP, NBT, 2, 1])
    scrA, scrB = Scr("A"), Scr("B")

    def madd(sc, new, acc, z, mu_re, mu_im, m, keys):
        n_re, n_im = new; a_re, a_im = acc
        mr, mi_ = bc3(mu_re, m), bc3(mu_im, m)
        t1, t2, t3, t4 = (sc.tsc[:, i, :, 0:m] for i in range(4))
        kd, kp = sc.k + "d", sc.k + "p"
        kk = list(keys) + ["MUS0", "MUS1"]
        dve(lambda e: e.tensor_tensor(out=t1, in0=a_re, in1=mr, op=ALU.mult), r=kk + [kd], w=[kd])
        dve(lambda e: e.tensor_tensor(out=t2, in0=a_im, in1=mi_, op=ALU.mult), r=kk + [kd], w=[kd])
        dve(lambda e: e.tensor_tensor(out=t1, in0=t1, in1=t2, op=ALU.subtract), r=[kd], w=[kd])
        pool(lambda e: e.tensor_tensor(out=t3, in0=a_im, in1=mr, op=ALU.mult), r=kk + [kp], w=[kp])
        pool(lambda e: e.tensor_tensor(out=t4, in0=a_re, in1=mi_, op=ALU.mult), r=kk + [kp], w=[kp])
        pool(lambda e: e.tensor_tensor(out=t3, in0=t3, in1=t4, op=ALU.add), r=[kp], w=[kp])
        dve(lambda e: e.tensor_tensor(out=n_re, in0=t1, in1=z[0], op=ALU.add), r=[kd] + kk, w=kk[:-2])
        dve(lambda e: e.tensor_tensor(out=n_im, in0=t3, in1=z[1], op=ALU.add), r=[kp] + kk, w=kk[:-2])

    def vw(Zt, sl):
        return Zt[:, :, 0, sl], Zt[:, :, 1, sl]

    def blk(Zt, k, n):
        return (Zt[:, :, 0, 0:n].rearrange("p j (m k) -> p j k m", k=8)[:, :, k, :],
                Zt[:, :, 1, 0:n].rearrange("p j (m k) -> p j k m", k=8)[:, :, k, :])

    def mu_of(lv, dgs):
        d_, J_ = dgs
        return MUS[:, lv, 0, d_, J_, :], MUS[:, lv, 1, d_, J_, :]

    def horner(sc, Zt, n, lv, dgs, desc, out, keys):
        nb = n // 8
        order = list(range(7, -1, -1)) if desc else list(range(8))
        mr, mi_ = mu_of(lv, dgs)
        cur = blk(Zt, order[0], n)
        for ii, k in enumerate(order[1:]):
            dst = vw(out, slice(0, nb)) if ii == 6 else vw(sc.accs[ii % 2], slice(0, nb))
            madd(sc, dst, cur, blk(Zt, k, n), mr, mi_, nb, keys)
            cur = dst

    def seq_top(sc, Zt, n, lv, dgs, desc, init, out, keys, ikey=None):
        mr, mi_ = mu_of(lv, dgs)
        idx = list(range(n - 1, -1, -1)) if desc else list(range(n))
        if init is None:
            dve(lambda e: e.memset(out[:, :, :, idx[0]:idx[0] + 1], 0.0), r=keys, w=keys)
        else:
            dve(lambda e: e.tensor_copy(out=out[:, :, :, idx[0]:idx[0] + 1], in_=init), r=keys + [ikey], w=keys)
        for a, b in zip(idx[:-1], idx[1:]):
            madd(sc, vw(out, slice(b, b + 1)), vw(out, slice(a, a + 1)), vw(Zt, slice(a, a + 1)), mr, mi_, 1, keys)

    def exscan(sc, Zt, n, lv, dgs, desc, init, out, keys, ikey=None):
        if n <= 4:
            seq_top(sc, Zt, n, lv, dgs, desc, init, out, keys, ikey)
            return
        nb = n // 8
        zu, pu = sc.zup[lv], sc.pup[lv]
        horner(sc, Zt, n, lv, dgs, desc, zu, keys)
        exscan(sc, zu, nb, lv + 1, dgs, desc, init, pu, keys, ikey)
        order = list(range(7, -1, -1)) if desc else list(range(8))
        mr, mi_ = mu_of(lv, dgs)
        o0 = blk(out, order[0], n)
        dve(lambda e: e.tensor_copy(out=o0[0], in_=pu[:, :, 0, 0:nb]), r=keys, w=keys)
        pool(lambda e: e.tensor_copy(out=o0[1], in_=pu[:, :, 1, 0:nb]), r=keys, w=keys)
        for a, b in zip(order[:-1], order[1:]):
            madd(sc, blk(out, b, n), blk(out, a, n), blk(Zt, a, n), mr, mi_, nb, keys)

    GEL = 1.5957691216057308
    psA, pkA = pf[3], "pf3"
    psB, pkB = pf[7], "pf7"

    def pre(J):
        zs = zsets[J % 2]
        tg = zs.t
        ld([(uTj.rearrange("p (s c) -> p s c", s=8), L["uT2_s"][J * 128:(J + 1) * 128, :, :])], w=["uTj"])
        ld([(WJ[:, 0:8, :], W_s[:, 8 * J:8 * J + 8, :]), (WJ[:, 8:16, :], W_s[:, 32 + 8 * J:40 + 8 * J, :])], w=["WJ"])
        uview = uTj.rearrange("p (s c) -> p s c", s=8)
        for hb_ in range(2):
            U = zs.Us[hb_]
            ph = slice(64 * hb_, 64 * hb_ + 64)
            for jj in range(NBT):
                j = hb_ * NBT + jj
                uk = "U%s%d_%d" % (tg, hb_, jj)

                def msel(e, j=j):
                    ins = None
                    for s_ in range(8):
                        ins = e.matmul(psA, lhsT=G[:, j, s_, :], rhs=uview[:, s_, :], start=(s_ == 0), stop=(s_ == 7))
                    return ins
                pe(msel, r=["G", "uTj", pkA], w=[pkA])
                dve(lambda e, jj=jj, U=U: e.tensor_copy(out=U[:, jj, :], in_=psA), r=[pkA], w=[uk])
                for (Zt, zk, dl, c0) in ((zs.ZA, "ZA" + tg, j, 0), (zs.ZB, "ZB" + tg, 8 + j, 0), (zs.ZO, "ZO" + tg, 8 + j, 256)):
                    pkh = pkB

                    def mz(e, jj=jj, dl=dl, c0=c0, U=U, ph=ph):
                        e.matmul(psB[ph, 0:256], lhsT=WJ[:, dl, 0:64], rhs=U[:, jj, c0:c0 + 256], start=True, stop=True)
                        return e.matmul(psB[ph, 256:512], lhsT=WJ[:, dl, 64:128], rhs=U[:, jj, c0:c0 + 256], start=True, stop=True)
                    pe(mz, r=["WJ", uk, pkB], w=[pkB])
                    dve(lambda e, Zt=Zt, jj=jj, ph=ph: e.tensor_copy(out=Zt[ph, jj, :, :], in_=psB[ph, :].rearrange("p (r c) -> p r c", r=2)),
                        r=[pkh, zk], w=[zk])

    def scans(J):
        zs = zsets[J % 2]
        tg = zs.t
        dA = (0, J); dB = (1, J)

        def chainA():
            exscan(scrA, zs.ZA, 256, 0, dA, False, None, PA, ["ZA" + tg, "PA"], None)
            dve(lambda e: e.tensor_copy(out=zs.PAb, in_=PA), r=["PA", "ZA" + tg], w=["PAb" + tg, "ZA" + tg])

        def chainB():
            sc = scrB
            horner(sc, zs.ZO, 256, 0, dB, True, sc.zup[0], ["ZO" + tg])
            horner(sc, sc.zup[0], 32, 1, dB, True, sc.zup[1], ["ZO" + tg])
            mr2, mi2 = mu_of(2, dB)
            cur = vw(sc.zup[1], slice(3, 4))
            for n_ in (2, 1, 0):
                dst = vw(carry, slice(0, 1)) if n_ == 0 else vw(sc.accs[n_ % 2], slice(0, 1))
                madd(sc, dst, cur, vw(sc.zup[1], slice(n_, n_ + 1)), mr2, mi2, 1, ["ZO" + tg, "carry"])
                cur = dst
            exscan(sc, zs.ZB, 256, 0, dB, True, carry, zs.NB, ["ZB" + tg, "ZO" + tg, "carry"], "carry")
            pool(lambda e: e.tensor_copy(out=zs.NBb, in_=zs.NB), r=["ZO" + tg, "ZB" + tg], w=["NBb" + tg, "ZB" + tg])
        return [P.capture(chainA), P.capture(chainB)]

    def post(J):
        zs = zsets[J % 2]
        tg = zs.t
        ld([(T0J[:, 0:8, :], T0_s[:, 8 * J:8 * J + 8, :]), (T0J[:, 8:16, :], T0_s[:, 32 + 8 * J:40 + 8 * J, :])], w=["T0J"])
        ld([(VJ[hh * 64:hh * 64 + 64, :, 0:8, :], Vb_s[:, :, 8 * J:8 * J + 8, :]) for hh in range(2)]
           + [(VJ[hh * 64:hh * 64 + 64, :, 8:16, :], Vb_s[:, :, 32 + 8 * J:40 + 8 * J, :]) for hh in range(2)], w=["VJ"])
        for hb_ in range(2):
            U = zs.Us[hb_]
            ph = slice(64 * hb_, 64 * hb_ + 64)
            for jj in range(NBT):
                j = hb_ * NBT + jj
                uk = "U%s%d_%d" % (tg, hb_, jj)

                def my(e, j=j, jj=jj, U=U, ph=ph):
                    o = psA[:, 0:256]
                    e.matmul(o, lhsT=T0J[:, j, :], rhs=U[:, jj, 0:256], start=True, stop=False)
                    e.matmul(o, lhsT=T0J[:, 8 + j, :], rhs=U[:, jj, 0:256], start=False, stop=False)
                    e.matmul(o, lhsT=VJ[ph, 0, j, :], rhs=zs.PAb[ph, jj, 0, :], start=False, stop=False)
                    e.matmul(o, lhsT=VJ[ph, 1, j, :], rhs=zs.PAb[ph, jj, 1, :], start=False, stop=False)
                    e.matmul(o, lhsT=VJ[ph, 0, 8 + j, :], rhs=zs.NBb[ph, jj, 0, :], start=False, stop=False)
                    return e.matmul(o, lhsT=VJ[ph, 1, 8 + j, :], rhs=zs.NBb[ph, jj, 1, :], start=False, stop=True)
                pe(my, r=["T0J", "VJ", uk, "PAb" + tg, "NBb" + tg, "ZA" + tg, "ZB" + tg, pkA], w=[pkA])
                dve(lambda e, j=j: e.tensor_copy(out=Yb[:, j, :], in_=psA[:, 0:256]), r=[pkA, "Yb"], w=["Yb"])
        yv = ysb.rearrange("p (c i) -> p i c", i=8)
        for i in range(8):
            psd = psB[:, (i % 2) * 256:(i % 2) * 256 + 256]

            def md(e, i=i, psd=psd):
                ins = None
                for j in range(8):
                    ins = e.matmul(psd, lhsT=G[:, i, j, :], rhs=Yb[:, j, :], start=(j == 0), stop=(j == 7))
                return ins
            pe(md, r=["G", "Yb"], w=[pkB])
            dve(lambda e, i=i, psd=psd: e.tensor_copy(out=yv[:, i, :], in_=psd), r=[pkB, "ysb"], w=["ysb"])
        ld([(y_s[J], ysb)], r=["ysb"], w=["y_s%d" % J], eng="pool")
        if "ys5" in dbg_d:
            ld([(dbg_d["ys5"][J * 128:(J + 1) * 128, :], ysb)], r=["ysb"], w=["dbg_ys5"], eng="pool")

    def s5_main():
        pre(0)
        for J in range(4):
            lists = scans(J)

            def side(J=J):
                if J > 0:
                    post(J - 1)
                if J + 1 < 4:
                    pre(J + 1)
            lists.append(P.capture(side))
            P.merge(lists)
        post(3)

    def s5_tail():
      wg_bf = T([128, 4, 2 * D], BF16)
      ld([(wg_bf, L["w_glu_d"].rearrange("(k p) n -> p k n", p=128))], w=["wg_bf"], eng="pool")
      GY = T([128, 4, OWN], BF16)
      yb_ = [T([128, OWN])] * 2
      ub_ = [T([128, OWN], BF16), T([128, OWN], BF16)]
      g3 = T([128, OWN]); g4 = T([128, OWN])
      dsk2 = T([128, 4])
      ld([(dsk2, L["sd_d"])], w=["dsk2"])
      for J in range(4):
          yt, yk = yb_[J % 2], "ytl"
          ut, uk = ub_[J % 2], "utl%d" % (J % 2)
          ld([(yt, y_s[J])], r=["y_s%d" % J], w=[yk])
          ld([(ut, uT_s[J * 128:(J + 1) * 128, 0:OWN])], w=[uk])
          dve(lambda e, J=J, yt=yt, ut=ut: e.scalar_tensor_tensor(out=yt, in0=ut, scalar=dsk2[:, J:J + 1], in1=yt, op0=ALU.mult, op1=ALU.add),
              r=[yk, uk, "dsk2"], w=[yk])
          act(lambda e, yt=yt: e.activation(out=g3, in_=yt, func=AF.Square), r=[yk, "g3"], w=["g3"])
          dve(lambda e: e.tensor_scalar(out=g3, in0=g3, scalar1=0.044715, scalar2=1.0, op0=ALU.mult, op1=ALU.add), r=["g3"], w=["g3"])
          dve(lambda e, yt=yt: e.tensor_tensor(out=g3, in0=g3, in1=yt, op=ALU.mult), r=["g3", yk], w=["g3"])
          act(lambda e: e.activation(out=g4, in_=g3, func=AF.Sigmoid, scale=GEL), r=["g3", "g4"], w=["g4"])
          dve(lambda e, J=J, yt=yt: e.tensor_tensor(out=GY[:, J, :], in0=g4, in1=yt, op=ALU.mult), r=["g4", yk, "GY"], w=["GY"])
      sg = T([128, 512]); bo = [T([128, 512], BF16), T([128, 512], BF16)]
      n_ = 0
      for tb in range(4):
          tc_ = slice(tb * 512, (tb + 1) * 512)
          for dc in range(8):
              def mg(e, dc=dc, tc_=tc_):
                  ins = None
                  for (psx, c0) in ((pf[0], dc * 128), (pf[1], D + dc * 128)):
                      for k in range(4):
                          ins = e.matmul(psx, lhsT=wg_bf[:, k, c0:c0 + 128], rhs=GY[:, k, tc_], start=(k == 0), stop=(k == 3))
                  return ins
              pe(mg, r=["wg_bf", "GY", "pf0", "pf1"], w=["pf0", "pf1"])
              act(lambda e: e.activation(out=sg, in_=pf[1], func=AF.Sigmoid), r=["pf1"], w=["sg"])
              ob, ok = bo[n_ % 2], "bo%d" % (n_ % 2)
              n_ += 1
              dve(lambda e, ob=ob: e.tensor_tensor(out=ob, in0=pf[0], in1=sg, op=ALU.mult), r=["pf0", "sg"], w=[ok])
              ld([(brb_s[dc, :, tc_], ob)], r=[ok], w=["brb_s"], eng="pool")
      if "brb" in dbg_d:
          ld([(dbg_d["brb"], brb_s)], r=["brb_s"], w=["dbg_brb"], eng="pool")


    return s5_main, s5_tail, base_mark


def stage_attn(L):
    P, nc = L["P"], L["nc"]
    act, dve, pool, pe, ld = L["act"], L["dve"], L["pool"], L["pe"], L["ld"]
    pf, ones_f, dbg_d = L["pf"], L["ones_f"], L["dbg_d"]
    KT_s, V_s, QT_s = L["KT_s"], L["V_s"], L["QT_s"]
    bra_s = L["bra_s"] = nc.dram_tensor("bra_s", [8, 128, OWN], BF16, kind="Internal").ap()
    OT_s = nc.dram_tensor("OT_s", [64, 8, OWN], BF16, kind="Internal").ap()
    Vh = [P.sb([128, 32, 65], BF16, "Vh%d" % i) for i in range(2)]
    V_v = V_s.rearrange("(t p) (h d) -> p t h d", p=128, h=8)
    KTh = [P.sb([96, SEQ], BF16, "KTh%d" % i) for i in range(2)]
    QTh = [P.sb([96, OWN], BF16, "QTh%d" % i) for i in range(2)]
    PT3 = [P.sb([128, 512], BF16, "PT%d" % i) for i in range(3)]
    rcs = P.sb([128, 512], F32, "rcs")
    bcs = rcs[0:64, :]
    otb = [P.sb([64, 512], BF16, "otb%d" % i) for i in range(2)]
    sbank = [pf[0], pf[1], pf[6]]
    sbk = ["pf0", "pf1", "pf6"]
    steps = [(h, qb, kt) for h in range(NH) for qb in range(4) for kt in range(32)]
    LOOK = 2

    def emit_S(i):
        h, qb, kt = steps[i]
        kb, kk = KTh[h % 2], "KTh%d" % (h % 2)
        qb_, qk = QTh[h % 2], "QTh%d" % (h % 2)
        if qb == 0 and kt == 0:
            ld([(kb[:, 0:2048], KT_s[:, h, 0:2048]), (kb[:, 2048:], KT_s[:, h, 2048:])], w=[kk])
            ld([(qb_, QT_s[:, h, :])], w=[qk])
            ld([(Vh[h % 2][:, 0:16, :], V_v[:, 0:16, h, :]), (Vh[h % 2][:, 16:32, :], V_v[:, 16:32, h, :])], w=["Vh%d" % (h % 2)])
        qs = slice(qb * 512, (qb + 1) * 512)
        pss, pks = sbank[i % 3], sbk[i % 3]
        pe(lambda e: e.matmul(pss, lhsT=kb[:, kt * 128:(kt + 1) * 128], rhs=qb_[:, qs], start=True, stop=True), r=[kk, qk, pks], w=[pks])

    def emit_PV(i):
        h, qb, kt = steps[i]
        qs = slice(qb * 512, (qb + 1) * 512)
        pss, pks = sbank[i % 3], sbk[i % 3]
        pt, ptk = PT3[i % 3], "PT%d" % (i % 3)
        blk_i = i // 32
        acc, acck = (pf[2], "pf2") if blk_i % 2 == 0 else (pf[5], "pf5")
        act(lambda e: e.activation(out=pt, in_=pss, func=AF.Exp), r=[pks], w=[ptk])
        vb_, vk = Vh[h % 2], "Vh%d" % (h % 2)
        pe(lambda e: e.matmul(acc[0:65, :], lhsT=vb_[:, kt, :], rhs=pt, start=(kt == 0), stop=(kt == 31)),
           r=[vk, ptk, acck], w=[acck])
        if kt == 31:
            ob_, obk = otb[blk_i % 2], "otb%d" % (blk_i % 2)
            dve(lambda e: e.reciprocal(out=rcs[64:65, :], in_=acc[64:65, :]), r=[acck], w=["rcs"])

            def epilogue():
                pe(lambda e: e.matmul(pf[4][0:64, :], lhsT=ones_f[64:65, 0:64], rhs=rcs[64:65, :], start=True, stop=True), r=["rcs", "pf4"], w=["pf4"])
                dve(lambda e: e.tensor_copy(out=bcs, in_=pf[4][0:64, :]), r=["pf4"], w=["bcs"])
                dve(lambda e: e.tensor_tensor(out=ob_, in0=acc[0:64, :], in1=bcs, op=ALU.mult), r=[acck, "bcs", obk], w=[obk])
                ld([(OT_s[:, h, qs], ob_)], r=[obk], w=["OT_s"], eng="sp")
            pending.append((i + EPI_DELAY, epilogue))

    pending = []
    EPI_DELAY = 8

    def attn_core():
        for i in range(len(steps) + LOOK):
            if i < len(steps):
                emit_S(i)
            if i >= LOOK:
                emit_PV(i - LOOK)
            while pending and pending[0][0] <= i - LOOK:
                pending.pop(0)[1]()
        while pending:
            pending.pop(0)[1]()

    def attn_proj():
        OT = P.sb([64, 8, OWN], BF16, "OT")
        wo_bf = P.sb([64, 8, D], BF16, "wo_bf")
        ob = [P.sb([128, 512], BF16, "aob%d" % i) for i in range(2)]
        ld([(OT, OT_s)], r=["OT_s"], w=["OT"])
        ld([(wo_bf, L["w_o_d"].rearrange("(h p) n -> p h n", p=64))], w=["wo_bf"], eng="pool")
        n_ = 0
        for tb in range(4):
            ts_ = slice(tb * 512, (tb + 1) * 512)
            for j in range(8):
                psx, pk = pf[2 + n_ % 2], "pf%d" % (2 + n_ % 2)
                o_, okk = ob[n_ % 2], "aob%d" % (n_ % 2)
                n_ += 1

                def mo(e, j=j, ts_=ts_, psx=psx):
                    ins = None
                    for h in range(8):
                        ins = e.matmul(psx, lhsT=wo_bf[:, h, j * 128:(j + 1) * 128], rhs=OT[:, h, ts_], start=(h == 0), stop=(h == 7))
                    return ins
                pe(mo, r=["wo_bf", "OT", pk], w=[pk])
                act(lambda e, psx=psx, o_=o_: e.copy(out=o_, in_=psx), r=[pk], w=[okk])
                ld([(bra_s[j, :, ts_], o_)], r=[okk], w=["bra_s"], eng="pool")
        if "bra" in dbg_d:
            ld([(dbg_d["bra"], bra_s)], r=["bra_s"], w=["dbg_bra"], eng="pool")

    return attn_core, attn_proj


def stage_ffn(L):
    P, nc = L["P"], L["nc"]
    act, dve, pool, pe, ld = L["act"], L["dve"], L["pool"], L["pe"], L["ld"]
    pf, pb, ident, dbg_d = L["pf"], L["pb"], L["ident"], L["dbg_d"]
    gate_s, bra_s, brb_s, x_d, out_d = L["gate_s"], L["bra_s"], L["brb_s"], L["x_d"], L["out_d"]
    x1_s = nc.dram_tensor("x1_s", [OWN, D], F32, kind="Internal").ap()
    base_mark = P.sb_mark
    AB, wm_bf, w2_bf = L["ABf0"], L["wm_bf0"], L["w2_bf0"]
    P.skip([128, 4, D], F32); P.skip([128, 8, D], BF16); P.skip([128, 22, D], BF16)
    w1c = [P.sb([128, 8, 256], BF16, "w1c%d" % i) for i in range(3)]
    gin = [P.sb([128, 4, 512], BF16, "gin%d" % i) for i in range(2)]
    mt = P.sb([128, 2, 512], F32, "mt")
    mT = P.sb([128, 8, 512], BF16, "mT")
    mx = P.sb([128, D], F32, "mx"); sq = P.sb([128, D], F32, "fsq"); tmp = P.sb([128, D], F32, "ftmp")
    xt = P.sb([128, D], F32, "fxt"); x1t = P.sb([128, D], F32, "x1t")
    fhbs = [P.sb([128, D], BF16, "fhb%d" % i) for i in range(2)]
    st = P.sb([128, 16], F32, "fst")
    mx2 = P.sb([128, D], F32, "mx2"); sq2 = P.sb([128, D], F32, "fsq2"); tmp2 = P.sb([128, D], F32, "ftmp2")
    xt2 = P.sb([128, D], F32, "fxt2"); st2 = P.sb([128, 16], F32, "fst2")
    h2Ts = [P.sb([128, 8, 512], BF16, "h2T%d" % i) for i in range(2)]
    aT = P.sb([128, 22, 512], BF16, "aT")
    sgls = [P.sb([128, 512], F32, "sgl%d" % i) for i in range(2)]
    w1v = L["w_f1_d"].rearrange("(k p) n -> p k n", p=128)
    nw = [0]

    def chainF(tb):
        ts_ = slice(tb * 512, (tb + 1) * 512)
        h2T, hk = h2Ts[tb % 2], "h2T%d" % (tb % 2)
        for j in range(8):
            gb_, gk = gin[j % 2], "gin%d" % (j % 2)
            ld([(gb_[:, 0, :], gate_s[j, :, ts_]), (gb_[:, 1, :], gate_s[8 + j, :, ts_]),
                (gb_[:, 2, :], bra_s[j, :, ts_]), (gb_[:, 3, :], brb_s[j, :, ts_])], r=["gate_s", "bra_s", "brb_s"], w=[gk])
            dve(lambda e, gb_=gb_: e.tensor_tensor(out=mt[:, 0, :], in0=gb_[:, 0, :], in1=gb_[:, 2, :], op=ALU.mult), r=[gk, "mt"], w=["mt"])
            dve(lambda e, gb_=gb_: e.tensor_tensor(out=mt[:, 1, :], in0=gb_[:, 1, :], in1=gb_[:, 3, :], op=ALU.mult), r=[gk, "mt"], w=["mt"])
            dve(lambda e, j=j: e.tensor_tensor(out=mT[:, j, :], in0=mt[:, 0, :], in1=mt[:, 1, :], op=ALU.add), r=["mt", "mT"], w=["mT"])
        def partA(tt):
            t = tb * 4 + tt
            tk = slice(tt * 128, (tt + 1) * 128)
            hbt, hbk = fhbs[tt % 2], "fhb%d" % (tt % 2)
            ld([(xt, x_d[t * 128:(t + 1) * 128, :])], w=["fxt"])
            for hh in range(2):
                def mmx(e, tk=tk, hh=hh):
                    ins = None
                    for k in range(8):
                        ins = e.matmul(pf[7], lhsT=mT[:, k, tk], rhs=wm_bf[:, k, hh * 512:(hh + 1) * 512], start=(k == 0), stop=(k == 7))
                    return ins
                pe(mmx, r=["mT", "wm_bf", "pf7"], w=["pf7"])
                act(lambda e, hh=hh: e.copy(out=mx[:, hh * 512:(hh + 1) * 512], in_=pf[7]), r=["pf7", "mx"], w=["mx"])
            rms_ops(P, L, mx, D, st[:, 0:1], st[:, 1:2], st[:, 2:3], sq, "mx", "m", "fsq")
            dve(lambda e: e.scalar_tensor_tensor(out=tmp, in0=mx, scalar=st[:, 2:3], in1=AB[:, 0, :], op0=ALU.mult, op1=ALU.mult),
                r=["mx", "rstdm", "AB", "ftmp"], w=["ftmp"])
            dve(lambda e: e.tensor_tensor(out=x1t, in0=tmp, in1=xt, op=ALU.add), r=["ftmp", "fxt", "x1t"], w=["x1t"])
            ld([(x1_s[t * 128:(t + 1) * 128, :], x1t)], r=["x1t"], w=["x1_s%d" % t], eng="pool")
            rms_ops(P, L, x1t, D, st[:, 4:5], st[:, 5:6], st[:, 6:7], sq, "x1t", "h2", "fsq")
            dve(lambda e: e.scalar_tensor_tensor(out=tmp, in0=x1t, scalar=st[:, 6:7], in1=AB[:, 1, :], op0=ALU.mult, op1=ALU.mult),
                r=["x1t", "rstdh2", "AB", "ftmp"], w=["ftmp"])
            dve(lambda e, hbt=hbt: e.tensor_tensor(out=hbt, in0=tmp, in1=AB[:, 2, :], op=ALU.add), r=["ftmp", "AB"], w=[hbk])

        def partB(tt):
            hbt, hbk = fhbs[tt % 2], "fhb%d" % (tt % 2)

            def tr(e, hbt=hbt):
                ins = None
                for k in range(8):
                    ins = e.transpose(pb[0][:, k * 128:(k + 1) * 128], hbt[:, k * 128:(k + 1) * 128], ident)
                return ins
            pe(tr, r=[hbk, "ident", "pb0"], w=["pb0"])
            act(lambda e, tt=tt, h2T=h2T: e.copy(out=h2T[:, :, tt * 128:(tt + 1) * 128], in_=pb[0].rearrange("p (k t) -> p k t", k=8)),
                r=["pb0", hk], w=[hk])
        for tt in range(5):
            if tt < 4:
                partA(tt)
            if tt >= 1:
                partB(tt - 1)

    def ffnF(tb):
        h2T, hk = h2Ts[tb % 2], "h2T%d" % (tb % 2)
        for hc in range(22):
            wc, wk = w1c[nw[0] % 3], "w1c%d" % (nw[0] % 3)
            nw[0] += 1
            ld([(wc, L["w1b_s"][hc])], w=[wk])
            pa, pbk = (2, 3) if hc % 2 == 0 else (4, 5)
            sgl, sgk = sgls[hc % 2], "sgl%d" % (hc % 2)

            def mf(e, wc=wc, pa=pa, pbk=pbk):
                ins = None
                for (psx, c0) in ((pf[pa], 0), (pf[pbk], 128)):
                    for k in range(8):
                        ins = e.matmul(psx, lhsT=wc[:, k, c0:c0 + 128], rhs=h2T[:, k, :], start=(k == 0), stop=(k == 7))
                return ins
            pe(mf, r=[wk, hk, "pf%d" % pa, "pf%d" % pbk], w=["pf%d" % pa, "pf%d" % pbk])
            act(lambda e, pa=pa, sgl=sgl: e.activation(out=sgl, in_=pf[pa], func=AF.Silu), r=["pf%d" % pa, sgk], w=[sgk])
            dve(lambda e, hc=hc, pbk=pbk, sgl=sgl: e.tensor_tensor(out=aT[:, hc, :], in0=pf[pbk], in1=sgl, op=ALU.mult),
                r=["pf%d" % pbk, sgk], w=["aT%d" % hc])
        for tt in range(4):
            t = tb * 4 + tt
            tk = slice(tt * 128, (tt + 1) * 128)
            ld([(xt2, x1_s[t * 128:(t + 1) * 128, :])], r=["x1_s%d" % t], w=["fxt2"])

            def mo(e, tk=tk):
                ins = None
                for hh in range(2):
                    for k in range(22):
                        ins = e.matmul(pf[hh], lhsT=aT[:, k, tk], rhs=w2_bf[:, k, hh * 512:(hh + 1) * 512], start=(k == 0), stop=(k == 21))
                return ins
            pe(mo, r=["aT%d" % k for k in range(22)] + ["w2_bf", "pf0", "pf1"], w=["pf0", "pf1"])
            act(lambda e: e.copy(out=mx2[:, 0:512], in_=pf[0]), r=["pf0", "mx2"], w=["mx2"])
            act(lambda e: e.copy(out=mx2[:, 512:], in_=pf[1]), r=["pf1", "mx2"], w=["mx2"])
            rms_ops(P, L, mx2, D, st2[:, 8:9], st2[:, 9:10], st2[:, 10:11], sq2, "mx2", "f", "fsq2")
            dve(lambda e: e.scalar_tensor_tensor(out=tmp2, in0=mx2, scalar=st2[:, 10:11], in1=AB[:, 3, :], op0=ALU.mult, op1=ALU.mult),
                r=["mx2", "rstdf", "AB", "ftmp2"], w=["ftmp2"])
            dve(lambda e: e.tensor_tensor(out=tmp2, in0=tmp2, in1=xt2, op=ALU.add), r=["ftmp2", "fxt2"], w=["ftmp2"])
            ld([(out_d[t * 128:(t + 1) * 128, :], tmp2)], r=["ftmp2"], w=["out_d"], eng="pool")

    P.merge([P.capture(lambda: chainF(0))])
    for tb in range(4):
        lists = [P.capture(lambda: ffnF(tb))]
        if tb + 1 < 4:
            lists.append(P.capture(lambda: chainF(tb + 1)))
        P.merge(lists)
    P.flush()
    P.sb_off = P.sb_mark = base_mark


def prep_core(inp, core):
    b, hf = core // 2, core % 2
    rev = (hf == 1)
    f = lambda a: np.ascontiguousarray(a, dtype=np.float32)
    x = inp["x"][b]
    pos = inp["positions"][b]
    if rev:
        x = x[::-1]
        pos = pos[::-1]
    m = {}
    m["x_loc"] = f(x)
    m["pos_loc"] = np.ascontiguousarray(np.asarray(pos, dtype=np.int32).reshape(32, 128).T)
    m["c_col"] = f(inp["c"][b].reshape(8, 128).T)
    m["w_ada"] = f(inp["w_ada"][0])
    m["b_ada"] = f(inp["b_ada"][0].reshape(1, -1))
    m["gvec"] = f(np.concatenate([inp["g_pre_mix"][0], inp["g_post_mix"][0], inp["g_pre_ffn"][0], inp["g_post_ffn"][0]]).reshape(1, -1))
    m["w_in"] = f(inp["w_in"][0])
    m["g_q"] = f(inp["g_q_norm"][0].reshape(1, -1))
    m["g_kv"] = f(inp["g_kv_norm"][0].reshape(1, -1))
    m["w_uq"] = f(inp["w_uq"][0])
    m["w_uk"] = f(inp["w_uk"][0])
    m["w_uv"] = f(inp["w_uv"][0])
    m["w_o"] = f(inp["w_attn_out"][0])
    order = [1, 0] if rev else [0, 1]
    def dg(a):
        a = np.asarray(a)[order]
        return a.reshape((64,) + a.shape[2:])
    lre = dg(inp["ssm_lambda_re"][0]); lim = dg(inp["ssm_lambda_im"][0])
    m["lam"] = f(np.stack([lre.T, lim.T], axis=1))
    m["ldt"] = f(np.broadcast_to(dg(inp["ssm_log_dt"][0]).reshape(1, 64), (64, 64)))
    bre = dg(inp["ssm_b_re"][0]); bim = dg(inp["ssm_b_im"][0])
    m["ssm_b"] = f(np.stack([bre.transpose(1, 0, 2), bim.transpose(1, 0, 2)], axis=1))
    cre = dg(inp["ssm_c_re"][0]); cim = dg(inp["ssm_c_im"][0])
    m["ssm_c"] = f(np.stack([cre.transpose(2, 0, 1), cim.transpose(2, 0, 1)], axis=1))
    m["ssm_d"] = f(inp["ssm_d"][0].reshape(4, 128).T)
    m["w_glu"] = f(inp["w_glu"][0])
    m["w_mix"] = f(inp["w_mix_out"][0])
    m["w_f1"] = f(inp["w_ffn_in"][0])
    m["w_f2"] = f(inp["w_ffn_out"][0])
    return m


def kernel(**inputs):
    inputs = {k: np.asarray(v) for k, v in inputs.items()}
    nc = bass.Bass("TRN2", target_bir_lowering=False)
    build(nc)
    in_maps = [prep_core(inputs, c) for c in range(8)]
    res = run_bass_kernel_spmd(nc, in_maps, core_ids=list(range(8)))
    out = np.zeros((4, SEQ, D), np.float32)
    for c in range(8):
        b, hf = c // 2, c % 2
        o = np.asarray(res.results[c]["out_loc"], dtype=np.float32)
        if hf == 0:
            out[b, :OWN] = o
        else:
            out[b, OWN:] = o[::-1]
    return out
```

```python
import math
from contextlib import ExitStack
import numpy as np
import concourse.bass as bass
import concourse.mybir as mybir
from concourse.bass_utils import run_bass_kernel_spmd

F32 = mybir.dt.float32
BF16 = mybir.dt.bfloat16
I32 = mybir.dt.int32
AF = mybir.ActivationFunctionType
ALU = mybir.AluOpType
AX = mybir.AxisListType

D = 1024
SEQ = 4096
OWN = 2048
NH = 8
QLR = 384
KVR = 256
ROPE = 32
SSMC = 512
NG = 32
PST = 64
FFH = 2816
EPS = 1e-6
IN_COLS = 3232
DEBUG = {}


class Op:
    __slots__ = ("eng", "fn", "reads", "writes", "dma", "deps", "needed", "token", "grp")

    def __init__(self, eng, fn, reads, writes, dma, grp=None):
        self.eng, self.fn, self.reads, self.writes, self.dma = eng, fn, reads, writes, dma
        self.grp = grp
        self.deps = ()
        self.needed = False
        self.token = None


class Prog:
    ENGS = ("pe", "act", "dve", "pool", "sp")

    def __init__(self, nc, es):
        self.nc, self.es = nc, es
        self.ops = []
        self.psem = {e: es.enter_context(nc.semaphore("ps_" + e)) for e in self.ENGS}
        self.pcnt = {e: 0 for e in self.ENGS}
        self.dsem = {}
        self.dcnt = {}
        self.waited = {e: {} for e in self.ENGS}
        self.sb_off = 16448
        self.sb_mark = 16448
        self.nalloc = 0

    def sb(self, shape, dt, name=None):
        nbytes = int(np.prod(shape[1:])) * (4 if dt in (F32, I32) else 2)
        nbytes = (nbytes + 63) // 64 * 64
        off = self.sb_off
        self.sb_off += nbytes
        assert self.sb_off <= 229000, ("SBUF overflow", self.sb_off, name)
        self.nalloc += 1
        t = self.nc.alloc_sbuf_tensor_at("%s_%d" % (name or "t", self.nalloc), list(shape), dt, offset=off)
        return t.ap()

    def skip(self, shape, dt):
        nbytes = int(np.prod(shape[1:])) * (4 if dt in (F32, I32) else 2)
        self.sb_off += (nbytes + 63) // 64 * 64
        assert self.sb_off <= 229000, ("SBUF overflow", self.sb_off)

    def mark(self):
        self.sb_mark = self.sb_off

    def release(self):
        self.sb_off = self.sb_mark

    grp = None

    def op(self, eng, fn, r=(), w=()):
        self.ops.append(Op(eng, fn, tuple(r), tuple(w), False, self.grp))

    def dma(self, eng, fn, r=(), w=()):
        self.ops.append(Op(eng, fn, tuple(r), tuple(w), True, self.grp))

    def capture(self, fn):
        saved, self.ops = self.ops, []
        fn()
        out, self.ops = self.ops, saved
        return out

    def merge(self, lists):
        lists = [l for l in lists if l]
        pos = [0] * len(lists)
        total = sum(len(l) for l in lists)
        while sum(pos) < total:
            best, bi = None, -1
            for i, l in enumerate(lists):
                if pos[i] < len(l):
                    frac = pos[i] / len(l)
                    if best is None or frac < best:
                        best, bi = frac, i
            l = lists[bi]
            g = l[pos[bi]].grp
            self.ops.append(l[pos[bi]])
            pos[bi] += 1
            while g is not None and pos[bi] < len(l) and l[pos[bi]].grp == g:
                self.ops.append(l[pos[bi]])
                pos[bi] += 1

    def flush(self):
        ops, self.ops = self.ops, []
        lastw, readers = {}, {}
        for i, o in enumerate(ops):
            deps = set()
            for k in o.reads:
                if k in lastw:
                    deps.add(lastw[k])
            for k in o.writes:
                if k in lastw:
                    deps.add(lastw[k])
                deps |= readers.get(k, set())
            deps.discard(i)
            o.deps = sorted(deps)
            for d in deps:
                ops[d].needed = True
            for k in o.reads:
                readers.setdefault(k, set()).add(i)
            for k in o.writes:
                lastw[k] = i
                readers[k] = set()
        last = {}
        for i, o in enumerate(ops):
            if not o.dma:
                last[o.eng] = i
        for i in last.values():
            ops[i].needed = True
        used_d = set()
        for o in ops:
            if o.dma:
                key = o.writes[0]
                if key not in self.dsem:
                    self.dsem[key] = self.es.enter_context(self.nc.semaphore("ds%d" % len(self.dsem)))
                    self.dcnt[key] = 0
                o.token = [self.dsem[key], None, key]
                used_d.add(key)
            elif o.needed:
                self.pcnt[o.eng] += 1
                o.token = (self.psem[o.eng], self.pcnt[o.eng])
        for o in ops:
            if o.dma:
                self.dcnt[o.token[2]] += 16 * o.fn.ndma
                o.token = (o.token[0], self.dcnt[o.token[2]])
        finals = {e: self.pcnt[e] for e in self.ENGS}
        dfinals = [(self.dsem[k], self.dcnt[k]) for k in sorted(used_d)]
        nc = self.nc
        prog = self

        def emit(ename, e):
            wt = prog.waited[ename]

            def wait(sem, val):
                if wt.get(sem.num, 0) < val:
                    e.wait_ge(sem, val)
                    wt[sem.num] = val

            for o in ops:
                if o.eng != ename:
                    continue
                need = {}
                for d in o.deps:
                    sem, val = ops[d].token
                    if need.get(sem.num, (None, 0))[1] < val:
                        need[sem.num] = (sem, val)
                for sem, val in need.values():
                    wait(sem, val)
                res = o.fn(e)
                if o.dma:
                    for ins in res:
                        ins.then_inc(o.token[0], 16)
                elif o.needed:
                    res.then_inc(o.token[0], 1)
            for e2 in prog.ENGS:
                if e2 != ename and finals[e2] > 0:
                    wait(prog.psem[e2], finals[e2])
            for sem, val in dfinals:
                wait(sem, val)

        with nc.Block() as block:
            @block.tensor
            def _(e):
                emit("pe", e)

            @block.scalar
            def _(e):
                emit("act", e)

            @block.vector
            def _(e):
                emit("dve", e)

            @block.gpsimd
            def _(e):
                emit("pool", e)

            @block.sync
            def _(e):
                emit("sp", e)


class DmaFn:
    def __init__(self, pairs, cast=False):
        self.pairs = pairs
        self.ndma = len(pairs)

    def __call__(self, e):
        return [e.dma_start(out=o, in_=i) for (o, i) in self.pairs]


def build(nc, dbg=()):
    es = ExitStack()
    P = Prog(nc, es)

    def din(name, shape, dt=F32):
        return nc.dram_tensor(name, list(shape), dt, kind="ExternalInput").ap()

    def dscr0(name, shape, dt):
        return nc.dram_tensor(name, list(shape), dt, kind="Internal").ap()

    x_d = din("x_loc", [SEQ, D])
    pos_d = din("pos_loc", [128, 32], I32)
    c_d = din("c_col", [128, 8])
    w_ada_d = din("w_ada", [D, 6 * D])
    b_ada_d = din("b_ada", [1, 6 * D])
    gvec_d = din("gvec", [1, 4 * D])
    w_in_d = din("w_in", [D, IN_COLS])
    gq_d = din("g_q", [1, QLR])
    gkv_d = din("g_kv", [1, KVR])
    w_uq_d = din("w_uq", [QLR, 768])
    w_uk_d = din("w_uk", [KVR, 512])
    w_uv_d = din("w_uv", [KVR, 512])
    w_o_d = din("w_o", [512, D])
    lam_d = din("lam", [64, 2, 64])
    ldt_d = din("ldt", [64, 64])
    sb_d = din("ssm_b", [64, 2, 64, 16])
    sc_d = din("ssm_c", [64, 2, 64, 16])
    sd_d = din("ssm_d", [128, 4])
    w_glu_d = din("w_glu", [SSMC, 2 * D])
    w_mix_d = din("w_mix", [D, D])
    w_f1_d = din("w_f1", [D, 2 * FFH])
    w_f2_d = din("w_f2", [FFH, D])
    out_d = nc.dram_tensor("out_loc", [OWN, D], F32, kind="ExternalOutput").ap()
    dbg_d = {}
    for nm, shp in dbg:
        dbg_d[nm] = nc.dram_tensor("dbg_" + nm, list(shp), F32, kind="ExternalOutput").ap()

    def dscr(name, shape, dt):
        return nc.dram_tensor(name, list(shape), dt, kind="Internal").ap()

    gate_s = dscr("gate_s", [16, 128, OWN], BF16)
    brb_s = dscr("brb_s", [8, 128, OWN], BF16)

    ident = P.sb([128, 128], BF16, "ident")
    identf = P.sb([128, 128], F32, "identf")
    ones_f = P.sb([128, 128], F32, "ones_f")
    AB_s = dscr0("AB_s", [128, 6, D], F32)
    cs_tab = P.sb([128, 32, 2, 16], F32, "cs_tab")
    iota_f_keep = P.sb([128, 128], F32, "iota_f_keep")
    eps_t = P.sb([128, 1], F32, "eps_t")
    P.mark()
    pf = [nc.alloc_psum_tensor("pf%d" % i, [128, 512], F32).ap() for i in range(8)]
    pb = [pf[6].bitcast(BF16), pf[7].bitcast(BF16)]
    ps_ada = pf[0]

    def act(fn, r=(), w=()):
        P.op("act", fn, r, w)

    def dve(fn, r=(), w=()):
        P.op("dve", fn, r, w)

    def pool(fn, r=(), w=()):
        P.op("pool", fn, r, w)

    def pe(fn, r=(), w=()):
        P.op("pe", fn, r, w)

    def ld(pairs, r=(), w=(), eng="sp"):
        P.dma(eng, DmaFn(pairs), r, w)

    w_in_bf0 = P.sb([128, 8, IN_COLS], BF16, "w_in_bf")
    w_uq_bf0 = P.sb([128, 3, 768], BF16, "w_uq_bf")
    w_uk_bf0 = P.sb([128, 2, 512], BF16, "w_uk_bf")
    w_uv_bf0 = P.sb([128, 2, 512], BF16, "w_uv_bf")
    w_in_v0 = w_in_d.rearrange("(k p) n -> p k n", p=128)
    for k0 in range(0, 8, 2):
        ld([(w_in_bf0[:, k0:k0 + 2, :], w_in_v0[:, k0:k0 + 2, :])], w=["w_in_bf%d" % k0], eng="pool")
    ld([(w_uq_bf0, w_uq_d.rearrange("(k p) n -> p k n", p=128))], w=["w_uq_bf"], eng="pool")
    ld([(w_uk_bf0, w_uk_d.rearrange("(k p) n -> p k n", p=128))], w=["w_uk_bf"], eng="pool")
    ld([(w_uv_bf0, w_uv_d.rearrange("(k p) n -> p k n", p=128))], w=["w_uv_bf"], eng="pool")
    AB = P.sb([128, 6, D], F32, "AB")
    iota_i = P.sb([128, 128], I32, "iota_i")
    iota_f = iota_f_keep
    pool(lambda e: e.iota(iota_i, pattern=[[1, 128]], base=0, channel_multiplier=-1), w=["iota_i"])
    dve(lambda e: e.tensor_copy(out=iota_f, in_=iota_i), r=["iota_i"], w=["iota_f"])
    dve(lambda e: e.tensor_single_scalar(out=identf, in_=iota_f, scalar=0.0, op=ALU.is_equal), r=["iota_f"], w=["identf"])
    dve(lambda e: e.tensor_copy(out=ident, in_=identf), r=["identf"], w=["ident"])
    dve(lambda e: e.memset(ones_f, 1.0), w=["ones_f"])
    dve(lambda e: e.memset(eps_t, EPS), w=["eps_t"])

    c_sb = P.sb([128, 8], F32, "c_sb")
    sc_sb = P.sb([128, 8], F32, "sc_sb")
    ld([(c_sb, c_d)], w=["c_sb"])
    act(lambda e: e.activation(out=sc_sb, in_=c_sb, func=AF.Silu), r=["c_sb"], w=["sc_sb"])
    ada_row = P.sb([1, 6 * D], F32, "ada_row")
    bada = P.sb([1, 6 * D], F32, "bada")
    gv = P.sb([1, 4 * D], F32, "gv")
    ld([(bada, b_ada_d)], w=["bada"])
    ld([(gv, gvec_d)], w=["gv"])
    wst = [P.sb([128, 8, 512], F32, "wst%d" % i) for i in range(2)]
    w_ada_v = w_ada_d.rearrange("(k p) n -> p k n", p=128)
    for ct in range(12):
        wb = wst[ct % 2]
        key = "wst%d" % (ct % 2)
        ld([(wb[:, 0:4, :], w_ada_v[:, 0:4, ct * 512:(ct + 1) * 512]),
            (wb[:, 4:8, :], w_ada_v[:, 4:8, ct * 512:(ct + 1) * 512])], w=[key])
        ps = ps_ada

        def mm(e, wb=wb, ps=ps):
            ins = None
            for k in range(8):
                ins = e.matmul(ps[0:1, :], lhsT=sc_sb[:, k:k + 1], rhs=wb[:, k, :], start=(k == 0), stop=(k == 7))
            return ins
        pe(mm, r=[key, "sc_sb"], w=["ps_ada"])
        dve(lambda e, ct=ct, ps=ps: e.tensor_tensor(out=ada_row[0:1, ct * 512:(ct + 1) * 512], in0=ps[0:1, :],
                                                  in1=bada[0:1, ct * 512:(ct + 1) * 512], op=ALU.add),
            r=["ps_ada", "bada"], w=["ada_row"])
    rows = bada.rearrange("p (k n) -> p k n", k=6)

    def seg(k):
        return ada_row[0:1, k * D:(k + 1) * D]

    def gseg(k):
        return gv[0:1, k * D:(k + 1) * D]
    dve(lambda e: e.scalar_tensor_tensor(out=rows[0:1, 0, :], in0=seg(1), scalar=1.0, in1=gseg(0), op0=ALU.add, op1=ALU.mult),
        r=["ada_row", "gv"], w=["rows"])
    dve(lambda e: e.tensor_copy(out=rows[0:1, 1, :], in_=seg(0)), r=["ada_row", "rows"], w=["rows"])
    dve(lambda e: e.tensor_tensor(out=rows[0:1, 2, :], in0=seg(2), in1=gseg(1), op=ALU.mult), r=["ada_row", "gv", "rows"], w=["rows"])
    dve(lambda e: e.scalar_tensor_tensor(out=rows[0:1, 3, :], in0=seg(4), scalar=1.0, in1=gseg(2), op0=ALU.add, op1=ALU.mult),
        r=["ada_row", "gv", "rows"], w=["rows"])
    dve(lambda e: e.tensor_copy(out=rows[0:1, 4, :], in_=seg(3)), r=["ada_row", "rows"], w=["rows"])
    dve(lambda e: e.tensor_tensor(out=rows[0:1, 5, :], in0=seg(5), in1=gseg(3), op=ALU.mult), r=["ada_row", "gv", "rows"], w=["rows"])
    for k in range(6):
        for hh in range(2):
            pe(lambda e, k=k, hh=hh: e.matmul(ps_ada[:, :], lhsT=ones_f[0:1, :], rhs=rows[0:1, k, hh * 512:(hh + 1) * 512],
                                             start=True, stop=True), r=["rows", "ones_f", "ps_ada"], w=["ps_ada"])
            act(lambda e, k=k, hh=hh: e.copy(out=AB[:, k, hh * 512:(hh + 1) * 512], in_=ps_ada[:, :]), r=["ps_ada"], w=["AB"])

    pos_i = P.sb([128, 32], I32, "pos_i")
    pos_f = P.sb([128, 32], F32, "pos_f")
    ld([(pos_i, pos_d)], w=["pos_i"])
    dve(lambda e: e.tensor_copy(out=pos_f, in_=pos_i), r=["pos_i"], w=["pos_f"])
    invf = P.sb([128, 16], F32, "invf")
    for j in range(16):
        val = float(np.float32(10000.0) ** np.float32(-(2.0 * j) / 32.0)) / (2.0 * math.pi)
        dve(lambda e, j=j, val=val: e.memset(invf[:, j:j + 1], val), r=["invf"], w=["invf"])
    turns = P.sb([128, 32, 2, 16], F32, "turns")
    tint = P.sb([128, 32, 2, 16], I32, "tint")
    tfl = P.sb([128, 32, 2, 16], F32, "tfl")
    for t in range(32):
        dve(lambda e, t=t: e.tensor_scalar(out=turns[:, t, 1, :], in0=invf, scalar1=pos_f[:, t:t + 1], scalar2=None, op0=ALU.mult),
            r=["invf", "pos_f", "turns"], w=["turns"])
    dve(lambda e: e.tensor_scalar_add(out=turns[:, :, 0, :], in0=turns[:, :, 1, :], scalar1=0.25), r=["turns"], w=["turns"])
    dve(lambda e: e.tensor_copy(out=tint, in_=turns), r=["turns"], w=["tint"])
    dve(lambda e: e.tensor_copy(out=tfl, in_=tint), r=["tint"], w=["tfl"])
    dve(lambda e: e.tensor_tensor(out=turns, in0=turns, in1=tfl, op=ALU.subtract), r=["turns", "tfl"], w=["turns"])
    dve(lambda e: e.tensor_single_scalar(out=tfl, in_=turns, scalar=0.5, op=ALU.is_gt), r=["turns", "tfl"], w=["tfl"])
    dve(lambda e: e.tensor_tensor(out=turns, in0=turns, in1=tfl, op=ALU.subtract), r=["turns", "tfl"], w=["turns"])
    dve(lambda e: e.tensor_single_scalar(out=tfl, in_=turns, scalar=-0.5, op=ALU.is_lt), r=["turns", "tfl"], w=["tfl"])
    dve(lambda e: e.tensor_tensor(out=turns, in0=turns, in1=tfl, op=ALU.add), r=["turns", "tfl"], w=["turns"])
    act(lambda e: e.activation(out=cs_tab, in_=turns, func=AF.Sin, scale=2.0 * math.pi), r=["turns"], w=["cs_tab"])
    ld([(AB_s, AB)], r=["AB"], w=["AB_s"], eng="pool")
    if "AB" in dbg_d:
        ld([(dbg_d["AB"], AB[0:1, :, :])], r=["AB"], w=["dbg_AB"], eng="pool")
    if "cs" in dbg_d:
        ld([(dbg_d["cs"], cs_tab)], r=["cs_tab"], w=["dbg_cs"], eng="pool")
    P.flush()
    P.release()
    L = dict(locals())
    L["iota_f_p"] = L["iota_f_keep"]
    stage1(L)
    s5_main, s5_tail, base = stage_s5(L)
    attn_core, attn_proj = stage_attn(L)
    P.merge([P.capture(s5_main), P.capture(attn_core)])
    P.flush()
    P.sb_off = P.sb_mark = base
    ABf0 = P.sb([128, 4, D], F32, "ABf")
    wm_bf0 = P.sb([128, 8, D], BF16, "wm_bf")
    w2_bf0 = P.sb([128, 22, D], BF16, "w2_bf")
    L["ABf0"], L["wm_bf0"], L["w2_bf0"] = ABf0, wm_bf0, w2_bf0
    la = P.capture(s5_tail)
    lb = P.capture(attn_proj)
    P.merge([la, lb])
    ld([(ABf0, AB_s[:, 2:6, :])], w=["ABf"])
    ld([(wm_bf0, w_mix_d.rearrange("(k p) n -> p k n", p=128))], w=["wm_bf"], eng="pool")
    ld([(w2_bf0[:, 0:11, :], w_f2_d[0:1408, :].rearrange("(k p) n -> p k n", p=128)),
        (w2_bf0[:, 11:22, :], w_f2_d[1408:2816, :].rearrange("(k p) n -> p k n", p=128))], w=["w2_bf"], eng="pool")
    P.flush()
    P.sb_off = P.sb_mark = base
    stage_ffn(L)
    return P, es, L


def rms_ops(P, L, src, n, ssq, rs, rstd, scratch, key_src, tag, sqkey="sqbuf"):
    act, dve = L["act"], L["dve"]
    act(lambda e: e.activation(out=scratch, in_=src, func=AF.Square), r=[key_src], w=[sqkey])
    dve(lambda e: e.reduce_sum(out=ssq, in_=scratch, axis=AX.X), r=[sqkey], w=["ssq" + tag])
    act(lambda e: e.activation(out=rs, in_=ssq, func=AF.Sqrt, scale=1.0 / n, bias=L["eps_t"][:, 0:1]), r=["ssq" + tag], w=["rs" + tag])
    dve(lambda e: e.reciprocal(out=rstd, in_=rs), r=["rs" + tag], w=["rstd" + tag])


def stage1(L):
    P, nc = L["P"], L["nc"]
    act, dve, pool, pe, ld = L["act"], L["dve"], L["pool"], L["pe"], L["ld"]
    pf, pb, ident, cs_tab, ones_f = L["pf"], L["pb"], L["ident"], L["cs_tab"], L["ones_f"]
    x_d, w_in_d = L["x_d"], L["w_in_d"]
    w_in_bf, w_uq_bf, w_uk_bf, w_uv_bf = L["w_in_bf0"], L["w_uq_bf0"], L["w_uk_bf0"], L["w_uv_bf0"]
    P.skip([128, 8, IN_COLS], BF16); P.skip([128, 3, 768], BF16); P.skip([128, 2, 512], BF16); P.skip([128, 2, 512], BF16)
    AB = P.sb([128, 2, D], F32, "AB1")
    ld([(AB, L["AB_s"][:, 0:2, :])], w=["AB"])

    def dscr(name, shape, dt):
        return nc.dram_tensor(name, list(shape), dt, kind="Internal").ap()
    KT_s = L["KT_s"] = dscr("KT_s", [96, 8, SEQ], BF16)
    V_s = L["V_s"] = dscr("V_s", [SEQ, 8 * 65], BF16)
    QT_s = L["QT_s"] = dscr("QT_s", [96, 8, OWN], BF16)
    uT_s = L["uT_s"] = dscr("uT_s", [SSMC, SEQ], BF16)
    uT2_s = L["uT2_s"] = dscr("uT2_s", [SSMC, 8, SEQ // 8], BF16)
    osb2 = [P.sb([128, 8, 64], BF16, "osc%d" % i) for i in range(2)]
    gate_s = L["gate_s"]
    dbg_d = L["dbg_d"]

    eps_t = L["eps_t"]
    gq_b = P.sb([128, QLR], F32, "gq_b")
    gkv_b = P.sb([128, KVR], F32, "gkv_b")
    grow = P.sb([1, QLR + KVR], F32, "grow")
    ld([(grow[0:1, 0:QLR], L["gq_d"]), (grow[0:1, QLR:], L["gkv_d"])], w=["grow"])
    pe(lambda e: e.matmul(pf[0][:, 0:QLR], lhsT=ones_f[0:1, :], rhs=grow[0:1, 0:QLR], start=True, stop=True), r=["grow", "pf0"], w=["pf0"])
    act(lambda e: e.copy(out=gq_b, in_=pf[0][:, 0:QLR]), r=["pf0"], w=["gq_b"])
    pe(lambda e: e.matmul(pf[0][:, 0:KVR], lhsT=ones_f[0:1, :], rhs=grow[0:1, QLR:], start=True, stop=True), r=["grow", "pf0"], w=["pf0"])
    act(lambda e: e.copy(out=gkv_b, in_=pf[0][:, 0:KVR]), r=["pf0"], w=["gkv_b"])

    xt = [P.sb([128, D], F32, "xt%d" % i) for i in range(2)]
    sq = P.sb([128, D], F32, "sq")
    tmp = P.sb([128, D], F32, "tmp")
    hb = P.sb([128, D], BF16, "hb")
    hT = P.sb([128, 8, 512], BF16, "hT")
    st = P.sb([128, 16], F32, "st")
    osb = [P.sb([128, 512], BF16, "osb%d" % i) for i in range(2)]
    QSC = float(96.0 ** -0.5)
    oi = 0
    hTs = [hT, P.sb([128, 8, 512], BF16, "hT1")]
    sqx = P.sb([128, D], F32, "sqx")
    oi_box = [0]

    hbs = [hb, P.sb([128, D], BF16, "hb1")]

    def chain(bi):
        hTb, hk = hTs[bi % 2], "hT%d" % (bi % 2)

        def partA(tt):
            t = bi * 4 + tt
            xb, xk = xt[t % 2], "xt%d" % (t % 2)
            hbt, hbk = hbs[tt % 2], "hb%d" % (tt % 2)
            ld([(xb[:, 0:512], x_d[t * 128:(t + 1) * 128, 0:512]), (xb[:, 512:], x_d[t * 128:(t + 1) * 128, 512:])], w=[xk])
            rms_ops(P, L, xb, D, st[:, 0:1], st[:, 1:2], st[:, 2:3], sqx, xk, "x", "sqx")
            dve(lambda e, xb=xb: e.scalar_tensor_tensor(out=tmp, in0=xb, scalar=st[:, 2:3], in1=AB[:, 0, :], op0=ALU.mult, op1=ALU.mult),
                r=[xk, "rstdx", "AB"], w=["tmp"])
            dve(lambda e, hbt=hbt: e.tensor_tensor(out=hbt, in0=tmp, in1=AB[:, 1, :], op=ALU.add), r=["tmp", "AB"], w=[hbk])

        def partB(tt):
            hbt, hbk = hbs[tt % 2], "hb%d" % (tt % 2)

            def tr(e, hbt=hbt):
                ins = None
                for k in range(8):
                    ins = e.transpose(pb[0][:, k * 128:(k + 1) * 128], hbt[:, k * 128:(k + 1) * 128], ident)
                return ins
            pe(tr, r=[hbk, "ident"], w=["pb0"])
            act(lambda e, tt=tt, hTb=hTb: e.copy(out=hTb[:, :, tt * 128:(tt + 1) * 128], in_=pb[0].rearrange("p (k t) -> p k t", k=8)),
                r=["pb0"], w=[hk])
        for tt in range(5):
            if tt < 4:
                partA(tt)
            if tt >= 1:
                partB(tt - 1)

    def proj(bi):
        own = bi < 4
        hT, hk = hTs[bi % 2], "hT%d" % (bi % 2)
        oi = oi_box[0]
        bc = slice(bi * 512, (bi + 1) * 512)
        cols = [(672 + 128 * j, ("u", j)) for j in range(4)]
        if own:
            cols += [(1184 + 128 * j, ("g", j)) for j in range(16)]
        for c0, (kind, j) in cols:
            psx, pk = [(pf[2], "pf2"), (pf[4], "pf4"), (pf[1], "pf1")][oi % 3]
            ob, ok = osb[oi % 2], "osb%d" % (oi % 2)
            oi += 1
            oi_box[0] = oi

            def mm(e, c0=c0, psx=psx):
                ins = None
                for k in range(8):
                    ins = e.matmul(psx, lhsT=w_in_bf[:, k, c0:c0 + 128], rhs=hT[:, k, :], start=(k == 0), stop=(k == 7))
                return ins
            pe(mm, r=["w_in_bf", hk, pk], w=[pk])
            if kind == "u":
                act(lambda e, psx=psx, ob=ob: e.copy(out=ob, in_=psx), r=[pk], w=[ok])
                ld([(uT_s[j * 128:(j + 1) * 128, bc], ob)], r=[ok], w=["uT_s"], eng="pool")
                o2, o2k = osb2[j % 2], "osc%d" % (j % 2)
                dve(lambda e, ob=ob, o2=o2: e.tensor_copy(out=o2, in_=ob.rearrange("p (c s) -> p s c", s=8)), r=[ok], w=[o2k])
                ld([(uT2_s[j * 128:(j + 1) * 128, :, bi * 64:(bi + 1) * 64], o2)], r=[o2k], w=["uT2_s"], eng="pool")
            else:
                act(lambda e, psx=psx, ob=ob: e.activation(out=ob, in_=psx, func=AF.Sigmoid), r=[pk], w=[ok])
                ld([(gate_s[j, :, bc], ob)], r=[ok], w=["gate_s"], eng="pool")
        for t0 in (0, 2):
            P.merge([P.capture(lambda: tile_ops(bi, t0, hT, hk, own)), P.capture(lambda: tile_ops(bi, t0 + 1, hT, hk, own))])

    class TRes:
        def __init__(self, par, lat, latk, big, bigk, trb, trbk):
            self.p = str(par)
            self.lat, self.latk, self.big, self.bigk, self.trb, self.trbk = lat, latk, big, bigk, trb, trbk
            n = "_%d" % par
            self.st = P.sb([128, 16], F32, "tst" + n)
            self.sq = P.sb([128, QLR], F32, "tsq" + n)
            self.kvn = P.sb([128, KVR], BF16, "kvn" + n)
            self.kvnT = P.sb([128, 2, 128], BF16, "kvnT" + n)
            self.rp = P.sb([128, 6, 16], F32, "rp" + n)
            self.kr = P.sb([128, 32], BF16, "kr" + n)
            self.kasm = P.sb([128, 8, 96], BF16, "kasm" + n)
            self.vsb = P.sb([128, 8, 65], BF16, "vsb" + n)
            self.ktsb = P.sb([128, 8, 128], BF16, "ktsb" + n)
            self.qn = P.sb([128, QLR], BF16, "qn" + n)
            self.qnT = P.sb([128, 3, 128], BF16, "qnT" + n)
            self.qf = P.sb([128, 8, 96], F32, "qf" + n)
            self.qr = P.sb([128, 4, 8, 16], F32, "qr" + n)
            self.qasm = P.sb([128, 8, 96], BF16, "qasm" + n)
            self.qtsb = P.sb([128, 8, 128], BF16, "qtsb" + n)
            vs = self.vsb
            dve(lambda e: e.memset(vs, 1.0), w=["vsb" + self.p])
    tres = [TRes(0, pf[3], "pf3", pf[3], "pf3", pb[1], "pb1"),
            TRes(1, pf[0], "pf0", pf[0], "pf0", pf[5].bitcast(BF16), "pf5")]

    def tile_ops(bi, tt, hT, hk, own):
        R = tres[tt % 2]
        p_ = R.p
        K = lambda name: name + p_
        t = bi * 4 + tt
        tk = slice(tt * 128, (tt + 1) * 128)
        lat, latk, big, bigk, trb, trbk, st = R.lat, R.latk, R.big, R.bigk, R.trb, R.trbk, R.st
        kvn, kvnT, rp, kr, kasm, vsb, ktsb = R.kvn, R.kvnT, R.rp, R.kr, R.kasm, R.vsb, R.ktsb
        qn, qnT, qf, qr, qasm, qtsb, sq = R.qn, R.qnT, R.qf, R.qr, R.qasm, R.qtsb, R.sq

        def mmkv(e):
            ins = None
            for k in range(8):
                ins = e.matmul(lat[:, 0:288], lhsT=hT[:, k, tk], rhs=w_in_bf[:, k, 384:672], start=(k == 0), stop=(k == 7))
            return ins
        pe(mmkv, r=["w_in_bf", hk, latk], w=[latk])
        rms_ops(P, L, lat[:, 0:KVR], KVR, st[:, 4:5], st[:, 5:6], st[:, 6:7], sq[:, 0:KVR], latk, "kv" + p_, K("sqp"))
        dve(lambda e: e.scalar_tensor_tensor(out=kvn, in0=lat[:, 0:KVR], scalar=st[:, 6:7], in1=gkv_b, op0=ALU.mult, op1=ALU.mult),
            r=[latk, "rstdkv" + p_, "gkv_b"], w=[K("kvn")])
        cosv, sinv = cs_tab[:, t, 0, :], cs_tab[:, t, 1, :]
        x1, x2 = lat[:, 256:272], lat[:, 272:288]
        rk = [latk, "cs_tab", K("rp"), "rstdkv" + p_]
        dve(lambda e: e.tensor_tensor(out=rp[:, 0, :], in0=x1, in1=cosv, op=ALU.mult), r=rk, w=[K("rp")])
        dve(lambda e: e.tensor_tensor(out=rp[:, 1, :], in0=x2, in1=sinv, op=ALU.mult), r=rk, w=[K("rp")])
        dve(lambda e: e.tensor_tensor(out=rp[:, 2, :], in0=x2, in1=cosv, op=ALU.mult), r=rk, w=[K("rp")])
        dve(lambda e: e.tensor_tensor(out=rp[:, 3, :], in0=x1, in1=sinv, op=ALU.mult), r=rk, w=[K("rp")])
        dve(lambda e: e.tensor_tensor(out=kr[:, 0:16], in0=rp[:, 0, :], in1=rp[:, 1, :], op=ALU.subtract), r=[K("rp")], w=[K("kr")])
        dve(lambda e: e.tensor_tensor(out=kr[:, 16:32], in0=rp[:, 2, :], in1=rp[:, 3, :], op=ALU.add), r=[K("rp"), K("kr")], w=[K("kr")])
        dve(lambda e: e.tensor_copy(out=kasm[:, :, 64:96], in_=kr.unsqueeze(1).to_broadcast([128, 8, 32])), r=[K("kr"), K("kasm")], w=[K("kasm")])

        def tr2(e):
            ins = None
            for k in range(2):
                ins = e.transpose(trb[:, k * 128:(k + 1) * 128], kvn[:, k * 128:(k + 1) * 128], ident)
            return ins
        pe(tr2, r=[K("kvn"), "ident", trbk], w=[trbk])
        act(lambda e: e.copy(out=kvnT, in_=trb[:, 0:256].rearrange("p (k t) -> p k t", k=2)), r=[trbk], w=[K("kvnT")])

        def mmk(e, wsb):
            ins = None
            for k in range(2):
                ins = e.matmul(big, lhsT=kvnT[:, k, :], rhs=wsb[:, k, :], start=(k == 0), stop=(k == 1))
            return ins
        pe(lambda e: mmk(e, w_uk_bf), r=[K("kvnT"), "w_uk_bf", bigk], w=[bigk])
        act(lambda e: e.copy(out=kasm[:, :, 0:64], in_=big.rearrange("p (h d) -> p h d", h=8)), r=[bigk, K("kasm")], w=[K("kasm")])
        pe(lambda e: mmk(e, w_uv_bf), r=[K("kvnT"), "w_uv_bf", bigk], w=[bigk])
        act(lambda e: e.copy(out=vsb[:, :, 0:64], in_=big.rearrange("p (h d) -> p h d", h=8)), r=[bigk, K("vsb")], w=[K("vsb")])
        ld([(V_s[t * 128:(t + 1) * 128, :], vsb.rearrange("p h d -> p (h d)"))], r=[K("vsb")], w=["V_s" + p_], eng="pool")

        def trk(e, src):
            ins = None
            for h in range(8):
                ins = e.transpose(trb[0:96, h * 128:(h + 1) * 128], src[:, h, :], ident)
            return ins
        pe(lambda e: trk(e, kasm), r=[K("kasm"), "ident", trbk], w=[trbk])
        act(lambda e: e.copy(out=ktsb[0:96, :, :], in_=trb[0:96, :].rearrange("p (h t) -> p h t", h=8)), r=[trbk], w=[K("ktsb")])
        ld([(KT_s[:, :, t * 128:(t + 1) * 128], ktsb[0:96, :, :])], r=[K("ktsb")], w=["KT_s" + p_], eng="pool")
        if not own:
            return
        def mmq(e):
            ins = None
            for k in range(8):
                ins = e.matmul(lat[:, 0:QLR], lhsT=hT[:, k, tk], rhs=w_in_bf[:, k, 0:QLR], start=(k == 0), stop=(k == 7))
            return ins
        pe(mmq, r=["w_in_bf", hk, latk], w=[latk])
        rms_ops(P, L, lat[:, 0:QLR], QLR, st[:, 8:9], st[:, 9:10], st[:, 10:11], sq[:, 0:QLR], latk, "q" + p_, K("sqp"))
        dve(lambda e: e.scalar_tensor_tensor(out=qn, in0=lat[:, 0:QLR], scalar=st[:, 10:11], in1=gq_b, op0=ALU.mult, op1=ALU.mult),
            r=[latk, "rstdq" + p_, "gq_b"], w=[K("qn")])

        def tr3(e):
            ins = None
            for k in range(3):
                ins = e.transpose(trb[:, k * 128:(k + 1) * 128], qn[:, k * 128:(k + 1) * 128], ident)
            return ins
        pe(tr3, r=[K("qn"), "ident", trbk], w=[trbk])
        act(lambda e: e.copy(out=qnT, in_=trb[:, 0:384].rearrange("p (k t) -> p k t", k=3)), r=[trbk], w=[K("qnT")])

        qfl = qf.rearrange("p h d -> p (h d)")
        for (c0, cw) in ((0, 512), (512, 256)):
            def mmq2(e, c0=c0, cw=cw):
                ins = None
                for k in range(3):
                    ins = e.matmul(big[:, 0:cw], lhsT=qnT[:, k, :], rhs=w_uq_bf[:, k, c0:c0 + cw], start=(k == 0), stop=(k == 2))
                return ins
            pe(mmq2, r=[K("qnT"), "w_uq_bf", bigk], w=[bigk])
            act(lambda e, c0=c0, cw=cw: e.activation(out=qfl[:, c0:c0 + cw], in_=big[:, 0:cw], func=AF.Copy, scale=QSC), r=[bigk, K("qf")], w=[K("qf")])
        dve(lambda e: e.tensor_copy(out=qasm[:, :, 0:64], in_=qf[:, :, 0:64]), r=[K("qf"), K("qasm")], w=[K("qasm")])
        cb_ = cosv.unsqueeze(1).to_broadcast([128, 8, 16])
        sb_ = sinv.unsqueeze(1).to_broadcast([128, 8, 16])
        qk = [K("qf"), "cs_tab", K("qr")]
        dve(lambda e: e.tensor_tensor(out=qr[:, 0], in0=qf[:, :, 64:80], in1=cb_, op=ALU.mult), r=qk, w=[K("qr")])
        dve(lambda e: e.tensor_tensor(out=qr[:, 1], in0=qf[:, :, 80:96], in1=sb_, op=ALU.mult), r=qk, w=[K("qr")])
        dve(lambda e: e.tensor_tensor(out=qr[:, 2], in0=qf[:, :, 80:96], in1=cb_, op=ALU.mult), r=qk, w=[K("qr")])
        dve(lambda e: e.tensor_tensor(out=qr[:, 3], in0=qf[:, :, 64:80], in1=sb_, op=ALU.mult), r=qk, w=[K("qr")])
        dve(lambda e: e.tensor_tensor(out=qasm[:, :, 64:80], in0=qr[:, 0], in1=qr[:, 1], op=ALU.subtract), r=[K("qr"), K("qasm")], w=[K("qasm")])
        dve(lambda e: e.tensor_tensor(out=qasm[:, :, 80:96], in0=qr[:, 2], in1=qr[:, 3], op=ALU.add), r=[K("qr"), K("qasm")], w=[K("qasm")])
        pe(lambda e: trk(e, qasm), r=[K("qasm"), "ident", trbk], w=[trbk])
        act(lambda e: e.copy(out=qtsb[0:96, :, :], in_=trb[0:96, :].rearrange("p (h t) -> p h t", h=8)), r=[trbk], w=[K("qtsb")])
        ld([(QT_s[:, :, t * 128:(t + 1) * 128], qtsb[0:96, :, :])], r=[K("qtsb")], w=["QT_s" + p_], eng="pool")

    w1b_s = L["w1b_s"] = dscr("w1b_s", [22, 128, 8, 256], BF16)
    w1v_ = L["w_f1_d"].rearrange("(k p) n -> p k n", p=128)
    sc_list = P.capture(lambda: s5_scalar(L))
    P.merge([P.capture(lambda: chain(0))])
    for bi in range(8):
        lists = [P.capture(lambda: proj(bi))]
        if bi + 1 < 8:
            lists.append(P.capture(lambda: chain(bi + 1)))
        if bi == 0:
            lists.append(sc_list)
        P.merge(lists)
        if bi < 4:
            prs = []
            for hc in range(bi * 6, min(22, bi * 6 + 6)):
                prs += [(w1b_s[hc, :, :, 0:128], w1v_[:, :, hc * 128:(hc + 1) * 128]),
                        (w1b_s[hc, :, :, 128:256], w1v_[:, :, FFH + hc * 128:FFH + (hc + 1) * 128])]
            ld(prs, w=["w1b_s_%d" % bi], eng="pool")
    for nm, src in (("KT", KT_s), ("QT", QT_s), ("V", V_s), ("uT", uT_s)):
        if nm in dbg_d:
            ld([(dbg_d[nm], src)], r=[nm + "_s"], w=["dbg_" + nm], eng="pool")
    P.flush()
    P.release()


def s5_scalar(L):
    P, nc = L["P"], L["nc"]
    act, dve, pool, pe, ld = L["act"], L["dve"], L["pool"], L["pe"], L["ld"]

    def T(shape, dt=F32, name="s5s"):
        return P.sb(shape, dt, name)

    def bc3(v, m):
        return v.unsqueeze(2).to_broadcast([v.shape[0], v.shape[1], m])

    MU = T([64, 3, 2, 64])
    lam = T([64, 2, 64]); ldt = T([64, 64]); Bc = T([64, 2, 64, 16])
    ld([(lam, L["lam_d"]), (ldt, L["ldt_d"]), (Bc, L["sb_d"])], w=["lam", "ldt", "Bc"])
    sm = T([64, 24, 64])
    K = "sm"

    def d2(fn, r=(), w=()):
        dve(fn, r=list(r) + [K], w=list(w) + [K])
    lre, lim, dt_, a_, th, mag = (sm[:, i, :] for i in range(6))
    d2(lambda e: e.tensor_scalar_min(out=lre, in0=lam[:, 0, :], scalar1=-1e-4), r=["lam"])
    d2(lambda e: e.tensor_copy(out=lim, in_=lam[:, 1, :]), r=["lam"])
    act(lambda e: e.activation(out=dt_, in_=ldt, func=AF.Exp), r=["ldt", K], w=[K])
    d2(lambda e: e.tensor_tensor(out=a_, in0=lre, in1=dt_, op=ALU.mult))
    d2(lambda e: e.tensor_tensor(out=th, in0=lim, in1=dt_, op=ALU.mult))
    act(lambda e: e.activation(out=mag, in_=a_, func=AF.Exp), r=[K], w=[K])
    tr_ = T([64, 2, 64]); ti_ = T([64, 2, 64], I32); tf_ = T([64, 2, 64])
    d2(lambda e: e.tensor_single_scalar(out=tr_[:, 1, :], in_=th, scalar=1.0 / (2 * math.pi), op=ALU.mult), w=["tr_"])
    dve(lambda e: e.tensor_scalar_add(out=tr_[:, 0, :], in0=tr_[:, 1, :], scalar1=0.25), r=["tr_"], w=["tr_"])
    dve(lambda e: e.tensor_copy(out=ti_, in_=tr_), r=["tr_"], w=["ti_"])
    dve(lambda e: e.tensor_copy(out=tf_, in_=ti_), r=["ti_"], w=["tf_"])
    dve(lambda e: e.tensor_tensor(out=tr_, in0=tr_, in1=tf_, op=ALU.subtract), r=["tr_", "tf_"], w=["tr_"])
    dve(lambda e: e.tensor_single_scalar(out=tf_, in_=tr_, scalar=0.5, op=ALU.is_gt), r=["tr_", "tf_"], w=["tf_"])
    dve(lambda e: e.tensor_tensor(out=tr_, in0=tr_, in1=tf_, op=ALU.subtract), r=["tr_", "tf_"], w=["tr_"])
    dve(lambda e: e.tensor_single_scalar(out=tf_, in_=tr_, scalar=-0.5, op=ALU.is_lt), r=["tr_", "tf_"], w=["tf_"])
    dve(lambda e: e.tensor_tensor(out=tr_, in0=tr_, in1=tf_, op=ALU.add), r=["tr_", "tf_"], w=["tr_"])
    cs = T([64, 2, 64])
    act(lambda e: e.activation(out=cs, in_=tr_, func=AF.Sin, scale=2.0 * math.pi), r=["tr_"], w=["cs"])
    PW = T([64, 2, 16, 64])
    KP = "PW"

    def pw(k):
        return PW[:, 0, k + 7, :], PW[:, 1, k + 7, :]
    ab_re, ab_im = pw(1)
    dve(lambda e: e.tensor_tensor(out=ab_re, in0=mag, in1=cs[:, 0, :], op=ALU.mult), r=[K, "cs", KP], w=[KP])
    dve(lambda e: e.tensor_tensor(out=ab_im, in0=mag, in1=cs[:, 1, :], op=ALU.mult), r=[K, "cs", KP], w=[KP])
    one_re, one_im = pw(0)
    dve(lambda e: e.memset(one_re, 1.0), r=[KP], w=[KP]); dve(lambda e: e.memset(one_im, 0.0), r=[KP], w=[KP])
    s0, s1, s2, s3 = (sm[:, i, :] for i in range(6, 10))

    def cmul(o_re, o_im, x_re, x_im, y_re, y_im, keys):
        kk = list(keys) + [K]
        dve(lambda e: e.tensor_tensor(out=s0, in0=x_re, in1=y_re, op=ALU.mult), r=kk, w=[K])
        dve(lambda e: e.tensor_tensor(out=s1, in0=x_im, in1=y_im, op=ALU.mult), r=kk, w=[K])
        dve(lambda e: e.tensor_tensor(out=s2, in0=x_re, in1=y_im, op=ALU.mult), r=kk, w=[K])
        dve(lambda e: e.tensor_tensor(out=s3, in0=x_im, in1=y_re, op=ALU.mult), r=kk, w=[K])
        dve(lambda e: e.tensor_tensor(out=o_re, in0=s0, in1=s1, op=ALU.subtract), r=kk, w=kk)
        dve(lambda e: e.tensor_tensor(out=o_im, in0=s2, in1=s3, op=ALU.add), r=kk, w=kk)
    inv_re, inv_im = pw(-1)
    m2, rm2 = sm[:, 10, :], sm[:, 11, :]
    d2(lambda e: e.tensor_tensor(out=s0, in0=ab_re, in1=ab_re, op=ALU.mult), r=[KP])
    d2(lambda e: e.tensor_tensor(out=s1, in0=ab_im, in1=ab_im, op=ALU.mult), r=[KP])
    d2(lambda e: e.tensor_tensor(out=m2, in0=s0, in1=s1, op=ALU.add))
    d2(lambda e: e.reciprocal(out=rm2, in_=m2))
    dve(lambda e: e.tensor_tensor(out=inv_re, in0=ab_re, in1=rm2, op=ALU.mult), r=[K, KP], w=[KP])
    dve(lambda e: e.scalar_tensor_tensor(out=inv_im, in0=ab_im, scalar=-1.0, in1=rm2, op0=ALU.mult, op1=ALU.mult), r=[K, KP], w=[KP])
    for k in range(1, 8):
        cmul(*pw(k + 1), *pw(k), ab_re, ab_im, [KP])
    for k in range(1, 7):
        cmul(*pw(-k - 1), *pw(-k), inv_re, inv_im, [KP])
    dve(lambda e: e.tensor_copy(out=MU[:, 0, 0, :], in_=pw(8)[0]), r=[KP], w=["MU"])
    dve(lambda e: e.tensor_copy(out=MU[:, 0, 1, :], in_=pw(8)[1]), r=[KP, "MU"], w=["MU"])
    sq_ = T([64, 2, 2, 64])
    for lv in (1, 2):
        src = (MU[:, lv - 1, 0, :], MU[:, lv - 1, 1, :])
        for it in range(3):
            dst = (MU[:, lv, 0, :], MU[:, lv, 1, :]) if it == 2 else (sq_[:, it, 0, :], sq_[:, it, 1, :])
            cmul(dst[0], dst[1], src[0], src[1], src[0], src[1], ["MU", "sq_"])
            src = dst
    nr, den, rden, f_re, f_im = (sm[:, i, :] for i in range(12, 17))
    d2(lambda e: e.tensor_scalar_add(out=nr, in0=ab_re, scalar1=-1.0), r=[KP])
    d2(lambda e: e.tensor_tensor(out=s0, in0=lre, in1=lre, op=ALU.mult))
    d2(lambda e: e.tensor_tensor(out=s1, in0=lim, in1=lim, op=ALU.mult))
    d2(lambda e: e.tensor_tensor(out=den, in0=s0, in1=s1, op=ALU.add))
    d2(lambda e: e.reciprocal(out=rden, in_=den))
    d2(lambda e: e.tensor_tensor(out=s0, in0=nr, in1=lre, op=ALU.mult))
    d2(lambda e: e.tensor_tensor(out=s1, in0=ab_im, in1=lim, op=ALU.mult), r=[KP])
    d2(lambda e: e.tensor_tensor(out=s0, in0=s0, in1=s1, op=ALU.add))
    d2(lambda e: e.tensor_tensor(out=f_re, in0=s0, in1=rden, op=ALU.mult))
    d2(lambda e: e.tensor_tensor(out=s0, in0=ab_im, in1=lre, op=ALU.mult), r=[KP])
    d2(lambda e: e.tensor_tensor(out=s1, in0=nr, in1=lim, op=ALU.mult))
    d2(lambda e: e.tensor_tensor(out=s0, in0=s0, in1=s1, op=ALU.subtract))
    d2(lambda e: e.tensor_tensor(out=f_im, in0=s0, in1=rden, op=ALU.mult))
    Bb = T([64, 2, 64, 16]); t16 = T([64, 2, 64, 16])

    def cmul16(o_re, o_im, s_re, s_im, x_re, x_im, n, keys_r, keys_w, ta, tb, eng=None, tk="t16"):
        eng = eng or dve
        sr, si = bc3(s_re, 16), bc3(s_im, 16)
        kr = list(keys_r) + list(keys_w) + [tk]
        eng(lambda e: e.tensor_tensor(out=ta, in0=x_re, in1=sr, op=ALU.mult), r=kr, w=[tk])
        eng(lambda e: e.tensor_tensor(out=tb, in0=x_im, in1=si, op=ALU.mult), r=kr, w=[tk])
        eng(lambda e: e.tensor_tensor(out=o_re, in0=ta, in1=tb, op=ALU.subtract), r=kr, w=list(keys_w) + [tk])
        eng(lambda e: e.tensor_tensor(out=ta, in0=x_im, in1=sr, op=ALU.mult), r=kr, w=[tk])
        eng(lambda e: e.tensor_tensor(out=tb, in0=x_re, in1=si, op=ALU.mult), r=kr, w=[tk])
        eng(lambda e: e.tensor_tensor(out=o_im, in0=ta, in1=tb, op=ALU.add), r=kr, w=list(keys_w) + [tk])
    cmul16(Bb[:, 0], Bb[:, 1], f_re, f_im, Bc[:, 0], Bc[:, 1], 64, [K, "Bc"], ["Bb"], t16[:, 0], t16[:, 1])
    PW_s = L["PW_s"] = nc.dram_tensor("PW_s", [64, 2, 16, 64], F32, kind="Internal").ap()
    Bb_s = L["Bb_s"] = nc.dram_tensor("Bb_s", [64, 2, 64, 16], F32, kind="Internal").ap()
    MU_s = L["MU_s"] = nc.dram_tensor("MU_s", [64, 3, 2, 64], F32, kind="Internal").ap()
    ld([(PW_s, PW), (Bb_s, Bb), (MU_s, MU)], r=[KP, "Bb", "MU"], w=["PW_s", "Bb_s", "MU_s"], eng="pool")


def stage_s5(L):
    P, nc = L["P"], L["nc"]
    act, dve, pool, pe, ld = L["act"], L["dve"], L["pool"], L["pe"], L["ld"]
    pf, pb, identf, iota_f = L["pf"], L["pb"], L["identf"], L["iota_f_p"]
    uT_s, brb_s, dbg_d = L["uT_s"], L["brb_s"], L["dbg_d"]
    base_mark = P.sb_mark

    def T(shape, dt=F32, name="s5"):
        return P.sb(shape, dt, name)

    def bc3(v, m):
        return v.unsqueeze(2).to_broadcast([v.shape[0], v.shape[1], m])

    G = T([128, 8, 8, 128], BF16); MU = T([64, 3, 2, 64]); dsk = T([128, 4])
    MUS = T([128, 3, 2, 2, 4, 4])
    P.mark()
    Wq = [T([128, 32, 128], BF16)]
    T0q = [T([128, 32, 128], BF16)]
    Vbq = [T([128, 2, 16, 128], BF16)]
    W_s = nc.dram_tensor("W_s", [128, 64, 128], BF16, kind="Internal").ap()
    T0_s = nc.dram_tensor("T0_s", [128, 64, 128], BF16, kind="Internal").ap()
    Vb_s = nc.dram_tensor("Vb_s", [64, 2, 64, 128], BF16, kind="Internal").ap()
    GY_s = L["GY_s"] = nc.dram_tensor("GY_s", [4, 128, OWN], BF16, kind="Internal").ap()
    KP = "PW"
    PW = T([64, 2, 16, 64]); Bb = T([64, 2, 64, 16])
    ld([(PW, L["PW_s"]), (Bb, L["Bb_s"]), (MU, L["MU_s"])], w=[KP, "Bb", "MU"])

    def cmul16(o_re, o_im, s_re, s_im, x_re, x_im, n, keys_r, keys_w, ta, tb, eng=None, tk="t16"):
        eng = eng or dve
        sr, si = bc3(s_re, 16), bc3(s_im, 16)
        kr = list(keys_r) + list(keys_w) + [tk]
        eng(lambda e: e.tensor_tensor(out=ta, in0=x_re, in1=sr, op=ALU.mult), r=kr, w=[tk])
        eng(lambda e: e.tensor_tensor(out=tb, in0=x_im, in1=si, op=ALU.mult), r=kr, w=[tk])
        eng(lambda e: e.tensor_tensor(out=o_re, in0=ta, in1=tb, op=ALU.subtract), r=kr, w=list(keys_w) + [tk])
        eng(lambda e: e.tensor_tensor(out=ta, in0=x_im, in1=sr, op=ALU.mult), r=kr, w=[tk])
        eng(lambda e: e.tensor_tensor(out=tb, in0=x_re, in1=si, op=ALU.mult), r=kr, w=[tk])
        eng(lambda e: e.tensor_tensor(out=o_im, in0=ta, in1=tb, op=ALU.add), r=kr, w=list(keys_w) + [tk])
    expsE = ([7 - s_ for s_ in range(8)], [s_ for s_ in range(8)])
    expsF = ([i - 7 for i in range(8)], [-i for i in range(8)])
    expsV = ([i + 1 for i in range(8)], [8 - i for i in range(8)])
    Es = [T([128, 2, 16, 8, 16]) for _ in range(2)]
    Fs = [T([128, 2, 16, 8, 16]) for _ in range(2)]
    tS = T([128, 8, 2, 16, 16])
    tSv = T([128, 8, 2, 16, 16])
    PT = {nm: T([128, 2, 8, 32]) for nm in ("E", "F", "V")}
    Bb2 = T([128, 2, 32, 16]); Cc2 = T([128, 2, 32, 16])
    dmas = []
    for nm, exps in (("E", expsE), ("F", expsF), ("V", expsV)):
        for s_ in range(8):
            kA, kB = exps[0][s_] + 7, exps[1][s_] + 7
            dve(lambda e, nm=nm, s_=s_, kA=kA: e.tensor_copy(out=PT[nm][0:64, :, s_, :], in_=PW[:, :, kA, 0:32]), r=[KP, "PT"], w=["PT"])
            dmas.append((PT[nm][64:128, :, s_, :], PW[:, :, kB, 32:64]))
    ld(dmas, r=[KP], w=["PTb"])
    dve(lambda e: e.tensor_copy(out=Bb2[0:64], in_=Bb[:, :, 0:32, :]), r=["Bb"], w=["Bb2"])
    ld([(Bb2[64:128], Bb[:, :, 32:64, :])], r=["Bb"], w=["Bb2b"])
    ld([(Cc2[0:64], L["sc_d"][:, :, 0:32, :]), (Cc2[64:128], L["sc_d"][:, :, 32:64, :])], w=["Cc2"])

    def gen(dst, src, nm, keyd, keys, negim, b, scr=None, scrk="tS"):
        scr = tS if scr is None else scr
        gs = slice(b * 16, (b + 1) * 16)
        tab = PT[nm]
        for s_ in range(8):
            cmul16(dst[:, 0, :, s_, :], dst[:, 1, :, s_, :], tab[:, 0, s_, gs], tab[:, 1, s_, gs], src[:, 0, gs], src[:, 1, gs], 16,
                   ["PT", "PTb"] + list(keys), ["%s%d" % (keyd, s_)], scr[:, s_, 0], scr[:, s_, 1], tk="%s%d" % (scrk, s_))
            if negim:
                dve(lambda e, s_=s_: e.tensor_single_scalar(out=dst[:, 1, :, s_, :], in_=dst[:, 1, :, s_, :], scalar=-1.0, op=ALU.mult),
                    r=["%s%d" % (keyd, s_)], w=["%s%d" % (keyd, s_)])
    mi = T([128, 2, 128], I32); mf = T([128, 2, 128]); mask = T([128, 2, 128])
    pool(lambda e: e.iota(mi[:, 0, :], pattern=[[1, 128]], base=0, channel_multiplier=0), w=["mi"])
    pool(lambda e: e.iota(mi[:, 1, :], pattern=[[0, 128]], base=0, channel_multiplier=1), r=["mi"], w=["mi"])
    dve(lambda e: e.tensor_single_scalar(out=mi, in_=mi, scalar=4, op=ALU.arith_shift_right), r=["mi"], w=["mi"])
    dve(lambda e: e.tensor_copy(out=mf, in_=mi), r=["mi"], w=["mf"])
    dve(lambda e: e.tensor_tensor(out=mask[:, 0, :], in0=mf[:, 0, :], in1=mf[:, 1, :], op=ALU.is_ge), r=["mf"], w=["mask"])
    dve(lambda e: e.tensor_tensor(out=mask[:, 1, :], in0=mf[:, 1, :], in1=mf[:, 0, :], op=ALU.is_ge), r=["mf", "mask"], w=["mask"])
    rowm = T([128, 8])
    for a in range(8):
        dve(lambda e, a=a: e.tensor_single_scalar(out=rowm[:, a:a + 1], in_=mf[:, 1, 0:1], scalar=float(a), op=ALU.is_equal), r=["mf", "rowm"], w=["rowm"])
    for a in range(8):
        for b in range(8):
            dve(lambda e, a=a, b=b: e.tensor_scalar(out=G[:, a, b, :], in0=iota_f, scalar1=float(16 * (b - a)), scalar2=rowm[:, a:a + 1],
                                                   op0=ALU.is_equal, op1=ALU.mult), r=["iota_f", "rowm", "G"], w=["G"])
    def genA(b):
        gen(Es[b % 2], Bb2, "E", "E%d_" % (b % 2), ["Bb2", "Bb2b"], False, b)
        gen(Fs[b % 2], Cc2, "F", "F%d_" % (b % 2), ["Cc2"], True, b)

    def secB(b):
        par = b % 2
        E, F = Es[par], Fs[par]
        Wt, T0t, Vbt = Wq[0], T0q[0], Vbq[0]
        ek = ["E%d_%d" % (par, i_) for i_ in range(8)]
        fk = ["F%d_%d" % (par, i_) for i_ in range(8)]
        n_ = 0
        for d_ in range(2):
            pr_ = slice(64 * d_, 64 * d_ + 64)
            idn = identf[pr_, pr_]
            for gl in range(16):
                slot = d_ * 16 + gl
                psw = pf[n_ % 2]; pk = "pf%d" % (n_ % 2)
                pst = pf[2 + n_ % 2]; pk2 = "pf%d" % (2 + n_ % 2)
                n_ += 1

                def trw(e, gl=gl, psw=psw, pr_=pr_, idn=idn):
                    e.transpose(psw[:, 0:64], E[pr_, 0, gl].rearrange("p s h -> p (s h)"), idn)
                    return e.transpose(psw[:, 64:128], E[pr_, 1, gl].rearrange("p s h -> p (s h)"), idn)
                pe(trw, r=ek + [pk], w=[pk])
                act(lambda e, slot=slot, psw=psw: e.copy(out=Wt[:, slot, :], in_=psw[:, 0:128]), r=[pk, "Wq0"], w=["Wq0"])

                def mt0(e, gl=gl, pst=pst, pr_=pr_):
                    e.matmul(pst[:, 0:128], lhsT=E[pr_, 0, gl].rearrange("p s h -> p (s h)"), rhs=F[pr_, 0, gl].rearrange("p s h -> p (s h)"), start=True, stop=False)
                    return e.matmul(pst[:, 0:128], lhsT=E[pr_, 1, gl].rearrange("p s h -> p (s h)"), rhs=F[pr_, 1, gl].rearrange("p s h -> p (s h)"), start=False, stop=True)
                pe(mt0, r=ek + fk + [pk2], w=[pk2])
                dve(lambda e, slot=slot, d_=d_, pst=pst: e.tensor_tensor(out=T0t[:, slot, :], in0=pst[:, 0:128], in1=mask[:, d_, :], op=ALU.mult),
                    r=[pk2, "mask", "T0q0"], w=["T0q0"])
        gen(F, Cc2, "V", "F%d_" % par, ["Cc2"], True, b, tSv, "tSv")
        dve(lambda e: e.tensor_copy(out=Vbt, in_=F.rearrange("p r g i h -> p r g (i h)")), r=fk + ["Vbq0"], w=["Vbq0"])
        for d_ in range(2):
            qs = slice(d_ * 32 + 16 * b, d_ * 32 + 16 * b + 16)
            ld([(W_s[:, qs, :], Wt[:, d_ * 16:(d_ + 1) * 16, :])], r=["Wq0"], w=["W_s"], eng="pool")
            ld([(T0_s[:, qs, :], T0t[:, d_ * 16:(d_ + 1) * 16, :])], r=["T0q0"], w=["T0_s"], eng="pool")
            ld([(Vb_s[:, :, qs, :], Vbt[64 * d_:64 * d_ + 64])], r=["Vbq0"], w=["Vb_s"], eng="pool")

    genA(0)
    for b in range(2):
        lists = [P.capture(lambda: secB(b))]
        if b + 1 < 2:
            lists.append(P.capture(lambda: genA(b + 1)))
        P.merge(lists)
    ld([(dsk, L["sd_d"])], w=["dsk"])
    for hb__ in range(2):
        prs = []
        for lv in range(3):
            for ri in range(2):
                for d_ in range(2):
                    src = MU[:, lv, ri, d_ * 32:(d_ + 1) * 32].rearrange("p (j h q) -> p j h q", j=4, h=2, q=4)[:, :, hb__, :]
                    prs.append((MUS[64 * hb__:64 * hb__ + 64, lv, ri, d_, :, :], src))
        ld(prs, r=["MU"], w=["MUS%d" % hb__])
    P.flush()
    P.release()
    P.mark()

    NBT = 4
    uTj = T([128, SEQ], BF16)
    WJ = T([128, 16, 128], BF16); T0J = T([128, 16, 128], BF16); VJ = T([128, 2, 16, 128], BF16)
    Yb = T([128, 8, 256], BF16); ysb = T([128, OWN])
    y_s = nc.dram_tensor("y_s", [4, 128, OWN], F32, kind="Internal").ap()
    NP = 128

    class Scr:
        def __init__(self, tag):
            self.tag = tag
            self.zup = [T([NP, NBT, 2, 32]), T([NP, NBT, 2, 4])]
            self.pup = [T([NP, NBT, 2, 32]), T([NP, NBT, 2, 4])]
            self.accs = [T([NP, NBT, 2, 32]), T([NP, NBT, 2, 32])]
            self.tsc = T([NP, 4, NBT, 32])
            self.k = "scr" + tag
    class ZSet:
        def __init__(self, tag):
            self.t = tag
            self.Us = [T([128, NBT, 512], BF16), T([128, NBT, 512], BF16)]
            self.ZA = T([NP, NBT, 2, 256]); self.ZB = T([NP, NBT, 2, 256]); self.ZO = T([NP, NBT, 2, 256])
            self.NB = self.ZO
            self.PAb = self.ZA.bitcast(BF16)[:, :, :, 0:256]
            self.NBb = self.ZB.bitcast(BF16)[:, :, :, 0:256]
    zsets = [ZSet("e"), ZSet("o")]
    PA = T([NP, NBT, 2, 256])
    carry = T([NP, NBT, 2, 1])
    scrA, scrB = Scr("A"), Scr("B")

    def madd(sc, new, acc, z, mu_re, mu_im, m, keys):
        n_re, n_im = new; a_re, a_im = acc
        mr, mi_ = bc3(mu_re, m), bc3(mu_im, m)
        t1, t2, t3, t4 = (sc.tsc[:, i, :, 0:m] for i in range(4))
        kd, kp = sc.k + "d", sc.k + "p"
        kk = list(keys) + ["MUS0", "MUS1"]
        dve(lambda e: e.tensor_tensor(out=t1, in0=a_re, in1=mr, op=ALU.mult), r=kk + [kd], w=[kd])
        dve(lambda e: e.tensor_tensor(out=t2, in0=a_im, in1=mi_, op=ALU.mult), r=kk + [kd], w=[kd])
        dve(lambda e: e.tensor_tensor(out=t1, in0=t1, in1=t2, op=ALU.subtract), r=[kd], w=[kd])
        pool(lambda e: e.tensor_tensor(out=t3, in0=a_im, in1=mr, op=ALU.mult), r=kk + [kp], w=[kp])
        pool(lambda e: e.tensor_tensor(out=t4, in0=a_re, in1=mi_, op=ALU.mult), r=kk + [kp], w=[kp])
        pool(lambda e: e.tensor_tensor(out=t3, in0=t3, in1=t4, op=ALU.add), r=[kp], w=[kp])
        dve(lambda e: e.tensor_tensor(out=n_re, in0=t1, in1=z[0], op=ALU.add), r=[kd] + kk, w=kk[:-2])
        dve(lambda e: e.tensor_tensor(out=n_im, in0=t3, in1=z[1], op=ALU.add), r=[kp] + kk, w=kk[:-2])

    def vw(Zt, sl):
        return Zt[:, :, 0, sl], Zt[:, :, 1, sl]

    def blk(Zt, k, n):
        return (Zt[:, :, 0, 0:n].rearrange("p j (m k) -> p j k m", k=8)[:, :, k, :],
                Zt[:, :, 1, 0:n].rearrange("p j (m k) -> p j k m", k=8)[:, :, k, :])

    def mu_of(lv, dgs):
        d_, J_ = dgs
        return MUS[:, lv, 0, d_, J_, :], MUS[:, lv, 1, d_, J_, :]

    def horner(sc, Zt, n, lv, dgs, desc, out, keys):
        nb = n // 8
        order = list(range(7, -1, -1)) if desc else list(range(8))
        mr, mi_ = mu_of(lv, dgs)
        cur = blk(Zt, order[0], n)
        for ii, k in enumerate(order[1:]):
            dst = vw(out, slice(0, nb)) if ii == 6 else vw(sc.accs[ii % 2], slice(0, nb))
            madd(sc, dst, cur, blk(Zt, k, n), mr, mi_, nb, keys)
            cur = dst

    def seq_top(sc, Zt, n, lv, dgs, desc, init, out, keys, ikey=None):
        mr, mi_ = mu_of(lv, dgs)
        idx = list(range(n - 1, -1, -1)) if desc else list(range(n))
        if init is None:
            dve(lambda e: e.memset(out[:, :, :, idx[0]:idx[0] + 1], 0.0), r=keys, w=keys)
        else:
            dve(lambda e: e.tensor_copy(out=out[:, :, :, idx[0]:idx[0] + 1], in_=init), r=keys + [ikey], w=keys)
        for a, b in zip(idx[:-1], idx[1:]):
            madd(sc, vw(out, slice(b, b + 1)), vw(out, slice(a, a + 1)), vw(Zt, slice(a, a + 1)), mr, mi_, 1, keys)

    def exscan(sc, Zt, n, lv, dgs, desc, init, out, keys, ikey=None):
        if n <= 4:
            seq_top(sc, Zt, n, lv, dgs, desc, init, out, keys, ikey)
            return
        nb = n // 8
        zu, pu = sc.zup[lv], sc.pup[lv]
        horner(sc, Zt, n, lv, dgs, desc, zu, keys)
        exscan(sc, zu, nb, lv + 1, dgs, desc, init, pu, keys, ikey)
        order = list(range(7, -1, -1)) if desc else list(range(8))
        mr, mi_ = mu_of(lv, dgs)
        o0 = blk(out, order[0], n)
        dve(lambda e: e.tensor_copy(out=o0[0], in_=pu[:, :, 0, 0:nb]), r=keys, w=keys)
        pool(lambda e: e.tensor_copy(out=o0[1], in_=pu[:, :, 1, 0:nb]), r=keys, w=keys)
        for a, b in zip(order[:-1], order[1:]):
            madd(sc, blk(out, b, n), blk(out, a, n), blk(Zt, a, n), mr, mi_, nb, keys)

    GEL = 1.5957691216057308
    psA, pkA = pf[3], "pf3"
    psB, pkB = pf[7], "pf7"

    def pre(J):
        zs = zsets[J % 2]
        tg = zs.t
        ld([(uTj.rearrange("p (s c) -> p s c", s=8), L["uT2_s"][J * 128:(J + 1) * 128, :, :])], w=["uTj"])
        ld([(WJ[:, 0:8, :], W_s[:, 8 * J:8 * J + 8, :]), (WJ[:, 8:16, :], W_s[:, 32 + 8 * J:40 + 8 * J, :])], w=["WJ"])
        uview = uTj.rearrange("p (s c) -> p s c", s=8)
        for hb_ in range(2):
            U = zs.Us[hb_]
            ph = slice(64 * hb_, 64 * hb_ + 64)
            for jj in range(NBT):
                j = hb_ * NBT + jj
                uk = "U%s%d_%d" % (tg, hb_, jj)

                def msel(e, j=j):
                    ins = None
                    for s_ in range(8):
                        ins = e.matmul(psA, lhsT=G[:, j, s_, :], rhs=uview[:, s_, :], start=(s_ == 0), stop=(s_ == 7))
                    return ins
                pe(msel, r=["G", "uTj", pkA], w=[pkA])
                dve(lambda e, jj=jj, U=U: e.tensor_copy(out=U[:, jj, :], in_=psA), r=[pkA], w=[uk])
                for (Zt, zk, dl, c0) in ((zs.ZA, "ZA" + tg, j, 0), (zs.ZB, "ZB" + tg, 8 + j, 0), (zs.ZO, "ZO" + tg, 8 + j, 256)):
                    pkh = pkB

                    def mz(e, jj=jj, dl=dl, c0=c0, U=U, ph=ph):
                        e.matmul(psB[ph, 0:256], lhsT=WJ[:, dl, 0:64], rhs=U[:, jj, c0:c0 + 256], start=True, stop=True)
                        return e.matmul(psB[ph, 256:512], lhsT=WJ[:, dl, 64:128], rhs=U[:, jj, c0:c0 + 256], start=True, stop=True)
                    pe(mz, r=["WJ", uk, pkB], w=[pkB])
                    dve(lambda e, Zt=Zt, jj=jj, ph=ph: e.tensor_copy(out=Zt[ph, jj, :, :], in_=psB[ph, :].rearrange("p (r c) -> p r c", r=2)),
                        r=[pkh, zk], w=[zk])

    def scans(J):
        zs = zsets[J % 2]
        tg = zs.t
        dA = (0, J); dB = (1, J)

        def chainA():
            exscan(scrA, zs.ZA, 256, 0, dA, False, None, PA, ["ZA" + tg, "PA"], None)
            dve(lambda e: e.tensor_copy(out=zs.PAb, in_=PA), r=["PA", "ZA" + tg], w=["PAb" + tg, "ZA" + tg])

        def chainB():
            sc = scrB
            horner(sc, zs.ZO, 256, 0, dB, True, sc.zup[0], ["ZO" + tg])
            horner(sc, sc.zup[0], 32, 1, dB, True, sc.zup[1], ["ZO" + tg])
            mr2, mi2 = mu_of(2, dB)
            cur = vw(sc.zup[1], slice(3, 4))
            for n_ in (2, 1, 0):
                dst = vw(carry, slice(0, 1)) if n_ == 0 else vw(sc.accs[n_ % 2], slice(0, 1))
                madd(sc, dst, cur, vw(sc.zup[1], slice(n_, n_ + 1)), mr2, mi2, 1, ["ZO" + tg, "carry"])
                cur = dst
            exscan(sc, zs.ZB, 256, 0, dB, True, carry, zs.NB, ["ZB" + tg, "ZO" + tg, "carry"], "carry")
            pool(lambda e: e.tensor_copy(out=zs.NBb, in_=zs.NB), r=["ZO" + tg, "ZB" + tg], w=["NBb" + tg, "ZB" + tg])
        return [P.capture(chainA), P.capture(chainB)]

    def post(J):
        zs = zsets[J % 2]
        tg = zs.t
        ld([(T0J[:, 0:8, :], T0_s[:, 8 * J:8 * J + 8, :]), (T0J[:, 8:16, :], T0_s[:, 32 + 8 * J:40 + 8 * J, :])], w=["T0J"])
        ld([(VJ[hh * 64:hh * 64 + 64, :, 0:8, :], Vb_s[:, :, 8 * J:8 * J + 8, :]) for hh in range(2)]
           + [(VJ[hh * 64:hh * 64 + 64, :, 8:16, :], Vb_s[:, :, 32 + 8 * J:40 + 8 * J, :]) for hh in range(2)], w=["VJ"])
        for hb_ in range(2):
            U = zs.Us[hb_]
            ph = slice(64 * hb_, 64 * hb_ + 64)
            for jj in range(NBT):
                j = hb_ * NBT + jj
                uk = "U%s%d_%d" % (tg, hb_, jj)

                def my(e, j=j, jj=jj, U=U, ph=ph):
                    o = psA[:, 0:256]
                    e.matmul(o, lhsT=T0J[:, j, :], rhs=U[:, jj, 0:256], start=True, stop=False)
                    e.matmul(o, lhsT=T0J[:, 8 + j, :], rhs=U[:, jj, 0:256], start=False, stop=False)
                    e.matmul(o, lhsT=VJ[ph, 0, j, :], rhs=zs.PAb[ph, jj, 0, :], start=False, stop=False)
                    e.matmul(o, lhsT=VJ[ph, 1, j, :], rhs=zs.PAb[ph, jj, 1, :], start=False, stop=False)
                    e.matmul(o, lhsT=VJ[ph, 0, 8 + j, :], rhs=zs.NBb[ph, jj, 0, :], start=False, stop=False)
                    return e.matmul(o, lhsT=VJ[ph, 1, 8 + j, :], rhs=zs.NBb[ph, jj, 1, :], start=False, stop=True)
                pe(my, r=["T0J", "VJ", uk, "PAb" + tg, "NBb" + tg, "ZA" + tg, "ZB" + tg, pkA], w=[pkA])
                dve(lambda e, j=j: e.tensor_copy(out=Yb[:, j, :], in_=psA[:, 0:256]), r=[pkA, "Yb"], w=["Yb"])
        yv = ysb.rearrange("p (c i) -> p i c", i=8)
        for i in range(8):
            psd = psB[:, (i % 2) * 256:(i % 2) * 256 + 256]

            def md(e, i=i, psd=psd):
                ins = None
                for j in range(8):
                    ins = e.matmul(psd, lhsT=G[:, i, j, :], rhs=Yb[:, j, :], start=(j == 0), stop=(j == 7))
                return ins
            pe(md, r=["G", "Yb"], w=[pkB])
            dve(lambda e, i=i, psd=psd: e.tensor_copy(out=yv[:, i, :], in_=psd), r=[pkB, "ysb"], w=["ysb"])
        ld([(y_s[J], ysb)], r=["ysb"], w=["y_s%d" % J], eng="pool")
        if "ys5" in dbg_d:
            ld([(dbg_d["ys5"][J * 128:(J + 1) * 128, :], ysb)], r=["ysb"], w=["dbg_ys5"], eng="pool")

    def s5_main():
        pre(0)
        for J in range(4):
            lists = scans(J)

            def side(J=J):
                if J > 0:
                    post(J - 1)
                if J + 1 < 4:
                    pre(J + 1)
            lists.append(P.capture(side))
            P.merge(lists)
        post(3)

    def s5_tail():
      wg_bf = T([128, 4, 2 * D], BF16)
      ld([(wg_bf, L["w_glu_d"].rearrange("(k p) n -> p k n", p=128))], w=["wg_bf"], eng="pool")
      GY = T([128, 4, OWN], BF16)
      yb_ = [T([128, OWN])] * 2
      ub_ = [T([128, OWN], BF16), T([128, OWN], BF16)]
      g3 = T([128, OWN]); g4 = T([128, OWN])
      dsk2 = T([128, 4])
      ld([(dsk2, L["sd_d"])], w=["dsk2"])
      for J in range(4):
          yt, yk = yb_[J % 2], "ytl"
          ut, uk = ub_[J % 2], "utl%d" % (J % 2)
          ld([(yt, y_s[J])], r=["y_s%d" % J], w=[yk])
          ld([(ut, uT_s[J * 128:(J + 1) * 128, 0:OWN])], w=[uk])
          dve(lambda e, J=J, yt=yt, ut=ut: e.scalar_tensor_tensor(out=yt, in0=ut, scalar=dsk2[:, J:J + 1], in1=yt, op0=ALU.mult, op1=ALU.add),
              r=[yk, uk, "dsk2"], w=[yk])
          act(lambda e, yt=yt: e.activation(out=g3, in_=yt, func=AF.Square), r=[yk, "g3"], w=["g3"])
          dve(lambda e: e.tensor_scalar(out=g3, in0=g3, scalar1=0.044715, scalar2=1.0, op0=ALU.mult, op1=ALU.add), r=["g3"], w=["g3"])
          dve(lambda e, yt=yt: e.tensor_tensor(out=g3, in0=g3, in1=yt, op=ALU.mult), r=["g3", yk], w=["g3"])
          act(lambda e: e.activation(out=g4, in_=g3, func=AF.Sigmoid, scale=GEL), r=["g3", "g4"], w=["g4"])
          dve(lambda e, J=J, yt=yt: e.tensor_tensor(out=GY[:, J, :], in0=g4, in1=yt, op=ALU.mult), r=["g4", yk, "GY"], w=["GY"])
      sg = T([128, 512]); bo = [T([128, 512], BF16), T([128, 512], BF16)]
      n_ = 0
      for tb in range(4):
          tc_ = slice(tb * 512, (tb + 1) * 512)
          for dc in range(8):
              def mg(e, dc=dc, tc_=tc_):
                  ins = None
                  for (psx, c0) in ((pf[0], dc * 128), (pf[1], D + dc * 128)):
                      for k in range(4):
                          ins = e.matmul(psx, lhsT=wg_bf[:, k, c0:c0 + 128], rhs=GY[:, k, tc_], start=(k == 0), stop=(k == 3))
                  return ins
              pe(mg, r=["wg_bf", "GY", "pf0", "pf1"], w=["pf0", "pf1"])
              act(lambda e: e.activation(out=sg, in_=pf[1], func=AF.Sigmoid), r=["pf1"], w=["sg"])
              ob, ok = bo[n_ % 2], "bo%d" % (n_ % 2)
              n_ += 1
              dve(lambda e, ob=ob: e.tensor_tensor(out=ob, in0=pf[0], in1=sg, op=ALU.mult), r=["pf0", "sg"], w=[ok])
              ld([(brb_s[dc, :, tc_], ob)], r=[ok], w=["brb_s"], eng="pool")
      if "brb" in dbg_d:
          ld([(dbg_d["brb"], brb_s)], r=["brb_s"], w=["dbg_brb"], eng="pool")


    return s5_main, s5_tail, base_mark


def stage_attn(L):
    P, nc = L["P"], L["nc"]
    act, dve, pool, pe, ld = L["act"], L["dve"], L["pool"], L["pe"], L["ld"]
    pf, ones_f, dbg_d = L["pf"], L["ones_f"], L["dbg_d"]
    KT_s, V_s, QT_s = L["KT_s"], L["V_s"], L["QT_s"]
    bra_s = L["bra_s"] = nc.dram_tensor("bra_s", [8, 128, OWN], BF16, kind="Internal").ap()
    OT_s = nc.dram_tensor("OT_s", [64, 8, OWN], BF16, kind="Internal").ap()
    Vh = [P.sb([128, 32, 65], BF16, "Vh%d" % i) for i in range(2)]
    V_v = V_s.rearrange("(t p) (h d) -> p t h d", p=128, h=8)
    KTh = [P.sb([96, SEQ], BF16, "KTh%d" % i) for i in range(2)]
    QTh = [P.sb([96, OWN], BF16, "QTh%d" % i) for i in range(2)]
    PT3 = [P.sb([128, 512], BF16, "PT%d" % i) for i in range(3)]
    rcs = P.sb([128, 512], F32, "rcs")
    bcs = rcs[0:64, :]
    otb = [P.sb([64, 512], BF16, "otb%d" % i) for i in range(2)]
    sbank = [pf[0], pf[1], pf[6]]
    sbk = ["pf0", "pf1", "pf6"]
    steps = [(h, qb, kt) for h in range(NH) for qb in range(4) for kt in range(32)]
    LOOK = 2

    def emit_S(i):
        h, qb, kt = steps[i]
        kb, kk = KTh[h % 2], "KTh%d" % (h % 2)
        qb_, qk = QTh[h % 2], "QTh%d" % (h % 2)
        if qb == 0 and kt == 0:
            ld([(kb[:, 0:2048], KT_s[:, h, 0:2048]), (kb[:, 2048:], KT_s[:, h, 2048:])], w=[kk])
            ld([(qb_, QT_s[:, h, :])], w=[qk])
            ld([(Vh[h % 2][:, 0:16, :], V_v[:, 0:16, h, :]), (Vh[h % 2][:, 16:32, :], V_v[:, 16:32, h, :])], w=["Vh%d" % (h % 2)])
        qs = slice(qb * 512, (qb + 1) * 512)
        pss, pks = sbank[i % 3], sbk[i % 3]
        pe(lambda e: e.matmul(pss, lhsT=kb[:, kt * 128:(kt + 1) * 128], rhs=qb_[:, qs], start=True, stop=True), r=[kk, qk, pks], w=[pks])

    def emit_PV(i):
        h, qb, kt = steps[i]
        qs = slice(qb * 512, (qb + 1) * 512)
        pss, pks = sbank[i % 3], sbk[i % 3]
        pt, ptk = PT3[i % 3], "PT%d" % (i % 3)
        blk_i = i // 32
        acc, acck = (pf[2], "pf2") if blk_i % 2 == 0 else (pf[5], "pf5")
        act(lambda e: e.activation(out=pt, in_=pss, func=AF.Exp), r=[pks], w=[ptk])
        vb_, vk = Vh[h % 2], "Vh%d" % (h % 2)
        pe(lambda e: e.matmul(acc[0:65, :], lhsT=vb_[:, kt, :], rhs=pt, start=(kt == 0), stop=(kt == 31)),
           r=[vk, ptk, acck], w=[acck])
        if kt == 31:
            ob_, obk = otb[blk_i % 2], "otb%d" % (blk_i % 2)
            dve(lambda e: e.reciprocal(out=rcs[64:65, :], in_=acc[64:65, :]), r=[acck], w=["rcs"])

            def epilogue():
                pe(lambda e: e.matmul(pf[4][0:64, :], lhsT=ones_f[64:65, 0:64], rhs=rcs[64:65, :], start=True, stop=True), r=["rcs", "pf4"], w=["pf4"])
                dve(lambda e: e.tensor_copy(out=bcs, in_=pf[4][0:64, :]), r=["pf4"], w=["bcs"])
                dve(lambda e: e.tensor_tensor(out=ob_, in0=acc[0:64, :], in1=bcs, op=ALU.mult), r=[acck, "bcs", obk], w=[obk])
                ld([(OT_s[:, h, qs], ob_)], r=[obk], w=["OT_s"], eng="sp")
            pending.append((i + EPI_DELAY, epilogue))

    pending = []
    EPI_DELAY = 8

    def attn_core():
        for i in range(len(steps) + LOOK):
            if i < len(steps):
                emit_S(i)
            if i >= LOOK:
                emit_PV(i - LOOK)
            while pending and pending[0][0] <= i - LOOK:
                pending.pop(0)[1]()
        while pending:
            pending.pop(0)[1]()

    def attn_proj():
        OT = P.sb([64, 8, OWN], BF16, "OT")
        wo_bf = P.sb([64, 8, D], BF16, "wo_bf")
        ob = [P.sb([128, 512], BF16, "aob%d" % i) for i in range(2)]
        ld([(OT, OT_s)], r=["OT_s"], w=["OT"])
        ld([(wo_bf, L["w_o_d"].rearrange("(h p) n -> p h n", p=64))], w=["wo_bf"], eng="pool")
        n_ = 0
        for tb in range(4):
            ts_ = slice(tb * 512, (tb + 1) * 512)
            for j in range(8):
                psx, pk = pf[2 + n_ % 2], "pf%d" % (2 + n_ % 2)
                o_, okk = ob[n_ % 2], "aob%d" % (n_ % 2)
                n_ += 1

                def mo(e, j=j, ts_=ts_, psx=psx):
                    ins = None
                    for h in range(8):
                        ins = e.matmul(psx, lhsT=wo_bf[:, h, j * 128:(j + 1) * 128], rhs=OT[:, h, ts_], start=(h == 0), stop=(h == 7))
                    return ins
                pe(mo, r=["wo_bf", "OT", pk], w=[pk])
                act(lambda e, psx=psx, o_=o_: e.copy(out=o_, in_=psx), r=[pk], w=[okk])
                ld([(bra_s[j, :, ts_], o_)], r=[okk], w=["bra_s"], eng="pool")
        if "bra" in dbg_d:
            ld([(dbg_d["bra"], bra_s)], r=["bra_s"], w=["dbg_bra"], eng="pool")

    return attn_core, attn_proj


def stage_ffn(L):
    P, nc = L["P"], L["nc"]
    act, dve, pool, pe, ld = L["act"], L["dve"], L["pool"], L["pe"], L["ld"]
    pf, pb, ident, dbg_d = L["pf"], L["pb"], L["ident"], L["dbg_d"]
    gate_s, bra_s, brb_s, x_d, out_d = L["gate_s"], L["bra_s"], L["brb_s"], L["x_d"], L["out_d"]
    x1_s = nc.dram_tensor("x1_s", [OWN, D], F32, kind="Internal").ap()
    base_mark = P.sb_mark
    AB, wm_bf, w2_bf = L["ABf0"], L["wm_bf0"], L["w2_bf0"]
    P.skip([128, 4, D], F32); P.skip([128, 8, D], BF16); P.skip([128, 22, D], BF16)
    w1c = [P.sb([128, 8, 256], BF16, "w1c%d" % i) for i in range(3)]
    gin = [P.sb([128, 4, 512], BF16, "gin%d" % i) for i in range(2)]
    mt = P.sb([128, 2, 512], F32, "mt")
    mT = P.sb([128, 8, 512], BF16, "mT")
    mx = P.sb([128, D], F32, "mx"); sq = P.sb([128, D], F32, "fsq"); tmp = P.sb([128, D], F32, "ftmp")
    xt = P.sb([128, D], F32, "fxt"); x1t = P.sb([128, D], F32, "x1t")
    fhbs = [P.sb([128, D], BF16, "fhb%d" % i) for i in range(2)]
    st = P.sb([128, 16], F32, "fst")
    mx2 = P.sb([128, D], F32, "mx2"); sq2 = P.sb([128, D], F32, "fsq2"); tmp2 = P.sb([128, D], F32, "ftmp2")
    xt2 = P.sb([128, D], F32, "fxt2"); st2 = P.sb([128, 16], F32, "fst2")
    h2Ts = [P.sb([128, 8, 512], BF16, "h2T%d" % i) for i in range(2)]
    aT = P.sb([128, 22, 512], BF16, "aT")
    sgls = [P.sb([128, 512], F32, "sgl%d" % i) for i in range(2)]
    w1v = L["w_f1_d"].rearrange("(k p) n -> p k n", p=128)
    nw = [0]

    def chainF(tb):
        ts_ = slice(tb * 512, (tb + 1) * 512)
        h2T, hk = h2Ts[tb % 2], "h2T%d" % (tb % 2)
        for j in range(8):
            gb_, gk = gin[j % 2], "gin%d" % (j % 2)
            ld([(gb_[:, 0, :], gate_s[j, :, ts_]), (gb_[:, 1, :], gate_s[8 + j, :, ts_]),
                (gb_[:, 2, :], bra_s[j, :, ts_]), (gb_[:, 3, :], brb_s[j, :, ts_])], r=["gate_s", "bra_s", "brb_s"], w=[gk])
            dve(lambda e, gb_=gb_: e.tensor_tensor(out=mt[:, 0, :], in0=gb_[:, 0, :], in1=gb_[:, 2, :], op=ALU.mult), r=[gk, "mt"], w=["mt"])
            dve(lambda e, gb_=gb_: e.tensor_tensor(out=mt[:, 1, :], in0=gb_[:, 1, :], in1=gb_[:, 3, :], op=ALU.mult), r=[gk, "mt"], w=["mt"])
            dve(lambda e, j=j: e.tensor_tensor(out=mT[:, j, :], in0=mt[:, 0, :], in1=mt[:, 1, :], op=ALU.add), r=["mt", "mT"], w=["mT"])
        def partA(tt):
            t = tb * 4 + tt
            tk = slice(tt * 128, (tt + 1) * 128)
            hbt, hbk = fhbs[tt % 2], "fhb%d" % (tt % 2)
            ld([(xt, x_d[t * 128:(t + 1) * 128, :])], w=["fxt"])
            for hh in range(2):
                def mmx(e, tk=tk, hh=hh):
                    ins = None
                    for k in range(8):
                        ins = e.matmul(pf[7], lhsT=mT[:, k, tk], rhs=wm_bf[:, k, hh * 512:(hh + 1) * 512], start=(k == 0), stop=(k == 7))
                    return ins
                pe(mmx, r=["mT", "wm_bf", "pf7"], w=["pf7"])
                act(lambda e, hh=hh: e.copy(out=mx[:, hh * 512:(hh + 1) * 512], in_=pf[7]), r=["pf7", "mx"], w=["mx"])
            rms_ops(P, L, mx, D, st[:, 0:1], st[:, 1:2], st[:, 2:3], sq, "mx", "m", "fsq")
            dve(lambda e: e.scalar_tensor_tensor(out=tmp, in0=mx, scalar=st[:, 2:3], in1=AB[:, 0, :], op0=ALU.mult, op1=ALU.mult),
                r=["mx", "rstdm", "AB", "ftmp"], w=["ftmp"])
            dve(lambda e: e.tensor_tensor(out=x1t, in0=tmp, in1=xt, op=ALU.add), r=["ftmp", "fxt", "x1t"], w=["x1t"])
            ld([(x1_s[t * 128:(t + 1) * 128, :], x1t)], r=["x1t"], w=["x1_s%d" % t], eng="pool")
            rms_ops(P, L, x1t, D, st[:, 4:5], st[:, 5:6], st[:, 6:7], sq, "x1t", "h2", "fsq")
            dve(lambda e: e.scalar_tensor_tensor(out=tmp, in0=x1t, scalar=st[:, 6:7], in1=AB[:, 1, :], op0=ALU.mult, op1=ALU.mult),
                r=["x1t", "rstdh2", "AB", "ftmp"], w=["ftmp"])
            dve(lambda e, hbt=hbt: e.tensor_tensor(out=hbt, in0=tmp, in1=AB[:, 2, :], op=ALU.add), r=["ftmp", "AB"], w=[hbk])

        def partB(tt):
            hbt, hbk = fhbs[tt % 2], "fhb%d" % (tt % 2)

            def tr(e, hbt=hbt):
                ins = None
                for k in range(8):
                    ins = e.transpose(pb[0][:, k * 128:(k + 1) * 128], hbt[:, k * 128:(k + 1) * 128], ident)
                return ins
            pe(tr, r=[hbk, "ident", "pb0"], w=["pb0"])
            act(lambda e, tt=tt, h2T=h2T: e.copy(out=h2T[:, :, tt * 128:(tt + 1) * 128], in_=pb[0].rearrange("p (k t) -> p k t", k=8)),
                r=["pb0", hk], w=[hk])
        for tt in range(5):
            if tt < 4:
                partA(tt)
            if tt >= 1:
                partB(tt - 1)

    def ffnF(tb):
        h2T, hk = h2Ts[tb % 2], "h2T%d" % (tb % 2)
        for hc in range(22):
            wc, wk = w1c[nw[0] % 3], "w1c%d" % (nw[0] % 3)
            nw[0] += 1
            ld([(wc, L["w1b_s"][hc])], w=[wk])
            pa, pbk = (2, 3) if hc % 2 == 0 else (4, 5)
            sgl, sgk = sgls[hc % 2], "sgl%d" % (hc % 2)

            def mf(e, wc=wc, pa=pa, pbk=pbk):
                ins = None
                for (psx, c0) in ((pf[pa], 0), (pf[pbk], 128)):
                    for k in range(8):
                        ins = e.matmul(psx, lhsT=wc[:, k, c0:c0 + 128], rhs=h2T[:, k, :], start=(k == 0), stop=(k == 7))
                return ins
            pe(mf, r=[wk, hk, "pf%d" % pa, "pf%d" % pbk], w=["pf%d" % pa, "pf%d" % pbk])
            act(lambda e, pa=pa, sgl=sgl: e.activation(out=sgl, in_=pf[pa], func=AF.Silu), r=["pf%d" % pa, sgk], w=[sgk])
            dve(lambda e, hc=hc, pbk=pbk, sgl=sgl: e.tensor_tensor(out=aT[:, hc, :], in0=pf[pbk], in1=sgl, op=ALU.mult),
                r=["pf%d" % pbk, sgk], w=["aT%d" % hc])
        for tt in range(4):
            t = tb * 4 + tt
            tk = slice(tt * 128, (tt + 1) * 128)
            ld([(xt2, x1_s[t * 128:(t + 1) * 128, :])], r=["x1_s%d" % t], w=["fxt2"])

            def mo(e, tk=tk):
                ins = None
                for hh in range(2):
                    for k in range(22):
                        ins = e.matmul(pf[hh], lhsT=aT[:, k, tk], rhs=w2_bf[:, k, hh * 512:(hh + 1) * 512], start=(k == 0), stop=(k == 21))
                return ins
            pe(mo, r=["aT%d" % k for k in range(22)] + ["w2_bf", "pf0", "pf1"], w=["pf0", "pf1"])
            act(lambda e: e.copy(out=mx2[:, 0:512], in_=pf[0]), r=["pf0", "mx2"], w=["mx2"])
            act(lambda e: e.copy(out=mx2[:, 512:], in_=pf[1]), r=["pf1", "mx2"], w=["mx2"])
            rms_ops(P, L, mx2, D, st2[:, 8:9], st2[:, 9:10], st2[:, 10:11], sq2, "mx2", "f", "fsq2")
            dve(lambda e: e.scalar_tensor_tensor(out=tmp2, in0=mx2, scalar=st2[:, 10:11], in1=AB[:, 3, :], op0=ALU.mult, op1=ALU.mult),
                r=["mx2", "rstdf", "AB", "ftmp2"], w=["ftmp2"])
            dve(lambda e: e.tensor_tensor(out=tmp2, in0=tmp2, in1=xt2, op=ALU.add), r=["ftmp2", "fxt2"], w=["ftmp2"])
            ld([(out_d[t * 128:(t + 1) * 128, :], tmp2)], r=["ftmp2"], w=["out_d"], eng="pool")

    P.merge([P.capture(lambda: chainF(0))])
    for tb in range(4):
        lists = [P.capture(lambda: ffnF(tb))]
        if tb + 1 < 4:
            lists.append(P.capture(lambda: chainF(tb + 1)))
        P.merge(lists)
    P.flush()
    P.sb_off = P.sb_mark = base_mark


def prep_core(inp, core):
    b, hf = core // 2, core % 2
    rev = (hf == 1)
    f = lambda a: np.ascontiguousarray(a, dtype=np.float32)
    x = inp["x"][b]
    pos = inp["positions"][b]
    if rev:
        x = x[::-1]
        pos = pos[::-1]
    m = {}
    m["x_loc"] = f(x)
    m["pos_loc"] = np.ascontiguousarray(np.asarray(pos, dtype=np.int32).reshape(32, 128).T)
    m["c_col"] = f(inp["c"][b].reshape(8, 128).T)
    m["w_ada"] = f(inp["w_ada"][0])
    m["b_ada"] = f(inp["b_ada"][0].reshape(1, -1))
    m["gvec"] = f(np.concatenate([inp["g_pre_mix"][0], inp["g_post_mix"][0], inp["g_pre_ffn"][0], inp["g_post_ffn"][0]]).reshape(1, -1))
    m["w_in"] = f(inp["w_in"][0])
    m["g_q"] = f(inp["g_q_norm"][0].reshape(1, -1))
    m["g_kv"] = f(inp["g_kv_norm"][0].reshape(1, -1))
    m["w_uq"] = f(inp["w_uq"][0])
    m["w_uk"] = f(inp["w_uk"][0])
    m["w_uv"] = f(inp["w_uv"][0])
    m["w_o"] = f(inp["w_attn_out"][0])
    order = [1, 0] if rev else [0, 1]
    def dg(a):
        a = np.asarray(a)[order]
        return a.reshape((64,) + a.shape[2:])
    lre = dg(inp["ssm_lambda_re"][0]); lim = dg(inp["ssm_lambda_im"][0])
    m["lam"] = f(np.stack([lre.T, lim.T], axis=1))
    m["ldt"] = f(np.broadcast_to(dg(inp["ssm_log_dt"][0]).reshape(1, 64), (64, 64)))
    bre = dg(inp["ssm_b_re"][0]); bim = dg(inp["ssm_b_im"][0])
    m["ssm_b"] = f(np.stack([bre.transpose(1, 0, 2), bim.transpose(1, 0, 2)], axis=1))
    cre = dg(inp["ssm_c_re"][0]); cim = dg(inp["ssm_c_im"][0])
    m["ssm_c"] = f(np.stack([cre.transpose(2, 0, 1), cim.transpose(2, 0, 1)], axis=1))
    m["ssm_d"] = f(inp["ssm_d"][0].reshape(4, 128).T)
    m["w_glu"] = f(inp["w_glu"][0])
    m["w_mix"] = f(inp["w_mix_out"][0])
    m["w_f1"] = f(inp["w_ffn_in"][0])
    m["w_f2"] = f(inp["w_ffn_out"][0])
    return m


def kernel(**inputs):
    inputs = {k: np.asarray(v) for k, v in inputs.items()}
    nc = bass.Bass("TRN2", target_bir_lowering=False)
    build(nc)
    in_maps = [prep_core(inputs, c) for c in range(8)]
    res = run_bass_kernel_spmd(nc, in_maps, core_ids=list(range(8)))
    out = np.zeros((4, SEQ, D), np.float32)
    for c in range(8):
        b, hf = c // 2, c % 2
        o = np.asarray(res.results[c]["out_loc"], dtype=np.float32)
        if hf == 0:
            out[b, :OWN] = o
        else:
            out[b, OWN:] = o[::-1]
    return out
```
